# Optimizing a Trainium2 kernel written in Bass

```python
import math
import jax, jax.numpy as jnp
from jax import lax
import numpy as np

D_MODEL = 1024
BATCH = 4
SEQ = 4096
DEPTH = 2

N_A_LAYERS = DEPTH // 2
N_B_LAYERS = DEPTH - N_A_LAYERS
CONV_WIDTH = 31
FOX_HEADS = 16
FOX_HEAD_DIM = D_MODEL // FOX_HEADS
MEM_LEN = 256
MEM_HEADS = 4
MEM_HEAD_DIM = D_MODEL // MEM_HEADS
D_FF = 4 * D_MODEL
Q_BLOCK = 128
RMS_EPS = 1e-6
LN_EPS = 1e-5

kernel_name = "yoco_conformer_fox_hybrid"


def rmsnorm(x, g):
    xf = x.astype(jnp.float32)
    y = xf * lax.rsqrt(jnp.mean(xf * xf, axis=-1, keepdims=True) + RMS_EPS)
    return (y * g.astype(jnp.float32)).astype(x.dtype)


def layernorm(x, g, b):
    xf = x.astype(jnp.float32)
    mu = jnp.mean(xf, axis=-1, keepdims=True)
    var = jnp.mean(jnp.square(xf - mu), axis=-1, keepdims=True)
    y = (xf - mu) * lax.rsqrt(var + LN_EPS)
    return (y * g.astype(jnp.float32) + b.astype(jnp.float32)).astype(x.dtype)


def conformer_conv(h, w_pw1, b_pw1, w_dw, b_dw, ln_g, ln_b, w_pw2, b_pw2):
    u = h @ w_pw1 + b_pw1
    a, gate = jnp.split(u, 2, axis=-1)
    u = a * jax.nn.sigmoid(gate)
    u = lax.conv_general_dilated(
        u, w_dw[:, None, :].astype(u.dtype),
        window_strides=(1,), padding=[(CONV_WIDTH - 1, 0)],
        dimension_numbers=("NWC", "WIO", "NWC"),
        feature_group_count=D_MODEL) + b_dw
    u = jax.nn.silu(layernorm(u, ln_g, ln_b))
    return u @ w_pw2 + b_pw2


def fox_attention(h, w_q, w_o, k, v, c_bhs):
    B, S, _ = h.shape
    nb = S // Q_BLOCK
    q = (h @ w_q).reshape(B, nb, Q_BLOCK, FOX_HEADS, FOX_HEAD_DIM).transpose(1, 0, 2, 3, 4)
    c_q = c_bhs.reshape(B, FOX_HEADS, nb, Q_BLOCK).transpose(2, 0, 1, 3)
    key_pos = jnp.arange(S)
    scale = 1.0 / math.sqrt(FOX_HEAD_DIM)

    def block(args):
        qi, ci, i = args
        s = jnp.einsum("bqhd,bkhd->bhqk", qi, k).astype(jnp.float32) * scale
        s = s + (ci[..., :, None] - c_bhs[:, :, None, :])
        q_pos = i * Q_BLOCK + jnp.arange(Q_BLOCK)
        mask = key_pos[None, :] <= q_pos[:, None]
        s = jnp.where(mask[None, None], s, -jnp.inf)
        p = jax.nn.softmax(s, axis=-1)
        return jnp.einsum("bhqk,bkhd->bqhd", p.astype(v.dtype), v)

    out = lax.map(block, (q, c_q, jnp.arange(nb)))
    out = out.transpose(1, 0, 2, 3, 4).reshape(B, S, D_MODEL)
    return out @ w_o


def memory_cross_attention(h, mem_n, w_q, w_k, w_v, w_o):
    B, S, _ = h.shape
    q = (h @ w_q).reshape(B, S, MEM_HEADS, MEM_HEAD_DIM)
    k = (mem_n @ w_k).reshape(B, MEM_LEN, MEM_HEADS, MEM_HEAD_DIM)
    v = (mem_n @ w_v).reshape(B, MEM_LEN, MEM_HEADS, MEM_HEAD_DIM)
    s = jnp.einsum("bshd,bmhd->bhsm", q, k).astype(jnp.float32) / math.sqrt(MEM_HEAD_DIM)
    p = jax.nn.softmax(s, axis=-1)
    o = jnp.einsum("bhsm,bmhd->bshd", p.astype(v.dtype), v).reshape(B, S, D_MODEL)
    return o @ w_o


def sqrelu_mlp(h, w1, w2):
    return jnp.square(jax.nn.relu(h @ w1)) @ w2


def setup_inputs(seed: int = 0) -> dict:
    key = jax.random.key(seed)
    ks = iter(jax.random.split(key, 40))

    def nrm(shape, scale):
        return jax.random.normal(next(ks), shape, jnp.float32) * scale

    def gain(shape):
        return 1.0 + nrm(shape, 0.05)

    D = D_MODEL
    return {
        "x": nrm((BATCH, SEQ, D), 1.0),
        "mem": nrm((BATCH, MEM_LEN, D), 1.0),
        "norm_mix_g": gain((DEPTH, D)),
        "norm_mem_g": gain((DEPTH, D)),
        "norm_memsrc_g": gain((DEPTH, D)),
        "norm_ff_g": gain((DEPTH, D)),
        "mem_wq": nrm((DEPTH, D, D), D ** -0.5),
        "mem_wk": nrm((DEPTH, D, D), D ** -0.5),
        "mem_wv": nrm((DEPTH, D, D), D ** -0.5),
        "mem_wo": nrm((DEPTH, D, D), 0.5 * D ** -0.5),
        "ff_w1": nrm((DEPTH, D, D_FF), D ** -0.5),
        "ff_w2": nrm((DEPTH, D_FF, D), 0.5 * D_FF ** -0.5),
        "conv_pw1_w": nrm((N_A_LAYERS, D, 2 * D), D ** -0.5),
        "conv_pw1_b": nrm((N_A_LAYERS, 2 * D), 0.02),
        "conv_dw_w": nrm((N_A_LAYERS, CONV_WIDTH, D), CONV_WIDTH ** -0.5),
        "conv_dw_b": nrm((N_A_LAYERS, D), 0.02),
        "conv_ln_g": gain((N_A_LAYERS, D)),
        "conv_ln_b": nrm((N_A_LAYERS, D), 0.02),
        "conv_pw2_w": nrm((N_A_LAYERS, D, D), 0.5 * D ** -0.5),
        "conv_pw2_b": nrm((N_A_LAYERS, D), 0.02),
        "kv_norm_g": gain((D,)),
        "kvf_w": nrm((D, 2 * D + FOX_HEADS), D ** -0.5),
        "fgate_b": 3.0 + nrm((FOX_HEADS,), 0.5),
        "fox_wq": nrm((N_B_LAYERS, D, D), D ** -0.5),
        "fox_wo": nrm((N_B_LAYERS, D, D), 0.5 * D ** -0.5),
        "final_norm_g": gain((D,)),
    }


def reference(x, mem, norm_mix_g, norm_mem_g, norm_memsrc_g, norm_ff_g,
              mem_wq, mem_wk, mem_wv, mem_wo, ff_w1, ff_w2,
              conv_pw1_w, conv_pw1_b, conv_dw_w, conv_dw_b, conv_ln_g, conv_ln_b,
              conv_pw2_w, conv_pw2_b,
              kv_norm_g, kvf_w, fgate_b, fox_wq, fox_wo, final_norm_g):
    B, S, D = x.shape
    k = v = c_bhs = None
    for l in range(DEPTH):
        if l == N_A_LAYERS:
            hk = rmsnorm(x, kv_norm_g)
            kvf = hk @ kvf_w
            k = kvf[..., :D].reshape(B, S, FOX_HEADS, FOX_HEAD_DIM)
            v = kvf[..., D:2 * D].reshape(B, S, FOX_HEADS, FOX_HEAD_DIM)
            log_f = jax.nn.log_sigmoid((kvf[..., 2 * D:] + fgate_b).astype(jnp.float32))
            c_bhs = jnp.cumsum(log_f, axis=1).transpose(0, 2, 1)
        h = rmsnorm(x, norm_mix_g[l])
        if l < N_A_LAYERS:
            a = l
            x = x + conformer_conv(h, conv_pw1_w[a], conv_pw1_b[a], conv_dw_w[a], conv_dw_b[a],
                                   conv_ln_g[a], conv_ln_b[a], conv_pw2_w[a], conv_pw2_b[a])
        else:
            bl = l - N_A_LAYERS
            x = x + fox_attention(h, fox_wq[bl], fox_wo[bl], k, v, c_bhs)
        h = rmsnorm(x, norm_mem_g[l])
        mem_n = rmsnorm(mem, norm_memsrc_g[l])
        x = x + memory_cross_attention(h, mem_n, mem_wq[l], mem_wk[l], mem_wv[l], mem_wo[l])
        h = rmsnorm(x, norm_ff_g[l])
        x = x + sqrelu_mlp(h, ff_w1[l], ff_w2[l])
    return rmsnorm(x, final_norm_g)
```

```python
import contextlib
import numpy as np
import ml_dtypes
import concourse.bass as bass
import concourse.mybir as mybir
from concourse.bass_utils import run_bass_kernel_spmd

F32 = mybir.dt.float32
BF16 = mybir.dt.bfloat16
AF = mybir.ActivationFunctionType
ALU = mybir.AluOpType

ENGS = ["pe", "act", "dve", "pool", "sp"]
D = 1024
NTOK = 2048
TT = 1024
NEG = -30000.0
FUSED = False


class Buf:
    __slots__ = ("w", "r")

    def __init__(self):
        self.w = {}
        self.r = {}


class Prog:
    def __init__(self, nc, ndma=48):
        self.nc = nc
        self.ops = {e: [] for e in ENGS}
        self.cnt = {e: 0 for e in ENGS}
        self.seen = {e: {} for e in ENGS}
        self.ndma = ndma
        self.dma_cnt = [0] * ndma
        self.dma_rr = 0
        self.dma_rr_pool = 0

    def _deps(self, eng, reads, writes):
        waits = {}
        seen = self.seen[eng]

        def need(k, v):
            if seen.get(k, 0) >= v:
                return
            if waits.get(k, 0) < v:
                waits[k] = v
        for b in reads:
            for k, v in b.w.items():
                need(k, v)
        for b in writes:
            for k, v in b.w.items():
                need(k, v)
            for k, v in b.r.items():
                need(k, v)
        for k, v in waits.items():
            seen[k] = v
        return list(waits.items())

    def _mark(self, stamp, reads, writes):
        k, v = stamp
        for b in reads:
            if b.r.get(k, 0) < v:
                b.r[k] = v
        for b in writes:
            if b.w.get(k, 0) < v:
                b.w[k] = v

    def op(self, eng, fn, reads=(), writes=()):
        waits = self._deps(eng, reads, writes)
        self.cnt[eng] += 1
        stamp = (eng, self.cnt[eng])
        self.ops[eng].append((fn, waits, stamp, 1))
        self._mark(stamp, reads, writes)

    def dma(self, eng, fn, reads=(), writes=()):
        waits = self._deps(eng, reads, writes)
        half = self.ndma // 2
        if eng == "pool":
            j = half + self.dma_rr_pool
            self.dma_rr_pool = (self.dma_rr_pool + 1) % (self.ndma - half)
        else:
            j = self.dma_rr
            self.dma_rr = (self.dma_rr + 1) % half
        prev = self.dma_cnt[j]
        if prev > 0 and self.seen[eng].get(("dma", j), 0) < prev:
            waits.append((("dma", j), prev))
            self.seen[eng][("dma", j)] = prev
        self.dma_cnt[j] += 16
        stamp = (("dma", j), self.dma_cnt[j])
        self.ops[eng].append((fn, waits, stamp, 16))
        self._mark(stamp, reads, writes)

    def barrier(self):
        tot = {e: self.cnt[e] for e in ENGS if self.cnt[e] > 0}
        for j in range(self.ndma):
            if self.dma_cnt[j] > 0:
                tot[("dma", j)] = self.dma_cnt[j]
        for e in ENGS:
            waits = [(k, v) for k, v in tot.items() if k != e and self.seen[e].get(k, 0) < v]
            for k, v in waits:
                self.seen[e][k] = v
            self.ops[e].append((None, waits, None, 0))

    def wait_all(self, eng, bufs):
        waits = self._deps(eng, bufs, ())
        self.ops[eng].append((None, waits, None, 0))

    def emit(self):
        nc = self.nc
        with contextlib.ExitStack() as st:
            sems = {}
            for e in ENGS:
                sems[e] = st.enter_context(nc.semaphore("s_" + e))
            for j in range(self.ndma):
                sems[("dma", j)] = st.enter_context(nc.semaphore("s_dma%d" % j))
            block = st.enter_context(nc.Block())

            def run(eng_name):
                def body(eng):
                    for fn, waits, stamp, incv in self.ops[eng_name]:
                        for k, v in waits:
                            eng.wait_ge(sems[k], v)
                        if fn is None:
                            continue
                        ins = fn(eng)
                        ins.then_inc(sems[stamp[0]], incv)
                return body

            block.tensor(run("pe"))
            block.scalar(run("act"))
            block.vector(run("dve"))
            block.gpsimd(run("pool"))
            block.sync(run("sp"))


def build(mode, debug=False):
    nc = bass.Bass("TRN2", target_bir_lowering=False)
    P = Prog(nc)
    doA = "A" in mode
    doB = "B" in mode
    fused = mode == "AB"

    def din(name, shape, dt=F32):
        return nc.dram_tensor(name, shape, dt, kind="ExternalInput")

    def dout(name, shape, dt=F32):
        return nc.dram_tensor(name, shape, dt, kind="ExternalOutput")

    cfl_d = din("cflags", [128, 8])
    consts_d = din("consts", [128, 4, 128])
    gains_d = din("gains", [8, D])
    mem_d = din("mem", [256, D])
    gsrc_d = din("gsrc", [2, D])
    mem_wq = din("mem_wq", [2, D, D]); mem_wk = din("mem_wk", [2, D, D])
    mem_wv = din("mem_wv", [2, D, D]); mem_wo = din("mem_wo", [2, D, D])
    ff_w1 = din("ff_w1", [2, D, 4 * D]); ff_w2 = din("ff_w2", [2, 4 * D, D])
    if doA:
        x_d = din("x", [NTOK, D])
        xhalo_d = din("xhalo", [2, 32, D])
        pw1_d = din("pw1", [D, 2 * D]); pw2_d = din("pw2", [D, D])
        b1col_d = din("b1col", [128, 16]); dwcol_d = din("dwcol", [128, 8, 31])
        bdwcol_d = din("bdwcol", [128, 8]); lngcol_d = din("lngcol", [128, 8]); lnbcol_d = din("lnbcol", [128, 8])
        b2row_d = din("b2row", [1, D])
        kvf_d = din("kvf", [D, 2 * D + 16])
    if doB:
        fgb_d = din("fgbrep", [128, 512])
        masks_d = din("masks", [128, 8, 512])
        fox_wq = din("fox_wq", [D, D]); fox_wo = din("fox_wo", [D, D])
        out_d = dout("out", [NTOK, D])
        if debug:
            dbg_cK = dout("dbg_cK", [128, 512]); dbg_Rown = dout("dbg_Rown", [128, 256]); dbg_bias = dout("dbg_bias", [128, 4, 512])
            dbg_OT = dout("dbg_OT", [128, 8, TT], BF16); dbg_QT = dout("dbg_QT", [128, 8, TT], BF16)
            dbg_fl = dout("dbg_fl", [128, 512])
    if fused:
        KTd = nc.dram_tensor("KTd", [128, 8 * NTOK], BF16)
        Vd = nc.dram_tensor("Vd", [NTOK, D], BF16)
        Fd = nc.dram_tensor("Fd", [NTOK, 16], F32)
        KTg = nc.dram_tensor("KTg", [256, 8 * NTOK], BF16)
        Vg = nc.dram_tensor("Vg", [2 * NTOK, D], BF16)
        Fg = nc.dram_tensor("Fg", [2 * NTOK, 16], F32)
        x1_d = nc.dram_tensor("x1s", [NTOK, D], F32)
    else:
        if doA:
            KTd = dout("KTd", [128, 8 * NTOK], BF16)
            Vd = dout("Vd", [NTOK, D], BF16)
            Fd = dout("Fd", [NTOK, 16], F32)
            x1_d = dout("x1", [NTOK, D])
        if doB:
            KTg = din("KTg", [256, 8 * NTOK], BF16)
            Vg = din("Vg", [2 * NTOK, D], BF16)
            Fg = din("Fg", [2 * NTOK, 16], F32)
            x1_d = din("x1", [NTOK, D])
    x1_b = [Buf(), Buf()]

    with contextlib.ExitStack() as st:
        def sb(name, shape, dt):
            return st.enter_context(nc.sbuf_tensor("s_" + name, shape, dt))

        def psum(name, shape, dt):
            return st.enter_context(nc.psum_tensor("p_" + name, shape, dt))

        x_sb = sb("x_sb", [128, 8, D], F32); x_b = [Buf() for _ in range(8)]
        hT = sb("hT", [128, 8, TT], BF16); hT_b = [Buf() for _ in range(8)]
        W = [sb("w%d" % i, [128, 8, 1024], BF16) for i in range(3)]; W_b = [Buf() for _ in range(3)]
        wrr = [0]
        gbc = sb("gbc", [128, D], F32); gbc_b = Buf()
        cfl = sb("cfl", [128, 8], F32); cfl_b = Buf()
        cst = sb("cst", [128, 4, 128], F32); cst_b = Buf()
        ident = sb("ident", [128, 128], BF16); ones_bf = sb("ones_bf", [128, 128], BF16); cbf_b = Buf()
        meanw = sb("meanw", [128, 128], BF16)
        ssq = sb("ssq", [128, 16], F32); ssq_b = [Buf() for _ in range(16)]; srr = [0]
        xn = [sb("xn%d" % i, [128, D], BF16) for i in range(2)]; xn_b = [Buf(), Buf()]; xrr = [0]
        psF = [psum("psF%d" % i, [128, 512], F32) for i in range(6)]; psF_b = [Buf() for _ in range(6)]; prr = [0]
        psB = [psum("psB%d" % i, [128, 1024], BF16) for i in range(2)]; psB_b = [Buf(), Buf()]; pbr = [0]
        actA = sb("actA", [128, 8, 1056], BF16); actA_b = Buf()
        actB = [sb("actB%d" % i, [128, 8, 512], BF16) for i in range(2)]; actB_b = [Buf(), Buf()]; abr = [0]
        f32s = [sb("f32s%d" % i, [128, 512], F32) for i in range(3)]; f32s_b = [Buf() for _ in range(3)]; frr = [0]
        mpt = [sb("mpt%d" % i, [128, 512], BF16) for i in range(4)]; mpt_b = [Buf() for _ in range(4)]; mpr = [0]
        KmT = sb("KmT", [128, 8, 256], BF16); KmT_b = Buf()
        Vm = sb("Vm", [128, 2, D], BF16); Vm_b = Buf()
        memT = actB[0][:, :, 0:256]; memT_b = actB_b[0]
        memt = sb("memt", [128, D], F32); memt_b = Buf()
        big = [sb("big%d" % i, [128, 4096], BF16) for i in range(2)]; big_b = [Buf(), Buf()]
        fs = sb("fs", [128, 5, 512], F32); fs_b = [Buf() for _ in range(5)]
        Rown = sb("Rown", [128, 256], F32); Rown_b = Buf()
        out_b = Buf()

        def next_ps():
            i = prr[0]; prr[0] = (i + 1) % 4
            return psF[i], psF_b[i]

        arr = [0]

        def acc_ps():
            i = 4 + arr[0]; arr[0] ^= 1
            return psF[i], psF_b[i]

        def next_pt():
            i = mpr[0]; mpr[0] = (i + 1) % 4
            return mpt[i], mpt_b[i]

        def next_f32():
            i = frr[0]; frr[0] = (i + 1) % 3
            return f32s[i], f32s_b[i]

        def next_actB():
            i = abr[0]; abr[0] = (i + 1) % 2
            return actB[i], actB_b[i]

        def dma(eng, out, in_, reads=(), writes=()):
            P.dma(eng, lambda e: e.dma_start(out=out, in_=in_), reads=reads, writes=writes)

        def load_w(src):
            i = wrr[0]; wrr[0] = (i + 1) % 3
            n = src.shape[1]
            for c in range(8):
                dma("pool", W[i][:, c, 0:n], src[c * 128:(c + 1) * 128, :], writes=[W_b[i]])
            return W[i], W_b[i]

        def load_gain(src_row):
            dma("sp", gbc[:], src_row.broadcast_to([128, D]), writes=[gbc_b])
            return gbc, gbc_b

        def mm_group(ps_ap, pairs, reads, ps_b):
            n = len(pairs)

            def fn(e):
                ins = None
                for i, (l, r) in enumerate(pairs):
                    ins = e.matmul(ps_ap, lhsT=l, rhs=r, start=(i == 0), stop=(i == n - 1))
                return ins
            P.op("pe", fn, reads=reads, writes=[ps_b])

        def act(out, in_, func, reads, writes, bias=None, scale=None, accum_out=None):
            kw = {}
            if bias is not None:
                kw["bias"] = bias
            if scale is not None:
                kw["scale"] = scale
            if accum_out is not None:
                kw["accum_out"] = accum_out
            P.op("act", lambda e: e.activation(out=out, in_=in_, func=func, **kw), reads=reads, writes=writes)

        def tt_op(eng, out, in0, in1, op, reads, writes):
            P.op(eng, lambda e: e.tensor_tensor(out=out, in0=in0, in1=in1, op=op), reads=reads, writes=writes)

        def ts_op(eng, out, in0, s1, op0, reads, writes):
            P.op(eng, lambda e: e.tensor_scalar(out=out, in0=in0, scalar1=s1, scalar2=None, op0=op0),
                 reads=reads, writes=writes)

        def stt_op(eng, out, in0, scalar, in1, op0, op1, reads, writes):
            P.op(eng, lambda e: e.scalar_tensor_tensor(out=out, in0=in0, scalar=scalar, in1=in1, op0=op0, op1=op1),
                 reads=reads, writes=writes)

        def recip(out, in_, reads, writes):
            P.op("dve", lambda e: e.reciprocal(out=out, in_=in_), reads=reads, writes=writes)

        def copy_op(eng, out, in_, reads, writes):
            if eng == "act":
                act(out, in_, AF.Copy, reads, writes)
            else:
                P.op(eng, lambda e: e.tensor_copy(out=out, in_=in_), reads=reads, writes=writes)

        evr = [0]

        def evac_eng():
            evr[0] ^= 1
            return "act" if evr[0] else "dve"

        def rms_rstd(src_ap, src_b, npart, xi):
            i = srr[0]; srr[0] = (i + 1) % 16
            s = ssq[0:npart, i:i + 1]
            P.op("pool", lambda e: e.memset(s, 0.0), writes=[ssq_b[i]])
            act(xn[xi][0:npart, :], src_ap, AF.Square, [src_b, ssq_b[i]], [xn_b[xi], ssq_b[i]], accum_out=s)
            act(s, s, AF.Sqrt, [ssq_b[i], cfl_b], [ssq_b[i]], bias=cfl[0:npart, 6:7], scale=1.0 / D)
            recip(s, s, [ssq_b[i]], [ssq_b[i]])
            return s, ssq_b[i]

        def norm_tile(src_ap, src_b, g, g_b, dstT, dst_b, col0, npart=128):
            i = xrr[0]; xrr[0] ^= 1
            s, s_b = rms_rstd(src_ap, src_b, npart, i)
            stt_op("dve", xn[i][0:npart, :], src_ap, s, g[0:npart, :], ALU.mult, ALU.mult,
                   [src_b, s_b, g_b], [xn_b[i]])
            j = pbr[0]; pbr[0] ^= 1
            pv = psB[j][:].rearrange("p (c t) -> p c t", c=8)

            def fn(e):
                ins = None
                for c in range(8):
                    ins = e.transpose(out=pv[:, c, 0:npart], in_=xn[i][0:npart, c * 128:(c + 1) * 128],
                                      identity=ident[0:npart, 0:npart])
                return ins
            P.op("pe", fn, reads=[xn_b[i], cbf_b], writes=[psB_b[j]])
            copy_op(evac_eng(), dstT[:, :, col0:col0 + npart], pv[:, :, 0:npart], [psB_b[j]], [dst_b])

        def norm_x(gidx):
            g, g_b = load_gain(gains_d[gidx:gidx + 1, :])
            for t in range(8):
                norm_tile(x_sb[:, t, :], x_b[t], g, g_b, hT, hT_b[t], t * 128)
            return g, g_b

        def proj_fm(Wt, W_bf, wcol0, nchunk, src, src_bs, tok0, ntok, epi):
            for c in range(nchunk):
                ps, ps_b = next_ps()
                pairs = [(Wt[:, k, wcol0 + c * 128: wcol0 + (c + 1) * 128], src[:, k, tok0:tok0 + ntok]) for k in range(8)]
                mm_group(ps[:, 0:ntok], pairs, [W_bf] + list(src_bs), ps_b)
                epi(c, ps, ps_b)

        def proj_tm_resid(actT, act_bs, Wt, W_bf, sub0, nsub, extra=None):
            for s in range(nsub):
                xt = sub0 + s
                for dh in range(2):
                    ps, ps_b = next_ps()
                    pairs = [(actT[:, k, s * 128:(s + 1) * 128], Wt[:, k, dh * 512:(dh + 1) * 512]) for k in range(8)]
                    rd = [W_bf] + list(act_bs)
                    if extra is not None:
                        pairs.append((extra[0], extra[1][0:1, dh * 512:(dh + 1) * 512]))
                        rd.append(extra[2])
                    mm_group(ps[:, :], pairs, rd, ps_b)
                    tt_op("dve", x_sb[:, xt, dh * 512:(dh + 1) * 512], ps[:, :], x_sb[:, xt, dh * 512:(dh + 1) * 512],
                          ALU.add, [ps_b, x_b[xt]], [x_b[xt]])

        dma("sp", cfl[:], cfl_d[:, :], writes=[cfl_b])
        dma("sp", cst[:], consts_d[:, :, :], writes=[cst_b])
        copy_op("dve", ident[:], cst[:, 0, :], [cst_b], [cbf_b])
        copy_op("dve", ones_bf[:], cst[:, 3, :], [cst_b], [cbf_b])
        ts_op("dve", meanw[:], cst[:, 3, :], 1.0 / D, ALU.mult, [cst_b], [cbf_b])

        def mem_kv(l):
            g, g_b = load_gain(gsrc_d[l:l + 1, :])
            for i in range(2):
                dma("sp", memt[:], mem_d[i * 128:(i + 1) * 128, :], writes=[memt_b])
                norm_tile(memt[:], memt_b, g, g_b, memT, memT_b, i * 128)
            Wk, Wk_b = load_w(mem_wk[l])

            def epi_k(c, ps, ps_b):
                copy_op(evac_eng(), KmT[:, c, :], ps[:, 0:256], [ps_b], [KmT_b])
            proj_fm(Wk, Wk_b, 0, 8, memT, [memT_b], 0, 256, epi_k)
            Wv, Wv_b = load_w(mem_wv[l])
            for mb in range(2):
                for dh in range(2):
                    ps, ps_b = next_ps()
                    pairs = [(memT[:, k, mb * 128:(mb + 1) * 128], Wv[:, k, dh * 512:(dh + 1) * 512]) for k in range(8)]
                    mm_group(ps[:, :], pairs, [Wv_b, memT_b], ps_b)
                    copy_op(evac_eng(), Vm[:, mb, dh * 512:(dh + 1) * 512], ps[:, :], [ps_b], [Vm_b])

        def mem_attn(l):
            norm_x(1 if l == 0 else 5)
            Wq, Wq_b = load_w(mem_wq[l])
            Wo, Wo_b = load_w(mem_wo[l])
            for tt in range(2):
                QT, QT_b = next_actB()

                def epi_q(c, ps, ps_b, QT=QT, QT_b=QT_b):
                    eng = evac_eng()
                    if eng == "act":
                        P.op("act", lambda e, c=c: e.mul(out=QT[:, c, :], in_=ps[:, :], mul=1.0 / 16), reads=[ps_b], writes=[QT_b])
                    else:
                        ts_op("dve", QT[:, c, :], ps[:, :], 1.0 / 16, ALU.mult, [ps_b], [QT_b])
                proj_fm(Wq, Wq_b, 0, 8, hT, hT_b[tt * 4:(tt + 1) * 4], tt * 512, 512, epi_q)
                OT, OT_b = next_actB()
                for hd in range(4):
                    pts = []
                    for mb in range(2):
                        ps, ps_b = next_ps()
                        pairs = [(KmT[:, 2 * hd + cc, mb * 128:(mb + 1) * 128], QT[:, 2 * hd + cc, :]) for cc in range(2)]
                        mm_group(ps[:, :], pairs, [KmT_b, QT_b], ps_b)
                        pt, pt_b = next_pt()
                        act(pt[:, :], ps[:, :], AF.Exp, [ps_b], [pt_b])
                        pts.append((pt, pt_b))
                    psl, psl_b = acc_ps()
                    mm_group(psl[:, :], [(ones_bf[:, :], pts[mb][0][:, :]) for mb in range(2)],
                             [cbf_b, pts[0][1], pts[1][1]], psl_b)
                    rl, rl_b = next_f32()
                    recip(rl[:], psl[:, :], [psl_b], [rl_b])
                    for dvc in range(2):
                        pso, pso_b = next_ps()
                        col = hd * 256 + dvc * 128
                        mm_group(pso[:, :], [(Vm[:, mb, col:col + 128], pts[mb][0][:, :]) for mb in range(2)],
                                 [Vm_b, pts[0][1], pts[1][1]], pso_b)
                        tt_op("dve", OT[:, 2 * hd + dvc, :], pso[:, :], rl[:], ALU.mult, [pso_b, rl_b], [OT_b])
                proj_tm_resid(OT, [OT_b], Wo, Wo_b, tt * 4, 4)

        def mlp(l):
            norm_x(2 if l == 0 else 6)
            for sl in range(4):
                W1, W1_b = load_w(ff_w1[l][:, sl * 1024:(sl + 1) * 1024])
                W2, W2_b = load_w(ff_w2[l][sl * 1024:(sl + 1) * 1024, :])
                for tt in range(2):
                    fT, fT_b = next_actB()

                    def epi_f(c, ps, ps_b, fT=fT, fT_b=fT_b):
                        r, r_b = next_f32()
                        act(r[:], ps[:, :], AF.Relu, [ps_b], [r_b])
                        tt_op("pool", fT[:, c, :], r[:], r[:], ALU.mult, [r_b], [fT_b])
                    proj_fm(W1, W1_b, 0, 8, hT, hT_b[tt * 4:(tt + 1) * 4], tt * 512, 512, epi_f)
                    proj_tm_resid(fT, [fT_b], W2, W2_b, tt * 4, 4)

        if doA:
            b1col = sb("b1col", [128, 16], F32); dwcol = Rown[:, 0:248].rearrange("p (c k) -> p c k", k=31)
            bdwcol = sb("bdwcol", [128, 8], F32); lngcol = sb("lngcol", [128, 8], F32); lnbcol = sb("lnbcol", [128, 8], F32)
            b2row = sb("b2row", [1, D], BF16); small_b = Buf()
            dma("sp", b1col[:], b1col_d[:, :], writes=[small_b])
            dma("sp", dwcol, dwcol_d[:, :, :], writes=[small_b])
            dma("sp", bdwcol[:], bdwcol_d[:, :], writes=[small_b])
            dma("sp", lngcol[:], lngcol_d[:, :], writes=[small_b])
            dma("sp", lnbcol[:], lnbcol_d[:, :], writes=[small_b])
            dma("pool", b2row[:], b2row_d[:, :], writes=[small_b])
            hTh = sb("hTh", [128, 8, 32], BF16); hTh_b = Buf()
            dg = big[0][:, 0:31 * 128].rearrange("p (k j) -> p k j", j=128); dg_b = big_b[0]
            vbuf = big[1][:, :].rearrange("p (c t) -> p c t", t=512); vbuf_b = big_b[1]
            gluT = actA
            xv = x_d.ap().rearrange("(t p) d -> p t d", p=128)
            x1v = x1_d.ap().rearrange("(t p) d -> p t d", p=128)
            mem_kv(0)
            for tile0 in range(2):
                for t in range(8):
                    dma("sp", x_sb[:, t, :], xv[:, tile0 * 8 + t, :], writes=[x_b[t]])
                g, g_b = norm_x(0)
                dma("sp", memt[0:32, :], xhalo_d[tile0], writes=[memt_b])
                norm_tile(memt[0:32, :], memt_b, g, g_b, hTh, hTh_b, 0, npart=32)
                Wa, Wa_b = load_w(pw1_d[:, 0:1024])
                Wg, Wg_b = load_w(pw1_d[:, 1024:2048])
                for c in range(8):
                    for (src, src_bs, tok0, ntok, col0) in ((hTh, [hTh_b], 0, 32, 0), (hT, hT_b[0:4], 0, 512, 32),
                                                            (hT, hT_b[4:8], 512, 512, 544)):
                        psa, psa_b = next_ps()
                        mm_group(psa[:, 0:ntok], [(Wa[:, k, c * 128:(c + 1) * 128], src[:, k, tok0:tok0 + ntok]) for k in range(8)],
                                 [Wa_b] + list(src_bs), psa_b)
                        psg, psg_b = next_ps()
                        mm_group(psg[:, 0:ntok], [(Wg[:, k, c * 128:(c + 1) * 128], src[:, k, tok0:tok0 + ntok]) for k in range(8)],
                                 [Wg_b] + list(src_bs), psg_b)
                        sg, sg_b = next_f32()
                        act(sg[:, 0:ntok], psg[:, 0:ntok], AF.Sigmoid, [psg_b, small_b], [sg_b], bias=b1col[:, 8 + c:9 + c])
                        stt_op("dve", gluT[:, c, col0:col0 + ntok], psa[:, 0:ntok], b1col[:, c:c + 1], sg[:, 0:ntok],
                               ALU.add, ALU.mult, [psa_b, sg_b, small_b], [actA_b])
                ts_op("pool", gluT[:, :, 0:32], gluT[:, :, 0:32], cfl[:, tile0:tile0 + 1], ALU.mult,
                      [actA_b, cfl_b], [actA_b])
                W2c, W2c_b = load_w(pw2_d[:, :])
                for tt in range(2):
                    psM, psM_b = acc_ps()
                    psE, psE_b = acc_ps()
                    for c in range(8):
                        for k in range(31):
                            ts_op("pool", dg[:, k, :], ident[:, :], dwcol[:, c, k:k + 1], ALU.mult,
                                  [cbf_b, small_b], [dg_b])
                        ps, ps_b = next_ps()
                        base = tt * 512 + 2
                        mm_group(ps[:, :], [(dg[:, k, :], gluT[:, c, base + k: base + k + 512]) for k in range(31)],
                                 [dg_b, actA_b], ps_b)
                        act(vbuf[:, c, :], ps[:, :], AF.Identity, [ps_b, small_b], [vbuf_b], bias=bdwcol[:, c:c + 1])
                        v2, v2_b = next_pt()
                        tt_op("pool", v2[:, :], vbuf[:, c, :], vbuf[:, c, :], ALU.mult, [vbuf_b], [v2_b])

                        def fn_m(e, c=c, psM=psM):
                            return e.matmul(psM[:, :], lhsT=meanw[:, :], rhs=vbuf[:, c, :], start=(c == 0), stop=(c == 7))
                        P.op("pe", fn_m, reads=[vbuf_b, cbf_b], writes=[psM_b])

                        def fn_e(e, c=c, psE=psE, v2=v2):
                            return e.matmul(psE[:, :], lhsT=meanw[:, :], rhs=v2[:, :], start=(c == 0), stop=(c == 7))
                        P.op("pe", fn_e, reads=[v2_b, cbf_b], writes=[psE_b])
                    mean, msq, rstd = fs[:, 0, :], fs[:, 1, :], fs[:, 2, :]
                    copy_op("act", mean, psM[:, :], [psM_b], [fs_b[0]])
                    tt_op("dve", msq, mean, mean, ALU.mult, [fs_b[0]], [fs_b[1]])
                    tt_op("dve", rstd, psE[:, :], msq, ALU.subtract, [psE_b, fs_b[1]], [fs_b[2]])
                    act(rstd, rstd, AF.Sqrt, [fs_b[2], cfl_b], [fs_b[2]], bias=cfl[:, 7:8])
                    recip(rstd, rstd, [fs_b[2]], [fs_b[2]])
                    zT, zT_b = next_actB()
                    for c in range(8):
                        d, d_b = next_f32()
                        tt_op("dve", d[:], vbuf[:, c, :], mean, ALU.subtract, [vbuf_b, fs_b[0]], [d_b])
                        tt_op("pool", d[:], d[:], rstd, ALU.mult, [d_b, fs_b[2]], [d_b])
                        act(zT[:, c, :], d[:], AF.Silu, [d_b, small_b], [zT_b], bias=lnbcol[:, c:c + 1], scale=lngcol[:, c:c + 1])
                    proj_tm_resid(zT, [zT_b], W2c, W2c_b, tt * 4, 4, extra=(ones_bf[0:1, :], b2row, small_b))
                mem_attn(0)
                mlp(0)
                for t in range(8):
                    dma("sp", x1v[:, tile0 * 8 + t, :], x_sb[:, t, :], reads=[x_b[t]], writes=[out_b, x1_b[tile0]])
                norm_x(3)
                Wkk, Wkk_b = load_w(kvf_d[:, 0:1024])
                kst = actA; kst_b = actA_b
                for tt in range(2):
                    def epi_kk(c, ps, ps_b, tt=tt):
                        copy_op(evac_eng(), kst[:, c, tt * 512:(tt + 1) * 512], ps[:, :], [ps_b], [kst_b])
                    proj_fm(Wkk, Wkk_b, 0, 8, hT, hT_b[tt * 4:(tt + 1) * 4], tt * 512, 512, epi_kk)
                KTv = KTd.ap().rearrange("p (c t) -> p c t", c=8)
                for c in range(8):
                    dma("sp", KTv[:, c, tile0 * TT:(tile0 + 1) * TT], kst[:, c, 0:TT], reads=[kst_b], writes=[out_b])
                Wvv, Wvv_b = load_w(kvf_d[:, 1024:2048])
                Wff, Wff_b = load_w(kvf_d[:, 2048:2064])
                fst, fst_b = next_f32()
                for s in range(8):
                    vst, vst_b = next_actB()
                    vflat = vst[:].rearrange("p c t -> p (c t)")
                    for dh in range(2):
                        ps, ps_b = next_ps()
                        mm_group(ps[:, :], [(hT[:, k, s * 128:(s + 1) * 128], Wvv[:, k, dh * 512:(dh + 1) * 512]) for k in range(8)],
                                 [Wvv_b, hT_b[s]], ps_b)
                        copy_op(evac_eng(), vflat[:, dh * 512:(dh + 1) * 512], ps[:, :], [ps_b], [vst_b])
                    r0 = tile0 * TT + s * 128
                    dma("sp", Vd[r0:r0 + 128, :], vflat[:, 0:D], reads=[vst_b], writes=[out_b])
                    ps, ps_b = next_ps()
                    mm_group(ps[:, 0:16], [(hT[:, k, s * 128:(s + 1) * 128], Wff[:, k, 0:16]) for k in range(8)],
                             [Wff_b, hT_b[s]], ps_b)
                    copy_op("dve", fst[:, s * 16:(s + 1) * 16], ps[:, 0:16], [ps_b], [fst_b])
                Fv = Fd.ap().rearrange("(s p) h -> p s h", p=128)
                dma("sp", Fv[:, tile0 * 8:(tile0 + 1) * 8, :], fst[:, 0:128].rearrange("p (s h) -> p s h", h=16),
                    reads=[fst_b], writes=[out_b])

        gath_b = Buf()
        if fused:
            rg = [[0, 1], [2, 3], [4, 5], [6, 7]]
            for (src, dst) in ((KTd, KTg), (Vd, Vg), (Fd, Fg)):
                P.op("pool", lambda e, src=src, dst=dst: e.collective_compute(
                    "AllGather", ALU.bypass, replica_groups=rg, ins=[src.ap().opt()], outs=[dst.ap().opt()]),
                    reads=[out_b], writes=[gath_b])
            P.barrier()

        if doB:
            x1v = x1_d.ap().rearrange("(t p) d -> p t d", p=128)
            fl, Eb, fgb = f32s[0], f32s[1], f32s[2]
            fl_b, Eb_b, fgb_b = f32s_b[0], f32s_b[1], f32s_b[2]
            cK = fs[:, 4, :]; cK_b = fs_b[4]
            masks = sb("masks", [128, 8, 512], BF16); masks_b = Buf()
            ksb = sb("ksb", [128, 4096], BF16); ksb_b = Buf()
            vsb = [big[i][:, :].rearrange("p (b d) -> p b d", d=128) for i in range(2)]; vsb_b = big_b
            mem_kv(1)
            dma("sp", fgb[:], fgb_d[:, :], writes=[fgb_b])
            for i in range(8):
                dma("pool", masks[:, i, :], masks_d[:, i, :], writes=[masks_b])
            Fgv = Fg.ap().rearrange("(b p) h -> p b h", p=128)
            dma("sp", fl[:].rearrange("p (b h) -> p b h", h=16), Fgv, reads=[gath_b], writes=[fl_b])
            tt_op("dve", fl[:], fl[:], fgb[:], ALU.add, [fl_b, fgb_b], [fl_b])
            act(fl[:], fl[:], AF.Exp, [fl_b], [fl_b], scale=-1.0)
            act(fl[:], fl[:], AF.Ln, [fl_b, cfl_b], [fl_b], bias=cfl[:, 5:6])
            P.op("dve", lambda e: e.memset(Eb[:, 0:16], 0.0), writes=[Eb_b])
            for j in range(1, 32):
                tt_op("dve", Eb[:, j * 16:(j + 1) * 16], Eb[:, (j - 1) * 16:j * 16], fl[:, (j - 1) * 16:j * 16], ALU.add,
                      [Eb_b, fl_b], [Eb_b])
            ps, ps_b = next_ps()
            mm_group(ps[:, :], [(cst[:, 1, :], fl[:]), (cst[:, 3, :], Eb[:])], [cst_b, fl_b, Eb_b], ps_b)
            if debug:
                dma("sp", dbg_fl[:, :], fl[:], reads=[fl_b], writes=[out_b])
            copy_op("dve", cK, ps[:, :], [ps_b], [cK_b])
            ps, ps_b = next_ps()
            mm_group(ps[:, :], [(cst[:, 2, :], cK)], [cst_b, cK_b], ps_b)
            tmpR, tmpR_b = f32s[2], f32s_b[2]
            ts_op("dve", tmpR[:, 0:256], ps[:, 0:256], cfl[:, 2:3], ALU.mult, [ps_b, cfl_b], [tmpR_b])
            stt_op("dve", Rown[:], ps[:, 256:512], cfl[:, 3:4], tmpR[:, 0:256], ALU.mult, ALU.add,
                   [ps_b, cfl_b, tmpR_b], [Rown_b])
            P.op("pool", lambda e: e.memset(vsb[0][:, :, 64:128], 1.0), writes=[vsb_b[0]])
            P.op("pool", lambda e: e.memset(vsb[1][:, :, 0:64], 1.0), writes=[vsb_b[1]])
            QT = actA; QT_b = actA_b
            KTgv = KTg.ap().rearrange("p (c t) -> p c t", c=8)
            Vgv = Vg.ap().rearrange("(b p) d -> p b d", p=128)
            ov = out_d.ap().rearrange("(t p) d -> p t d", p=128)
            for tile0 in range(2):
                for t in range(8):
                    dma("sp", x_sb[:, t, :], x1v[:, tile0 * 8 + t, :], reads=[x1_b[tile0]], writes=[x_b[t]])
                for gl in range(4):
                    g = tile0 * 4 + gl
                    for kb in range(32):
                        tt_op("dve", fs[:, gl, kb * 16:(kb + 1) * 16], cK[:, kb * 16:(kb + 1) * 16],
                              Rown[:, (2 * g + 1) * 16:(2 * g + 2) * 16], ALU.subtract, [cK_b, Rown_b], [fs_b[gl]])
                    j = g // 2
                    lo, hi = (4 * j + 4) * 16, (16 + 4 * j + 4) * 16
                    ts_op("dve", fs[:, gl, lo:hi], fs[:, gl, lo:hi], cfl[:, 4:5], ALU.add, [fs_b[gl], cfl_b], [fs_b[gl]])
                norm_x(4)
                Wq, Wq_b = load_w(fox_wq[:, :])
                for tt in range(2):
                    def epi_q(c, ps, ps_b, tt=tt):
                        eng = evac_eng()
                        if eng == "act":
                            P.op("act", lambda e, c=c, tt=tt: e.mul(out=QT[:, c, tt * 512:(tt + 1) * 512], in_=ps[:, :], mul=0.125),
                                 reads=[ps_b], writes=[QT_b])
                        else:
                            ts_op("dve", QT[:, c, tt * 512:(tt + 1) * 512], ps[:, :], 0.125, ALU.mult, [ps_b], [QT_b])
                    proj_fm(Wq, Wq_b, 0, 8, hT, hT_b[tt * 4:(tt + 1) * 4], tt * 512, 512, epi_q)
                Wo, Wo_b = load_w(fox_wo[:, :])
                OTb = hT
                for hp in range(8):
                    for r in range(2):
                        dma("sp", ksb[:, r * NTOK:(r + 1) * NTOK], KTgv[r * 128:(r + 1) * 128, hp, :],
                            reads=[gath_b], writes=[ksb_b])
                    for e2 in range(2):
                        head = 2 * hp + e2
                        vcol = 0 if e2 == 0 else 64
                        for bq in range(4):
                            dma("sp", vsb[e2][:, bq * 8:(bq + 1) * 8, vcol:vcol + 64],
                                Vgv[:, bq * 8:(bq + 1) * 8, head * 64:(head + 1) * 64],
                                reads=[gath_b], writes=[vsb_b[e2]])
                        pr = slice(e2 * 64, (e2 + 1) * 64)
                        lr = slice((1 - e2) * 64, (2 - e2) * 64)
                        for jl in range(2):
                            j = tile0 * 2 + jl
                            nkb = 16 + 4 * j + 4
                            pso, pso_b = acc_ps()
                            for kb in range(nkb):
                                pss, pss_b = next_ps()
                                pairs = [(ksb[pr, kb * 128:(kb + 1) * 128], QT[pr, hp, jl * 512:(jl + 1) * 512])]
                                rd = [ksb_b, QT_b]
                                if 4 * j <= kb < 4 * j + 4:
                                    pairs.append((ident[:, :], masks[:, kb - 4 * j, :])); rd += [cbf_b, masks_b]
                                elif kb >= 16 + 4 * j:
                                    pairs.append((ident[:, :], masks[:, 4 + kb - 16 - 4 * j, :])); rd += [cbf_b, masks_b]
                                mm_group(pss[:, :], pairs, rd, pss_b)
                                pt, pt_b = next_pt()
                                for gg in range(2):
                                    gl = 2 * jl + gg
                                    act(pt[:, gg * 256:(gg + 1) * 256], pss[:, gg * 256:(gg + 1) * 256], AF.Exp,
                                        [pss_b, fs_b[gl]], [pt_b], bias=fs[:, gl, kb * 16 + head: kb * 16 + head + 1])

                                def fn_pv(e, kb=kb, nkb=nkb, pso=pso, e2=e2, pt=pt):
                                    return e.matmul(pso[:, :], lhsT=vsb[e2][:, kb, :], rhs=pt[:, :],
                                                    start=(kb == 0), stop=(kb == nkb - 1))
                                P.op("pe", fn_pv, reads=[vsb_b[e2], pt_b], writes=[pso_b])
                            rl, rl_b = next_f32()
                            recip(rl[lr, :], pso[lr, :], [pso_b], [rl_b])
                            tt_op("dve", OTb[pr, hp, jl * 512:(jl + 1) * 512], pso[pr, :], rl[lr, :], ALU.mult,
                                  [pso_b, rl_b], hT_b[jl * 4:(jl + 1) * 4])
                if debug and tile0 == 0:
                    dma("sp", dbg_cK[:, :], cK, reads=[cK_b], writes=[out_b])
                    dma("sp", dbg_Rown[:, :], Rown[:], reads=[Rown_b], writes=[out_b])
                    dma("sp", dbg_bias[:, :, :], fs[:, 0:4, :], reads=fs_b[0:4], writes=[out_b])
                    dma("sp", dbg_OT[:, :, :], OTb[:, :, :], reads=hT_b, writes=[out_b])
                    dma("sp", dbg_QT[:, :, :], QT[:, :, 0:TT], reads=[QT_b], writes=[out_b])
                for tt in range(2):
                    proj_tm_resid(OTb[:, :, tt * 512:(tt + 1) * 512], hT_b[tt * 4:(tt + 1) * 4], Wo, Wo_b, tt * 4, 4)
                if debug:
                    for t in range(8):
                        dma("sp", ov[:, tile0 * 8 + t, :], x_sb[:, t, :], reads=[x_b[t]], writes=[out_b])
                    break
                mem_attn(1)
                mlp(1)
                g, g_b = load_gain(gains_d[7:8, :])
                for t in range(8):
                    xi = xrr[0]; xrr[0] ^= 1
                    s, s_b = rms_rstd(x_sb[:, t, :], x_b[t], 128, xi)
                    stt_op("dve", x_sb[:, t, :], x_sb[:, t, :], s, g[:, :], ALU.mult, ALU.mult,
                           [x_b[t], s_b, g_b], [x_b[t]])
                    dma("sp", ov[:, tile0 * 8 + t, :], x_sb[:, t, :], reads=[x_b[t]], writes=[out_b])
        P.wait_all("sp", [out_b])
        P.emit()
    return nc


def _consts():
    c = np.zeros((128, 4, 128), np.float32)
    c[:, 0, :] = np.eye(128, dtype=np.float32)
    c[:, 1, :] = np.triu(np.ones((128, 128), np.float32))
    c[0, 2, :] = 1.0
    c[:, 3, :] = 1.0
    return c


def _masks(h):
    p = np.arange(128)[:, None]
    q = np.arange(512)[None, :]
    m = np.zeros((128, 8, 512), np.float32)
    for o in range(4):
        tri = np.where(o * 128 + p <= q, 0.0, NEG).astype(np.float32)
        m[:, o, :] = tri if h == 0 else 0.0
        m[:, 4 + o, :] = tri
    return m


def _common_inputs(inp, c):
    b, h = c // 2, c % 2
    gains = np.stack([inp["norm_mix_g"][0], inp["norm_mem_g"][0], inp["norm_ff_g"][0], inp["kv_norm_g"],
                      inp["norm_mix_g"][1], inp["norm_mem_g"][1], inp["norm_ff_g"][1], inp["final_norm_g"]]).astype(np.float32)
    cfl = np.zeros((128, 8), np.float32)
    cfl[:, 0] = float(h)
    cfl[:, 1] = 1.0
    cfl[:, 2] = 1.0 - h
    cfl[:, 3] = float(h)
    cfl[:, 4] = NEG * (1 - h)
    cfl[:, 5] = 1.0
    cfl[:, 6] = 1e-6
    cfl[:, 7] = 1e-5
    return {
        "cflags": cfl, "consts": _consts(), "gains": gains,
        "mem": np.ascontiguousarray(inp["mem"][b]), "gsrc": np.ascontiguousarray(inp["norm_memsrc_g"]),
        "mem_wq": inp["mem_wq"], "mem_wk": inp["mem_wk"], "mem_wv": inp["mem_wv"], "mem_wo": inp["mem_wo"],
        "ff_w1": inp["ff_w1"], "ff_w2": inp["ff_w2"],
    }


def _a_inputs(inp, c):
    b, h = c // 2, c % 2
    x = inp["x"]
    xh = np.zeros((2, 32, D), np.float32)
    if h == 1:
        xh[0] = x[b, NTOK - 32:NTOK]
    xh[1] = x[b, h * NTOK + TT - 32: h * NTOK + TT]
    col = lambda v: np.ascontiguousarray(v.reshape(-1, 128).T)
    return {
        "x": np.ascontiguousarray(x[b, h * NTOK:(h + 1) * NTOK]), "xhalo": xh,
        "pw1": inp["conv_pw1_w"][0], "pw2": inp["conv_pw2_w"][0],
        "b1col": col(inp["conv_pw1_b"][0]),
        "dwcol": np.ascontiguousarray(inp["conv_dw_w"][0].reshape(31, 8, 128).transpose(2, 1, 0)),
        "bdwcol": col(inp["conv_dw_b"][0]), "lngcol": col(inp["conv_ln_g"][0]), "lnbcol": col(inp["conv_ln_b"][0]),
        "b2row": np.ascontiguousarray(inp["conv_pw2_b"][0][None, :]),
        "kvf": inp["kvf_w"],
    }


def _b_inputs(inp, c):
    h = c % 2
    return {
        "fgbrep": np.ascontiguousarray(np.broadcast_to(np.tile(inp["fgate_b"], 32)[None, :], (128, 512))).astype(np.float32),
        "masks": _masks(h),
        "fox_wq": inp["fox_wq"][0], "fox_wo": inp["fox_wo"][0],
    }


_NC_CACHE = {}


def _get_nc(mode):
    if mode not in _NC_CACHE:
        _NC_CACHE[mode] = build(mode)
    return _NC_CACHE[mode]


def kernel(**inp):
    inp = {k: np.asarray(v) for k, v in inp.items()}
    cores = list(range(8))
    if FUSED:
        maps = []
        for c in cores:
            m = _common_inputs(inp, c); m.update(_a_inputs(inp, c)); m.update(_b_inputs(inp, c))
            maps.append(m)
        res = run_bass_kernel_spmd(_get_nc("AB"), maps, core_ids=cores)
        outs = [res.results[c]["out"] for c in cores]
    else:
        maps = []
        for c in cores:
            m = _common_inputs(inp, c); m.update(_a_inputs(inp, c))
            maps.append(m)
        ra = run_bass_kernel_spmd(_get_nc("A"), maps, core_ids=cores).results
        maps = []
        for c in cores:
            p = c - c % 2
            m = _common_inputs(inp, c); m.update(_b_inputs(inp, c))
            m["x1"] = ra[c]["x1"]
            m["KTg"] = np.concatenate([ra[p]["KTd"], ra[p + 1]["KTd"]], axis=0)
            m["Vg"] = np.concatenate([ra[p]["Vd"], ra[p + 1]["Vd"]], axis=0)
            m["Fg"] = np.concatenate([ra[p]["Fd"], ra[p + 1]["Fd"]], axis=0)
            maps.append(m)
        rb = run_bass_kernel_spmd(_get_nc("B"), maps, core_ids=cores).results
        outs = [rb[c]["out"] for c in cores]
    out = np.zeros((4, 4096, D), np.float32)
    for c in cores:
        out[c // 2, (c % 2) * NTOK:(c % 2 + 1) * NTOK] = outs[c]
    return out
```

```python
import contextlib
import numpy as np
import ml_dtypes
import concourse.bass as bass
import concourse.mybir as mybir
from concourse.bass_utils import run_bass_kernel_spmd

F32 = mybir.dt.float32
BF16 = mybir.dt.bfloat16
AF = mybir.ActivationFunctionType
ALU = mybir.AluOpType

ENGS = ["pe", "act", "dve", "pool", "sp"]
D = 1024
NTOK = 2048
TT = 1024
NEG = -30000.0
FUSED = True


class Buf:
    __slots__ = ("w", "r")

    def __init__(self):
        self.w = {}
        self.r = {}


class Prog:
    def __init__(self, nc, ndma=48):
        self.nc = nc
        self.ops = {e: [] for e in ENGS}
        self.cnt = {e: 0 for e in ENGS}
        self.seen = {e: {} for e in ENGS}
        self.ndma = ndma
        self.dma_cnt = [0] * ndma
        self.dma_rr = 0
        self.dma_rr_pool = 0

    def _deps(self, eng, reads, writes):
        waits = {}
        seen = self.seen[eng]

        def need(k, v):
            if seen.get(k, 0) >= v:
                return
            if waits.get(k, 0) < v:
                waits[k] = v
        for b in reads:
            for k, v in b.w.items():
                need(k, v)
        for b in writes:
            for k, v in b.w.items():
                need(k, v)
            for k, v in b.r.items():
                need(k, v)
        for k, v in waits.items():
            seen[k] = v
        return list(waits.items())

    def _mark(self, stamp, reads, writes):
        k, v = stamp
        for b in reads:
            if b.r.get(k, 0) < v:
                b.r[k] = v
        for b in writes:
            if b.w.get(k, 0) < v:
                b.w[k] = v

    def op(self, eng, fn, reads=(), writes=()):
        waits = self._deps(eng, reads, writes)
        self.cnt[eng] += 1
        stamp = (eng, self.cnt[eng])
        self.ops[eng].append((fn, waits, stamp, 1))
        self._mark(stamp, reads, writes)

    def dma(self, eng, fn, reads=(), writes=()):
        waits = self._deps(eng, reads, writes)
        half = self.ndma // 2
        if eng == "pool":
            j = half + self.dma_rr_pool
            self.dma_rr_pool = (self.dma_rr_pool + 1) % (self.ndma - half)
        else:
            j = self.dma_rr
            self.dma_rr = (self.dma_rr + 1) % half
        prev = self.dma_cnt[j]
        if prev > 0 and self.seen[eng].get(("dma", j), 0) < prev:
            waits.append((("dma", j), prev))
            self.seen[eng][("dma", j)] = prev
        self.dma_cnt[j] += 16
        stamp = (("dma", j), self.dma_cnt[j])
        self.ops[eng].append((fn, waits, stamp, 16))
        self._mark(stamp, reads, writes)

    def barrier(self):
        tot = {e: self.cnt[e] for e in ENGS if self.cnt[e] > 0}
        for j in range(self.ndma):
            if self.dma_cnt[j] > 0:
                tot[("dma", j)] = self.dma_cnt[j]
        for e in ENGS:
            waits = [(k, v) for k, v in tot.items() if self.seen[e].get(k, 0) < v]
            for k, v in waits:
                self.seen[e][k] = v
            self.ops[e].append((None, waits, None, 0))

    def wait_all(self, eng, bufs):
        waits = self._deps(eng, bufs, ())
        self.ops[eng].append((None, waits, None, 0))

    def emit(self):
        nc = self.nc
        with contextlib.ExitStack() as st:
            sems = {}
            for e in ENGS:
                sems[e] = st.enter_context(nc.semaphore("s_" + e))
            for j in range(self.ndma):
                sems[("dma", j)] = st.enter_context(nc.semaphore("s_dma%d" % j))
            block = st.enter_context(nc.Block())

            def run(eng_name):
                def body(eng):
                    for fn, waits, stamp, incv in self.ops[eng_name]:
                        for k, v in waits:
                            eng.wait_ge(sems[k], v)
                        if fn is None:
                            continue
                        ins = fn(eng)
                        ins.then_inc(sems[stamp[0]], incv)
                return body

            block.tensor(run("pe"))
            block.scalar(run("act"))
            block.vector(run("dve"))
            block.gpsimd(run("pool"))
            block.sync(run("sp"))


def build(mode, debug=False):
    nc = bass.Bass("TRN2", target_bir_lowering=False)
    P = Prog(nc)
    doA = "A" in mode
    doB = "B" in mode
    fused = mode == "AB"

    def din(name, shape, dt=F32):
        return nc.dram_tensor(name, shape, dt, kind="ExternalInput")

    def dout(name, shape, dt=F32):
        return nc.dram_tensor(name, shape, dt, kind="ExternalOutput")

    cfl_d = din("cflags", [128, 8])
    consts_d = din("consts", [128, 4, 128])
    gains_d = din("gains", [8, D])
    mem_d = din("mem", [256, D])
    gsrc_d = din("gsrc", [2, D])
    mem_wq = din("mem_wq", [2, D, D]); mem_wk = din("mem_wk", [2, D, D])
    mem_wv = din("mem_wv", [2, D, D]); mem_wo = din("mem_wo", [2, D, D])
    ff_w1 = din("ff_w1", [2, D, 4 * D]); ff_w2 = din("ff_w2", [2, 4 * D, D])
    if doA:
        x_d = din("x", [NTOK, D])
        xhalo_d = din("xhalo", [2, 32, D])
        pw1_d = din("pw1", [D, 2 * D]); pw2_d = din("pw2", [D, D])
        b1col_d = din("b1col", [128, 16]); dwcol_d = din("dwcol", [128, 8, 31])
        bdwcol_d = din("bdwcol", [128, 8]); lngcol_d = din("lngcol", [128, 8]); lnbcol_d = din("lnbcol", [128, 8])
        b2row_d = din("b2row", [1, D])
        kvf_d = din("kvf", [D, 2 * D + 16])
    if doB:
        fgb_d = din("fgbrep", [128, 512])
        masks_d = din("masks", [128, 8, 512])
        fox_wq = din("fox_wq", [D, D]); fox_wo = din("fox_wo", [D, D])
        out_d = dout("out", [NTOK, D])
        if debug:
            dbg_cK = dout("dbg_cK", [128, 512]); dbg_Rown = dout("dbg_Rown", [128, 256]); dbg_bias = dout("dbg_bias", [128, 4, 512])
            dbg_OT = dout("dbg_OT", [128, 8, TT], BF16); dbg_QT = dout("dbg_QT", [128, 8, TT], BF16)
            dbg_fl = dout("dbg_fl", [128, 512])
    if fused:
        KTd_c = [nc.dram_tensor("KTd%d" % i, [128, NTOK], BF16) for i in range(8)]
        Vd_c = [nc.dram_tensor("Vd%d" % i, [NTOK, 128], BF16) for i in range(8)]
        KTg_c = [nc.dram_tensor("KTg%d" % i, [256, NTOK], BF16) for i in range(8)]
        Vg_c = [nc.dram_tensor("Vg%d" % i, [2 * NTOK, 128], BF16) for i in range(8)]
        Fd = nc.dram_tensor("Fd", [NTOK, 16], F32)
        Fg = nc.dram_tensor("Fg", [2 * NTOK, 16], F32)
        x1_d = nc.dram_tensor("x1s", [NTOK, D], F32)
    else:
        if doA:
            KTd = dout("KTd", [128, 8 * NTOK], BF16)
            Vd = dout("Vd", [NTOK, D], BF16)
            Fd = dout("Fd", [NTOK, 16], F32)
            x1_d = dout("x1", [NTOK, D])
        if doB:
            KTg = din("KTg", [256, 8 * NTOK], BF16)
            Vg = din("Vg", [2 * NTOK, D], BF16)
            Fg = din("Fg", [2 * NTOK, 16], F32)
            x1_d = din("x1", [NTOK, D])
    x1_b = [Buf(), Buf()]

    def kt_store(hp):
        return KTd_c[hp][:, :] if fused else KTd.ap().rearrange("p (c t) -> p c t", c=8)[:, hp, :]

    def v_store(hp):
        return Vd_c[hp][:, :] if fused else Vd[:, hp * 128:(hp + 1) * 128]

    def kt_load(hp, r):
        return KTg_c[hp][r * 128:(r + 1) * 128, :] if fused else KTg.ap().rearrange("p (c t) -> p c t", c=8)[r * 128:(r + 1) * 128, hp, :]

    def v_load(hp):
        v = Vg_c[hp].ap() if fused else Vg[:, hp * 128:(hp + 1) * 128]
        return v.rearrange("(b p) d -> p b d", p=128)
    kg_b = [Buf() for _ in range(8)]; vg_b = [Buf() for _ in range(8)]; fg_b = Buf()
    kd_b = [Buf() for _ in range(8)]; vd_b = [Buf() for _ in range(8)]; fd_b = Buf()

    with contextlib.ExitStack() as st:
        def sb(name, shape, dt):
            return st.enter_context(nc.sbuf_tensor("s_" + name, shape, dt))

        def psum(name, shape, dt):
            return st.enter_context(nc.psum_tensor("p_" + name, shape, dt))

        x_sb = sb("x_sb", [128, 8, D], F32); x_b = [Buf() for _ in range(8)]
        hT = sb("hT", [128, 8, TT], BF16); hT_b = [Buf() for _ in range(8)]
        W = [sb("w%d" % i, [128, 8, 1024], BF16) for i in range(3)]; W_b = [Buf() for _ in range(3)]
        wrr = [0]
        gbc = sb("gbc", [128, D], F32); gbc_b = Buf()
        cfl = sb("cfl", [128, 8], F32); cfl_b = Buf()
        cst = sb("cst", [128, 4, 128], F32); cst_b = Buf()
        ident = sb("ident", [128, 128], BF16); ones_bf = sb("ones_bf", [128, 128], BF16); cbf_b = Buf()
        meanw = sb("meanw", [128, 128], BF16)
        ssq = sb("ssq", [128, 16], F32); ssq_b = [Buf() for _ in range(16)]; srr = [0]
        xn = [sb("xn%d" % i, [128, D], BF16) for i in range(2)]; xn_b = [Buf(), Buf()]; xrr = [0]
        psF = [psum("psF%d" % i, [128, 512], F32) for i in range(6)]; psF_b = [Buf() for _ in range(6)]; prr = [0]
        psB = [psum("psB%d" % i, [128, 1024], BF16) for i in range(2)]; psB_b = [Buf(), Buf()]; pbr = [0]
        actA = sb("actA", [128, 8, 1056], BF16); actA_b = Buf()
        actB = [sb("actB%d" % i, [128, 8, 512], BF16) for i in range(2)]; actB_b = [Buf(), Buf()]; abr = [0]
        f32s = [sb("f32s%d" % i, [128, 512], F32) for i in range(3)]; f32s_b = [Buf() for _ in range(3)]; frr = [0]
        mpt = [sb("mpt%d" % i, [128, 512], BF16) for i in range(4)]; mpt_b = [Buf() for _ in range(4)]; mpr = [0]
        KmT = sb("KmT", [128, 8, 256], BF16); KmT_b = Buf()
        Vm = sb("Vm", [128, 2, D], BF16); Vm_b = Buf()
        memT = actB[0][:, :, 0:256]; memT_b = actB_b[0]
        memt = sb("memt", [128, D], F32); memt_b = Buf()
        big = [sb("big%d" % i, [128, 4096], BF16) for i in range(2)]; big_b = [Buf(), Buf()]
        fs = sb("fs", [128, 5, 512], F32); fs_b = [Buf() for _ in range(5)]
        Rown = sb("Rown", [128, 256], F32); Rown_b = Buf()
        out_b = Buf()

        def next_ps():
            i = prr[0]; prr[0] = (i + 1) % 4
            return psF[i], psF_b[i]

        arr = [0]

        def acc_ps():
            i = 4 + arr[0]; arr[0] ^= 1
            return psF[i], psF_b[i]

        def next_pt():
            i = mpr[0]; mpr[0] = (i + 1) % 4
            return mpt[i], mpt_b[i]

        def next_f32():
            i = frr[0]; frr[0] = (i + 1) % 3
            return f32s[i], f32s_b[i]

        def next_actB():
            i = abr[0]; abr[0] = (i + 1) % 2
            return actB[i], actB_b[i]

        def dma(eng, out, in_, reads=(), writes=()):
            P.dma(eng, lambda e: e.dma_start(out=out, in_=in_), reads=reads, writes=writes)

        def load_w(src):
            i = wrr[0]; wrr[0] = (i + 1) % 3
            n = src.shape[1]
            for c in range(8):
                dma("pool", W[i][:, c, 0:n], src[c * 128:(c + 1) * 128, :], writes=[W_b[i]])
            return W[i], W_b[i]

        def load_gain(src_row):
            dma("sp", gbc[:], src_row.broadcast_to([128, D]), writes=[gbc_b])
            return gbc, gbc_b

        def mm_group(ps_ap, pairs, reads, ps_b):
            n = len(pairs)

            def fn(e):
                ins = None
                for i, (l, r) in enumerate(pairs):
                    ins = e.matmul(ps_ap, lhsT=l, rhs=r, start=(i == 0), stop=(i == n - 1))
                return ins
            P.op("pe", fn, reads=reads, writes=[ps_b])

        def act(out, in_, func, reads, writes, bias=None, scale=None, accum_out=None):
            kw = {}
            if bias is not None:
                kw["bias"] = bias
            if scale is not None:
                kw["scale"] = scale
            if accum_out is not None:
                kw["accum_out"] = accum_out
            P.op("act", lambda e: e.activation(out=out, in_=in_, func=func, **kw), reads=reads, writes=writes)

        def tt_op(eng, out, in0, in1, op, reads, writes):
            P.op(eng, lambda e: e.tensor_tensor(out=out, in0=in0, in1=in1, op=op), reads=reads, writes=writes)

        def ts_op(eng, out, in0, s1, op0, reads, writes):
            P.op(eng, lambda e: e.tensor_scalar(out=out, in0=in0, scalar1=s1, scalar2=None, op0=op0),
                 reads=reads, writes=writes)

        def stt_op(eng, out, in0, scalar, in1, op0, op1, reads, writes):
            P.op(eng, lambda e: e.scalar_tensor_tensor(out=out, in0=in0, scalar=scalar, in1=in1, op0=op0, op1=op1),
                 reads=reads, writes=writes)

        def recip(out, in_, reads, writes):
            P.op("dve", lambda e: e.reciprocal(out=out, in_=in_), reads=reads, writes=writes)

        def copy_op(eng, out, in_, reads, writes):
            if eng == "act":
                act(out, in_, AF.Copy, reads, writes)
            else:
                P.op(eng, lambda e: e.tensor_copy(out=out, in_=in_), reads=reads, writes=writes)

        evr = [0]

        def evac_eng():
            evr[0] ^= 1
            return "act" if evr[0] else "dve"

        def rms_rstd(src_ap, src_b, npart, xi):
            i = srr[0]; srr[0] = (i + 1) % 16
            s = ssq[0:npart, i:i + 1]
            P.op("dve", lambda e: e.memset(s, 0.0), writes=[ssq_b[i]])
            act(xn[xi][0:npart, :], src_ap, AF.Square, [src_b, ssq_b[i]], [xn_b[xi], ssq_b[i]], accum_out=s)
            act(s, s, AF.Sqrt, [ssq_b[i], cfl_b], [ssq_b[i]], bias=cfl[0:npart, 6:7], scale=1.0 / D)
            recip(s, s, [ssq_b[i]], [ssq_b[i]])
            return s, ssq_b[i]

        def norm_tile(src_ap, src_b, g, g_b, dstT, dst_b, col0, npart=128):
            i = xrr[0]; xrr[0] ^= 1
            s, s_b = rms_rstd(src_ap, src_b, npart, i)
            stt_op("dve", xn[i][0:npart, :], src_ap, s, g[0:npart, :], ALU.mult, ALU.mult,
                   [src_b, s_b, g_b], [xn_b[i]])
            j = pbr[0]; pbr[0] ^= 1
            pv = psB[j][:].rearrange("p (c t) -> p c t", c=8)

            def fn(e):
                ins = None
                for c in range(8):
                    ins = e.transpose(out=pv[:, c, 0:npart], in_=xn[i][0:npart, c * 128:(c + 1) * 128],
                                      identity=ident[0:npart, 0:npart])
                return ins
            P.op("pe", fn, reads=[xn_b[i], cbf_b], writes=[psB_b[j]])
            copy_op(evac_eng(), dstT[:, :, col0:col0 + npart], pv[:, :, 0:npart], [psB_b[j]], [dst_b])

        def norm_x(gidx):
            g, g_b = load_gain(gains_d[gidx:gidx + 1, :])
            for t in range(8):
                norm_tile(x_sb[:, t, :], x_b[t], g, g_b, hT, hT_b[t], t * 128)
            return g, g_b

        def proj_fm(Wt, W_bf, wcol0, nchunk, src, src_bs, tok0, ntok, epi):
            for c in range(nchunk):
                ps, ps_b = next_ps()
                pairs = [(Wt[:, k, wcol0 + c * 128: wcol0 + (c + 1) * 128], src[:, k, tok0:tok0 + ntok]) for k in range(8)]
                mm_group(ps[:, 0:ntok], pairs, [W_bf] + list(src_bs), ps_b)
                epi(c, ps, ps_b)

        def proj_tm_resid(actT, act_bs, Wt, W_bf, sub0, nsub, extra=None):
            for s in range(nsub):
                xt = sub0 + s
                for dh in range(2):
                    ps, ps_b = next_ps()
                    pairs = [(actT[:, k, s * 128:(s + 1) * 128], Wt[:, k, dh * 512:(dh + 1) * 512]) for k in range(8)]
                    rd = [W_bf] + list(act_bs)
                    if extra is not None:
                        pairs.append((extra[0], extra[1][0:1, dh * 512:(dh + 1) * 512]))
                        rd.append(extra[2])
                    mm_group(ps[:, :], pairs, rd, ps_b)
                    tt_op("dve", x_sb[:, xt, dh * 512:(dh + 1) * 512], ps[:, :], x_sb[:, xt, dh * 512:(dh + 1) * 512],
                          ALU.add, [ps_b, x_b[xt]], [x_b[xt]])

        dma("sp", cfl[:], cfl_d[:, :], writes=[cfl_b])
        dma("sp", cst[:], consts_d[:, :, :], writes=[cst_b])
        copy_op("dve", ident[:], cst[:, 0, :], [cst_b], [cbf_b])
        copy_op("dve", ones_bf[:], cst[:, 3, :], [cst_b], [cbf_b])
        ts_op("dve", meanw[:], cst[:, 3, :], 1.0 / D, ALU.mult, [cst_b], [cbf_b])

        def mem_kv(l):
            g, g_b = load_gain(gsrc_d[l:l + 1, :])
            for i in range(2):
                dma("sp", memt[:], mem_d[i * 128:(i + 1) * 128, :], writes=[memt_b])
                norm_tile(memt[:], memt_b, g, g_b, memT, memT_b, i * 128)
            Wk, Wk_b = load_w(mem_wk[l])

            def epi_k(c, ps, ps_b):
                copy_op(evac_eng(), KmT[:, c, :], ps[:, 0:256], [ps_b], [KmT_b])
            proj_fm(Wk, Wk_b, 0, 8, memT, [memT_b], 0, 256, epi_k)
            Wv, Wv_b = load_w(mem_wv[l])
            for mb in range(2):
                for dh in range(2):
                    ps, ps_b = next_ps()
                    pairs = [(memT[:, k, mb * 128:(mb + 1) * 128], Wv[:, k, dh * 512:(dh + 1) * 512]) for k in range(8)]
                    mm_group(ps[:, :], pairs, [Wv_b, memT_b], ps_b)
                    copy_op(evac_eng(), Vm[:, mb, dh * 512:(dh + 1) * 512], ps[:, :], [ps_b], [Vm_b])

        def mem_attn(l):
            Wq, Wq_b = load_w(mem_wq[l])
            Wo, Wo_b = load_w(mem_wo[l])
            norm_x(1 if l == 0 else 5)
            for tt in range(2):
                QT, QT_b = next_actB()

                def epi_q(c, ps, ps_b, QT=QT, QT_b=QT_b):
                    eng = evac_eng()
                    if eng == "act":
                        P.op("act", lambda e, c=c: e.mul(out=QT[:, c, :], in_=ps[:, :], mul=1.0 / 16), reads=[ps_b], writes=[QT_b])
                    else:
                        ts_op("dve", QT[:, c, :], ps[:, :], 1.0 / 16, ALU.mult, [ps_b], [QT_b])
                proj_fm(Wq, Wq_b, 0, 8, hT, hT_b[tt * 4:(tt + 1) * 4], tt * 512, 512, epi_q)
                OT, OT_b = next_actB()
                for hd in range(4):
                    pts = []
                    for mb in range(2):
                        ps, ps_b = next_ps()
                        pairs = [(KmT[:, 2 * hd + cc, mb * 128:(mb + 1) * 128], QT[:, 2 * hd + cc, :]) for cc in range(2)]
                        mm_group(ps[:, :], pairs, [KmT_b, QT_b], ps_b)
                        pt, pt_b = next_pt()
                        act(pt[:, :], ps[:, :], AF.Exp, [ps_b], [pt_b])
                        pts.append((pt, pt_b))
                    psl, psl_b = acc_ps()
                    mm_group(psl[:, :], [(ones_bf[:, :], pts[mb][0][:, :]) for mb in range(2)],
                             [cbf_b, pts[0][1], pts[1][1]], psl_b)
                    rl, rl_b = next_f32()
                    recip(rl[:], psl[:, :], [psl_b], [rl_b])
                    for dvc in range(2):
                        pso, pso_b = next_ps()
                        col = hd * 256 + dvc * 128
                        mm_group(pso[:, :], [(Vm[:, mb, col:col + 128], pts[mb][0][:, :]) for mb in range(2)],
                                 [Vm_b, pts[0][1], pts[1][1]], pso_b)
                        tt_op("dve", OT[:, 2 * hd + dvc, :], pso[:, :], rl[:], ALU.mult, [pso_b, rl_b], [OT_b])
                proj_tm_resid(OT, [OT_b], Wo, Wo_b, tt * 4, 4)

        def mlp(l):
            def load_slice(sl):
                i = wrr[0]; wrr[0] = (i + 1) % 3
                flat = W[i][:].rearrange("p c n -> p (c n)")
                w1 = flat[:, 0:4096].rearrange("p (c n) -> p c n", c=8)
                w2 = flat[:, 4096:8192].rearrange("p (c n) -> p c n", c=4)
                for c in range(8):
                    dma("pool", w1[:, c, :], ff_w1[l][c * 128:(c + 1) * 128, sl * 512:(sl + 1) * 512], writes=[W_b[i]])
                for c in range(4):
                    dma("pool", w2[:, c, :], ff_w2[l][sl * 512 + c * 128: sl * 512 + (c + 1) * 128, :], writes=[W_b[i]])
                return w1, w2, W_b[i]
            pend = [load_slice(0), load_slice(1)]
            norm_x(2 if l == 0 else 6)
            for sl in range(8):
                if sl + 2 < 8:
                    pend.append(load_slice(sl + 2))
                w1, w2, wb = pend[sl]
                fT, fT_b = next_actB()
                fTv = fT[:].rearrange("p c t -> p (c t)").rearrange("p (c t) -> p c t", c=4)
                for tt in range(2):
                    for c in range(4):
                        ps, ps_b = next_ps()
                        mm_group(ps[:, :], [(w1[:, k, c * 128:(c + 1) * 128], hT[:, k, tt * 512:(tt + 1) * 512]) for k in range(8)],
                                 [wb] + hT_b[tt * 4:(tt + 1) * 4], ps_b)
                        r, r_b = next_f32()
                        act(r[:], ps[:, :], AF.Relu, [ps_b], [r_b])
                        tt_op("dve", fTv[:, c, tt * 512:(tt + 1) * 512], r[:], r[:], ALU.mult, [r_b], [fT_b])
                for s_ in range(8):
                    for dh in range(2):
                        ps, ps_b = next_ps()
                        mm_group(ps[:, :], [(fTv[:, k, s_ * 128:(s_ + 1) * 128], w2[:, k, dh * 512:(dh + 1) * 512]) for k in range(4)],
                                 [wb, fT_b], ps_b)
                        tt_op("dve", x_sb[:, s_, dh * 512:(dh + 1) * 512], ps[:, :], x_sb[:, s_, dh * 512:(dh + 1) * 512],
                              ALU.add, [ps_b, x_b[s_]], [x_b[s_]])

        if doA:
            b1col = sb("b1col", [128, 16], F32); dwcol = Rown[:, 0:248].rearrange("p (c k) -> p c k", k=31)
            bdwcol = sb("bdwcol", [128, 8], F32); lngcol = sb("lngcol", [128, 8], F32); lnbcol = sb("lnbcol", [128, 8], F32)
            b2row = sb("b2row", [1, D], BF16); small_b = Buf()
            dma("sp", b1col[:], b1col_d[:, :], writes=[small_b])
            dma("sp", dwcol, dwcol_d[:, :, :], writes=[small_b])
            dma("sp", bdwcol[:], bdwcol_d[:, :], writes=[small_b])
            dma("sp", lngcol[:], lngcol_d[:, :], writes=[small_b])
            dma("sp", lnbcol[:], lnbcol_d[:, :], writes=[small_b])
            dma("pool", b2row[:], b2row_d[:, :], writes=[small_b])
            hTh = sb("hTh", [128, 8, 32], BF16); hTh_b = Buf()
            dg = big[0][:, 0:31 * 128].rearrange("p (k j) -> p k j", j=128); dg_b = big_b[0]
            vbuf = big[1][:, :].rearrange("p (c t) -> p c t", t=512); vbuf_b = big_b[1]
            gluT = actA
            xv = x_d.ap().rearrange("(t p) d -> p t d", p=128)
            x1v = x1_d.ap().rearrange("(t p) d -> p t d", p=128)
            mem_kv(0)
            for tile0 in range(2):
                for t in range(8):
                    dma("sp", x_sb[:, t, :], xv[:, tile0 * 8 + t, :], writes=[x_b[t]])
                Wa, Wa_b = load_w(pw1_d[:, 0:1024])
                Wg, Wg_b = load_w(pw1_d[:, 1024:2048])
                g, g_b = norm_x(0)
                dma("sp", memt[0:32, :], xhalo_d[tile0], writes=[memt_b])
                norm_tile(memt[0:32, :], memt_b, g, g_b, hTh, hTh_b, 0, npart=32)
                for c in range(8):
                    for (src, src_bs, tok0, ntok, col0) in ((hTh, [hTh_b], 0, 32, 0), (hT, hT_b[0:4], 0, 512, 32),
                                                            (hT, hT_b[4:8], 512, 512, 544)):
                        psa, psa_b = next_ps()
                        mm_group(psa[:, 0:ntok], [(Wa[:, k, c * 128:(c + 1) * 128], src[:, k, tok0:tok0 + ntok]) for k in range(8)],
                                 [Wa_b] + list(src_bs), psa_b)
                        psg, psg_b = next_ps()
                        mm_group(psg[:, 0:ntok], [(Wg[:, k, c * 128:(c + 1) * 128], src[:, k, tok0:tok0 + ntok]) for k in range(8)],
                                 [Wg_b] + list(src_bs), psg_b)
                        sg, sg_b = next_f32()
                        act(sg[:, 0:ntok], psg[:, 0:ntok], AF.Sigmoid, [psg_b, small_b], [sg_b], bias=b1col[:, 8 + c:9 + c])
                        stt_op("dve", gluT[:, c, col0:col0 + ntok], psa[:, 0:ntok], b1col[:, c:c + 1], sg[:, 0:ntok],
                               ALU.add, ALU.mult, [psa_b, sg_b, small_b], [actA_b])
                ts_op("dve", gluT[:, :, 0:32], gluT[:, :, 0:32], cfl[:, tile0:tile0 + 1], ALU.mult,
                      [actA_b, cfl_b], [actA_b])
                W2c, W2c_b = load_w(pw2_d[:, :])
                for tt in range(2):
                    psM, psM_b = acc_ps()
                    psE, psE_b = acc_ps()
                    for c in range(8):
                        for k in range(31):
                            ts_op("pool", dg[:, k, :], ident[:, :], dwcol[:, c, k:k + 1], ALU.mult,
                                  [cbf_b, small_b], [dg_b])
                        ps, ps_b = next_ps()
                        base = tt * 512 + 2
                        mm_group(ps[:, :], [(dg[:, k, :], gluT[:, c, base + k: base + k + 512]) for k in range(31)],
                                 [dg_b, actA_b], ps_b)
                        act(vbuf[:, c, :], ps[:, :], AF.Identity, [ps_b, small_b], [vbuf_b], bias=bdwcol[:, c:c + 1])
                        v2, v2_b = next_pt()
                        tt_op("dve", v2[:, :], vbuf[:, c, :], vbuf[:, c, :], ALU.mult, [vbuf_b], [v2_b])

                        def fn_m(e, c=c, psM=psM):
                            return e.matmul(psM[:, :], lhsT=meanw[:, :], rhs=vbuf[:, c, :], start=(c == 0), stop=(c == 7))
                        P.op("pe", fn_m, reads=[vbuf_b, cbf_b], writes=[psM_b])

                        def fn_e(e, c=c, psE=psE, v2=v2):
                            return e.matmul(psE[:, :], lhsT=meanw[:, :], rhs=v2[:, :], start=(c == 0), stop=(c == 7))
                        P.op("pe", fn_e, reads=[v2_b, cbf_b], writes=[psE_b])
                    mean, msq, rstd = fs[:, 0, :], fs[:, 1, :], fs[:, 2, :]
                    copy_op("act", mean, psM[:, :], [psM_b], [fs_b[0]])
                    tt_op("dve", msq, mean, mean, ALU.mult, [fs_b[0]], [fs_b[1]])
                    tt_op("dve", rstd, psE[:, :], msq, ALU.subtract, [psE_b, fs_b[1]], [fs_b[2]])
                    act(rstd, rstd, AF.Sqrt, [fs_b[2], cfl_b], [fs_b[2]], bias=cfl[:, 7:8])
                    recip(rstd, rstd, [fs_b[2]], [fs_b[2]])
                    zT, zT_b = next_actB()
                    for c in range(8):
                        d, d_b = next_f32()
                        tt_op("dve", d[:], vbuf[:, c, :], mean, ALU.subtract, [vbuf_b, fs_b[0]], [d_b])
                        tt_op("dve", d[:], d[:], rstd, ALU.mult, [d_b, fs_b[2]], [d_b])
                        act(zT[:, c, :], d[:], AF.Silu, [d_b, small_b], [zT_b], bias=lnbcol[:, c:c + 1], scale=lngcol[:, c:c + 1])
                    proj_tm_resid(zT, [zT_b], W2c, W2c_b, tt * 4, 4, extra=(ones_bf[0:1, :], b2row, small_b))
                mem_attn(0)
                mlp(0)
                for t in range(8):
                    dma("sp", x1v[:, tile0 * 8 + t, :], x_sb[:, t, :], reads=[x_b[t]], writes=[out_b, x1_b[tile0]])
                Wkk, Wkk_b = load_w(kvf_d[:, 0:1024])
                Wvv, Wvv_b = load_w(kvf_d[:, 1024:2048])
                norm_x(3)
                kst = actA; kst_b = actA_b
                for tt in range(2):
                    def epi_kk(c, ps, ps_b, tt=tt):
                        copy_op(evac_eng(), kst[:, c, tt * 512:(tt + 1) * 512], ps[:, :], [ps_b], [kst_b])
                    proj_fm(Wkk, Wkk_b, 0, 8, hT, hT_b[tt * 4:(tt + 1) * 4], tt * 512, 512, epi_kk)
                for c in range(8):
                    dma("sp", kt_store(c)[:, tile0 * TT:(tile0 + 1) * TT], kst[:, c, 0:TT], reads=[kst_b], writes=[out_b, kd_b[c]])
                Wff, Wff_b = load_w(kvf_d[:, 2048:2064])
                fst, fst_b = next_f32()
                for s in range(8):
                    vst, vst_b = next_actB()
                    vflat = vst[:].rearrange("p c t -> p (c t)")
                    for dh in range(2):
                        ps, ps_b = next_ps()
                        mm_group(ps[:, :], [(hT[:, k, s * 128:(s + 1) * 128], Wvv[:, k, dh * 512:(dh + 1) * 512]) for k in range(8)],
                                 [Wvv_b, hT_b[s]], ps_b)
                        copy_op(evac_eng(), vflat[:, dh * 512:(dh + 1) * 512], ps[:, :], [ps_b], [vst_b])
                    r0 = tile0 * TT + s * 128
                    if fused:
                        for c in range(8):
                            dma("sp", v_store(c)[r0:r0 + 128, :], vflat[:, c * 128:(c + 1) * 128], reads=[vst_b], writes=[out_b, vd_b[c]])
                    else:
                        dma("sp", Vd[r0:r0 + 128, :], vflat[:, 0:D], reads=[vst_b], writes=[out_b])
                    ps, ps_b = next_ps()
                    mm_group(ps[:, 0:16], [(hT[:, k, s * 128:(s + 1) * 128], Wff[:, k, 0:16]) for k in range(8)],
                             [Wff_b, hT_b[s]], ps_b)
                    copy_op("dve", fst[:, s * 16:(s + 1) * 16], ps[:, 0:16], [ps_b], [fst_b])
                Fv = Fd.ap().rearrange("(s p) h -> p s h", p=128)
                dma("sp", Fv[:, tile0 * 8:(tile0 + 1) * 8, :], fst[:, 0:128].rearrange("p (s h) -> p s h", h=16),
                    reads=[fst_b], writes=[out_b, fd_b])

        if fused:
            rg = [[0, 1], [2, 3], [4, 5], [6, 7]]

            def gather(src, dst, rb, wb):
                P.op("pool", lambda e: e.collective_compute(
                    "AllGather", ALU.bypass, replica_groups=rg, ins=[src.ap().opt()], outs=[dst.ap().opt()]),
                    reads=[rb], writes=[wb])
            gather(Fd, Fg, fd_b, fg_b)
            for c in range(8):
                gather(KTd_c[c], KTg_c[c], kd_b[c], kg_b[c])
                gather(Vd_c[c], Vg_c[c], vd_b[c], vg_b[c])
            P.barrier()

        if doB:
            x1v = x1_d.ap().rearrange("(t p) d -> p t d", p=128)
            fl, Eb, fgb = f32s[0], f32s[1], f32s[2]
            fl_b, Eb_b, fgb_b = f32s_b[0], f32s_b[1], f32s_b[2]
            cK = fs[:, 4, :]; cK_b = fs_b[4]
            masks = sb("masks", [128, 8, 512], BF16); masks_b = Buf()
            ksb = sb("ksb", [128, 4096], BF16); ksb_b = Buf()
            vsb = [big[i][:, :].rearrange("p (b d) -> p b d", d=128) for i in range(2)]; vsb_b = big_b
            mem_kv(1)
            dma("sp", fgb[:], fgb_d[:, :], writes=[fgb_b])
            for i in range(8):
                dma("pool", masks[:, i, :], masks_d[:, i, :], writes=[masks_b])
            Fgv = Fg.ap().rearrange("(b p) h -> p b h", p=128)
            dma("sp", fl[:].rearrange("p (b h) -> p b h", h=16), Fgv, reads=[fg_b], writes=[fl_b])
            tt_op("dve", fl[:], fl[:], fgb[:], ALU.add, [fl_b, fgb_b], [fl_b])
            act(fl[:], fl[:], AF.Exp, [fl_b], [fl_b], scale=-1.0)
            act(fl[:], fl[:], AF.Ln, [fl_b, cfl_b], [fl_b], bias=cfl[:, 5:6])
            P.op("dve", lambda e: e.memset(Eb[:, 0:16], 0.0), writes=[Eb_b])
            for j in range(1, 32):
                tt_op("dve", Eb[:, j * 16:(j + 1) * 16], Eb[:, (j - 1) * 16:j * 16], fl[:, (j - 1) * 16:j * 16], ALU.add,
                      [Eb_b, fl_b], [Eb_b])
            ps, ps_b = next_ps()
            mm_group(ps[:, :], [(cst[:, 1, :], fl[:]), (cst[:, 3, :], Eb[:])], [cst_b, fl_b, Eb_b], ps_b)
            if debug:
                dma("sp", dbg_fl[:, :], fl[:], reads=[fl_b], writes=[out_b])
            copy_op("dve", cK, ps[:, :], [ps_b], [cK_b])
            ps, ps_b = next_ps()
            mm_group(ps[:, :], [(cst[:, 2, :], cK)], [cst_b, cK_b], ps_b)
            tmpR, tmpR_b = f32s[2], f32s_b[2]
            ts_op("dve", tmpR[:, 0:256], ps[:, 0:256], cfl[:, 2:3], ALU.mult, [ps_b, cfl_b], [tmpR_b])
            stt_op("dve", Rown[:], ps[:, 256:512], cfl[:, 3:4], tmpR[:, 0:256], ALU.mult, ALU.add,
                   [ps_b, cfl_b, tmpR_b], [Rown_b])
            P.op("pool", lambda e: e.memset(vsb[0][:, :, 64:128], 1.0), writes=[vsb_b[0]])
            P.op("pool", lambda e: e.memset(vsb[1][:, :, 0:64], 1.0), writes=[vsb_b[1]])
            QT = actA; QT_b = actA_b
            ov = out_d.ap().rearrange("(t p) d -> p t d", p=128)
            for tile0 in range(2):
                for t in range(8):
                    dma("sp", x_sb[:, t, :], x1v[:, tile0 * 8 + t, :], reads=[x1_b[tile0]], writes=[x_b[t]])
                for jl in range(2):
                    j = tile0 * 2 + jl
                    for kb in range(32):
                        tt_op("dve", fs[:, jl, kb * 16:(kb + 1) * 16], cK[:, kb * 16:(kb + 1) * 16],
                              Rown[:, (4 * j + 2) * 16:(4 * j + 3) * 16], ALU.subtract, [cK_b, Rown_b], [fs_b[jl]])
                    lo, hi = (4 * j + 4) * 16, (16 + 4 * j + 4) * 16
                    ts_op("dve", fs[:, jl, lo:hi], fs[:, jl, lo:hi], cfl[:, 4:5], ALU.add, [fs_b[jl], cfl_b], [fs_b[jl]])
                Wq, Wq_b = load_w(fox_wq[:, :])
                Wo, Wo_b = load_w(fox_wo[:, :])
                norm_x(4)
                for tt in range(2):
                    def epi_q(c, ps, ps_b, tt=tt):
                        eng = evac_eng()
                        if eng == "act":
                            P.op("act", lambda e, c=c, tt=tt: e.mul(out=QT[:, c, tt * 512:(tt + 1) * 512], in_=ps[:, :], mul=0.125),
                                 reads=[ps_b], writes=[QT_b])
                        else:
                            ts_op("dve", QT[:, c, tt * 512:(tt + 1) * 512], ps[:, :], 0.125, ALU.mult, [ps_b], [QT_b])
                    proj_fm(Wq, Wq_b, 0, 8, hT, hT_b[tt * 4:(tt + 1) * 4], tt * 512, 512, epi_q)
                OTb = hT
                for hp in range(8):
                    for r in range(2):
                        dma("sp", ksb[:, r * NTOK:(r + 1) * NTOK], kt_load(hp, r),
                            reads=[kg_b[hp]], writes=[ksb_b])
                    for e2 in range(2):
                        head = 2 * hp + e2
                        vcol = 0 if e2 == 0 else 64
                        for bq in range(4):
                            dma("sp", vsb[e2][:, bq * 8:(bq + 1) * 8, vcol:vcol + 64],
                                v_load(hp)[:, bq * 8:(bq + 1) * 8, e2 * 64:(e2 + 1) * 64],
                                reads=[vg_b[hp]], writes=[vsb_b[e2]])
                    for jl in range(2):
                        j = tile0 * 2 + jl
                        nkb = 16 + 4 * j + 4
                        accs = [acc_ps(), acc_ps()]

                        def s_step(kb, j=j, jl=jl, hp=hp):
                            outs = []
                            for e2 in range(2):
                                pr = slice(e2 * 64, (e2 + 1) * 64)
                                pss, pss_b = next_ps()
                                pairs = [(ksb[pr, kb * 128:(kb + 1) * 128], QT[pr, hp, jl * 512:(jl + 1) * 512])]
                                rd = [ksb_b, QT_b]
                                if 4 * j <= kb < 4 * j + 4:
                                    pairs.append((ident[:, :], masks[:, kb - 4 * j, :])); rd += [cbf_b, masks_b]
                                elif kb >= 16 + 4 * j:
                                    pairs.append((ident[:, :], masks[:, 4 + kb - 16 - 4 * j, :])); rd += [cbf_b, masks_b]
                                mm_group(pss[:, :], pairs, rd, pss_b)
                                outs.append((pss, pss_b))
                            pts = []
                            for e2 in range(2):
                                head = 2 * hp + e2
                                pss, pss_b = outs[e2]
                                pt, pt_b = next_pt()
                                act(pt[:, :], pss[:, :], AF.Exp, [pss_b, fs_b[jl]], [pt_b],
                                    bias=fs[:, jl, kb * 16 + head: kb * 16 + head + 1])
                                pts.append((pt, pt_b))
                            return pts

                        def pv_step(kb, pts, nkb=nkb, accs=accs):
                            for e2 in range(2):
                                pso, pso_b = accs[e2]
                                pt, pt_b = pts[e2]

                                def fn_pv(e, kb=kb, pso=pso, e2=e2, pt=pt):
                                    return e.matmul(pso[:, :], lhsT=vsb[e2][:, kb, :], rhs=pt[:, :],
                                                    start=(kb == 0), stop=(kb == nkb - 1))
                                P.op("pe", fn_pv, reads=[vsb_b[e2], pt_b], writes=[pso_b])
                        prev = s_step(0)
                        for kb in range(1, nkb):
                            cur = s_step(kb)
                            pv_step(kb - 1, prev)
                            prev = cur
                        pv_step(nkb - 1, prev)
                        for e2 in range(2):
                            pr = slice(e2 * 64, (e2 + 1) * 64)
                            lr = slice((1 - e2) * 64, (2 - e2) * 64)
                            pso, pso_b = accs[e2]
                            rl, rl_b = next_f32()
                            recip(rl[lr, :], pso[lr, :], [pso_b], [rl_b])
                            tt_op("dve", OTb[pr, hp, jl * 512:(jl + 1) * 512], pso[pr, :], rl[lr, :], ALU.mult,
                                  [pso_b, rl_b], hT_b[jl * 4:(jl + 1) * 4])
                if debug and tile0 == 0:
                    dma("sp", dbg_cK[:, :], cK, reads=[cK_b], writes=[out_b])
                    dma("sp", dbg_Rown[:, :], Rown[:], reads=[Rown_b], writes=[out_b])
                    dma("sp", dbg_bias[:, :, :], fs[:, 0:4, :], reads=fs_b[0:4], writes=[out_b])
                    dma("sp", dbg_OT[:, :, :], OTb[:, :, :], reads=hT_b, writes=[out_b])
                    dma("sp", dbg_QT[:, :, :], QT[:, :, 0:TT], reads=[QT_b], writes=[out_b])
                for tt in range(2):
                    proj_tm_resid(OTb[:, :, tt * 512:(tt + 1) * 512], hT_b[tt * 4:(tt + 1) * 4], Wo, Wo_b, tt * 4, 4)
                if debug:
                    for t in range(8):
                        dma("sp", ov[:, tile0 * 8 + t, :], x_sb[:, t, :], reads=[x_b[t]], writes=[out_b])
                    break
                mem_attn(1)
                mlp(1)
                g, g_b = load_gain(gains_d[7:8, :])
                for t in range(8):
                    xi = xrr[0]; xrr[0] ^= 1
                    s, s_b = rms_rstd(x_sb[:, t, :], x_b[t], 128, xi)
                    stt_op("dve", x_sb[:, t, :], x_sb[:, t, :], s, g[:, :], ALU.mult, ALU.mult,
                           [x_b[t], s_b, g_b], [x_b[t]])
                    dma("sp", ov[:, tile0 * 8 + t, :], x_sb[:, t, :], reads=[x_b[t]], writes=[out_b])
        P.wait_all("sp", [out_b])
        P.emit()
    return nc


def _consts():
    c = np.zeros((128, 4, 128), np.float32)
    c[:, 0, :] = np.eye(128, dtype=np.float32)
    c[:, 1, :] = np.triu(np.ones((128, 128), np.float32))
    c[0, 2, :] = 1.0
    c[:, 3, :] = 1.0
    return c


def _masks(h):
    p = np.arange(128)[:, None]
    q = np.arange(512)[None, :]
    m = np.zeros((128, 8, 512), np.float32)
    for o in range(4):
        tri = np.where(o * 128 + p <= q, 0.0, NEG).astype(np.float32)
        m[:, o, :] = tri if h == 0 else 0.0
        m[:, 4 + o, :] = tri
    return m


def _common_inputs(inp, c):
    b, h = c // 2, c % 2
    gains = np.stack([inp["norm_mix_g"][0], inp["norm_mem_g"][0], inp["norm_ff_g"][0], inp["kv_norm_g"],
                      inp["norm_mix_g"][1], inp["norm_mem_g"][1], inp["norm_ff_g"][1], inp["final_norm_g"]]).astype(np.float32)
    cfl = np.zeros((128, 8), np.float32)
    cfl[:, 0] = float(h)
    cfl[:, 1] = 1.0
    cfl[:, 2] = 1.0 - h
    cfl[:, 3] = float(h)
    cfl[:, 4] = NEG * (1 - h)
    cfl[:, 5] = 1.0
    cfl[:, 6] = 1e-6
    cfl[:, 7] = 1e-5
    return {
        "cflags": cfl, "consts": _consts(), "gains": gains,
        "mem": np.ascontiguousarray(inp["mem"][b]), "gsrc": np.ascontiguousarray(inp["norm_memsrc_g"]),
        "mem_wq": inp["mem_wq"], "mem_wk": inp["mem_wk"], "mem_wv": inp["mem_wv"], "mem_wo": inp["mem_wo"],
        "ff_w1": inp["ff_w1"], "ff_w2": inp["ff_w2"],
    }


def _a_inputs(inp, c):
    b, h = c // 2, c % 2
    x = inp["x"]
    xh = np.zeros((2, 32, D), np.float32)
    if h == 1:
        xh[0] = x[b, NTOK - 32:NTOK]
    xh[1] = x[b, h * NTOK + TT - 32: h * NTOK + TT]
    col = lambda v: np.ascontiguousarray(v.reshape(-1, 128).T)
    return {
        "x": np.ascontiguousarray(x[b, h * NTOK:(h + 1) * NTOK]), "xhalo": xh,
        "pw1": inp["conv_pw1_w"][0], "pw2": inp["conv_pw2_w"][0],
        "b1col": col(inp["conv_pw1_b"][0]),
        "dwcol": np.ascontiguousarray(inp["conv_dw_w"][0].reshape(31, 8, 128).transpose(2, 1, 0)),
        "bdwcol": col(inp["conv_dw_b"][0]), "lngcol": col(inp["conv_ln_g"][0]), "lnbcol": col(inp["conv_ln_b"][0]),
        "b2row": np.ascontiguousarray(inp["conv_pw2_b"][0][None, :]),
        "kvf": inp["kvf_w"],
    }


def _b_inputs(inp, c):
    h = c % 2
    return {
        "fgbrep": np.ascontiguousarray(np.broadcast_to(np.tile(inp["fgate_b"], 32)[None, :], (128, 512))).astype(np.float32),
        "masks": _masks(h),
        "fox_wq": inp["fox_wq"][0], "fox_wo": inp["fox_wo"][0],
    }


_NC_CACHE = {}


def _get_nc(mode):
    if mode not in _NC_CACHE:
        _NC_CACHE[mode] = build(mode)
    return _NC_CACHE[mode]


def kernel(**inp):
    inp = {k: np.asarray(v) for k, v in inp.items()}
    cores = list(range(8))
    if FUSED:
        maps = []
        for c in cores:
            m = _common_inputs(inp, c); m.update(_a_inputs(inp, c)); m.update(_b_inputs(inp, c))
            maps.append(m)
        res = run_bass_kernel_spmd(_get_nc("AB"), maps, core_ids=cores)
        outs = [res.results[c]["out"] for c in cores]
    else:
        maps = []
        for c in cores:
            m = _common_inputs(inp, c); m.update(_a_inputs(inp, c))
            maps.append(m)
        ra = run_bass_kernel_spmd(_get_nc("A"), maps, core_ids=cores).results
        maps = []
        for c in cores:
            p = c - c % 2
            m = _common_inputs(inp, c); m.update(_b_inputs(inp, c))
            m["x1"] = ra[c]["x1"]
            m["KTg"] = np.concatenate([ra[p]["KTd"], ra[p + 1]["KTd"]], axis=0)
            m["Vg"] = np.concatenate([ra[p]["Vd"], ra[p + 1]["Vd"]], axis=0)
            m["Fg"] = np.concatenate([ra[p]["Fd"], ra[p + 1]["Fd"]], axis=0)
            maps.append(m)
        rb = run_bass_kernel_spmd(_get_nc("B"), maps, core_ids=cores).results
        outs = [rb[c]["out"] for c in cores]
    out = np.zeros((4, 4096, D), np.float32)
    for c in cores:
        out[c // 2, (c % 2) * NTOK:(c % 2 + 1) * NTOK] = outs[c]
    return out
```

```python
import contextlib
import numpy as np
import ml_dtypes
import concourse.bass as bass
import concourse.mybir as mybir
from concourse.bass_utils import run_bass_kernel_spmd

F32 = mybir.dt.float32
BF16 = mybir.dt.bfloat16
AF = mybir.ActivationFunctionType
ALU = mybir.AluOpType

ENGS = ["pe", "act", "dve", "pool", "sp"]
D = 1024
NTOK = 2048
TT = 1024
NEG = -30000.0
FUSED = True


class Buf:
    __slots__ = ("w", "r")

    def __init__(self):
        self.w = {}
        self.r = {}


class Prog:
    def __init__(self, nc, ndma=48):
        self.nc = nc
        self.ops = {e: [] for e in ENGS}
        self.cnt = {e: 0 for e in ENGS}
        self.seen = {e: {} for e in ENGS}
        self.ndma = ndma
        self.dma_cnt = [0] * ndma
        self.dma_rr = 0
        self.dma_rr_pool = 0

    def _deps(self, eng, reads, writes):
        waits = {}
        seen = self.seen[eng]

        def need(k, v):
            if seen.get(k, 0) >= v:
                return
            if waits.get(k, 0) < v:
                waits[k] = v
        for b in reads:
            for k, v in b.w.items():
                need(k, v)
        for b in writes:
            for k, v in b.w.items():
                need(k, v)
            for k, v in b.r.items():
                need(k, v)
        for k, v in waits.items():
            seen[k] = v
        return list(waits.items())

    def _mark(self, stamp, reads, writes):
        k, v = stamp
        for b in reads:
            if b.r.get(k, 0) < v:
                b.r[k] = v
        for b in writes:
            if b.w.get(k, 0) < v:
                b.w[k] = v

    def op(self, eng, fn, reads=(), writes=()):
        waits = self._deps(eng, reads, writes)
        self.cnt[eng] += 1
        stamp = (eng, self.cnt[eng])
        self.ops[eng].append((fn, waits, stamp, 1))
        self._mark(stamp, reads, writes)

    def dma(self, eng, fn, reads=(), writes=()):
        waits = self._deps(eng, reads, writes)
        half = self.ndma // 2
        if eng == "pool":
            j = half + self.dma_rr_pool
            self.dma_rr_pool = (self.dma_rr_pool + 1) % (self.ndma - half)
        else:
            j = self.dma_rr
            self.dma_rr = (self.dma_rr + 1) % half
        prev = self.dma_cnt[j]
        if prev > 0 and self.seen[eng].get(("dma", j), 0) < prev:
            waits.append((("dma", j), prev))
            self.seen[eng][("dma", j)] = prev
        self.dma_cnt[j] += 16
        stamp = (("dma", j), self.dma_cnt[j])
        self.ops[eng].append((fn, waits, stamp, 16))
        self._mark(stamp, reads, writes)

    def barrier(self):
        tot = {e: self.cnt[e] for e in ENGS if self.cnt[e] > 0}
        for j in range(self.ndma):
            if self.dma_cnt[j] > 0:
                tot[("dma", j)] = self.dma_cnt[j]
        for e in ENGS:
            waits = [(k, v) for k, v in tot.items() if self.seen[e].get(k, 0) < v]
            for k, v in waits:
                self.seen[e][k] = v
            self.ops[e].append((None, waits, None, 0))

    def wait_all(self, eng, bufs):
        waits = self._deps(eng, bufs, ())
        self.ops[eng].append((None, waits, None, 0))

    def emit(self):
        nc = self.nc
        with contextlib.ExitStack() as st:
            sems = {}
            for e in ENGS:
                sems[e] = st.enter_context(nc.semaphore("s_" + e))
            for j in range(self.ndma):
                sems[("dma", j)] = st.enter_context(nc.semaphore("s_dma%d" % j))
            block = st.enter_context(nc.Block())

            def run(eng_name):
                def body(eng):
                    for fn, waits, stamp, incv in self.ops[eng_name]:
                        for k, v in waits:
                            eng.wait_ge(sems[k], v)
                        if fn is None:
                            continue
                        ins = fn(eng)
                        ins.then_inc(sems[stamp[0]], incv)
                return body

            block.tensor(run("pe"))
            block.scalar(run("act"))
            block.vector(run("dve"))
            block.gpsimd(run("pool"))
            block.sync(run("sp"))


def build(mode, debug=False):
    nc = bass.Bass("TRN2", target_bir_lowering=False)
    P = Prog(nc)
    doA = "A" in mode
    doB = "B" in mode
    fused = mode == "AB"

    def din(name, shape, dt=F32):
        return nc.dram_tensor(name, shape, dt, kind="ExternalInput")

    def dout(name, shape, dt=F32):
        return nc.dram_tensor(name, shape, dt, kind="ExternalOutput")

    cfl_d = din("cflags", [128, 8])
    consts_d = din("consts", [128, 4, 128])
    gains_d = din("gains", [8, D])
    mem_d = din("mem", [256, D])
    gsrc_d = din("gsrc", [2, D])
    mem_wq = din("mem_wq", [2, D, D]); mem_wk = din("mem_wk", [2, D, D])
    mem_wv = din("mem_wv", [2, D, D]); mem_wo = din("mem_wo", [2, D, D])
    ff_w1 = din("ff_w1", [2, D, 4 * D]); ff_w2 = din("ff_w2", [2, 4 * D, D])
    if doA:
        x_d = din("x", [NTOK, D])
        xhalo_d = din("xhalo", [2, 32, D])
        pw1_d = din("pw1", [D, 2 * D]); pw2_d = din("pw2", [D, D])
        b1col_d = din("b1col", [128, 16]); dwcol_d = din("dwcol", [128, 8, 31])
        bdwcol_d = din("bdwcol", [128, 8]); lngcol_d = din("lngcol", [128, 8]); lnbcol_d = din("lnbcol", [128, 8])
        b2row_d = din("b2row", [1, D])
        kvf_d = din("kvf", [D, 2 * D + 16])
    if doB:
        fgb_d = din("fgbrep", [128, 512])
        masks_d = din("masks", [128, 8, 512])
        fox_wq = din("fox_wq", [D, D]); fox_wo = din("fox_wo", [D, D])
        out_d = dout("out", [NTOK, D])
        if debug:
            dbg_cK = dout("dbg_cK", [128, 512]); dbg_Rown = dout("dbg_Rown", [128, 256]); dbg_bias = dout("dbg_bias", [128, 4, 512])
            dbg_OT = dout("dbg_OT", [128, 8, TT], BF16); dbg_QT = dout("dbg_QT", [128, 8, TT], BF16)
            dbg_fl = dout("dbg_fl", [128, 512])
    if fused:
        KTd_c = [nc.dram_tensor("KTd%d" % i, [128, NTOK], BF16) for i in range(8)]
        Vd_c = [nc.dram_tensor("Vd%d" % i, [NTOK, 128], BF16) for i in range(8)]
        KTg_c = [nc.dram_tensor("KTg%d" % i, [256, NTOK], BF16) for i in range(8)]
        Vg_c = [nc.dram_tensor("Vg%d" % i, [2 * NTOK, 128], BF16) for i in range(8)]
        Fd = nc.dram_tensor("Fd", [NTOK, 16], F32)
        Fg = nc.dram_tensor("Fg", [2 * NTOK, 16], F32)
        x1_d = nc.dram_tensor("x1s", [NTOK, D], F32)
    else:
        if doA:
            KTd = dout("KTd", [128, 8 * NTOK], BF16)
            Vd = dout("Vd", [NTOK, D], BF16)
            Fd = dout("Fd", [NTOK, 16], F32)
            x1_d = dout("x1", [NTOK, D])
        if doB:
            KTg = din("KTg", [256, 8 * NTOK], BF16)
            Vg = din("Vg", [2 * NTOK, D], BF16)
            Fg = din("Fg", [2 * NTOK, 16], F32)
            x1_d = din("x1", [NTOK, D])
    x1_b = [Buf(), Buf()]

    def kt_store(hp):
        return KTd_c[hp][:, :] if fused else KTd.ap().rearrange("p (c t) -> p c t", c=8)[:, hp, :]

    def v_store(hp):
        return Vd_c[hp][:, :] if fused else Vd[:, hp * 128:(hp + 1) * 128]

    def kt_load(hp, r):
        return KTg_c[hp][r * 128:(r + 1) * 128, :] if fused else KTg.ap().rearrange("p (c t) -> p c t", c=8)[r * 128:(r + 1) * 128, hp, :]

    def v_load(hp):
        v = Vg_c[hp].ap() if fused else Vg[:, hp * 128:(hp + 1) * 128]
        return v.rearrange("(b p) d -> p b d", p=128)
    kg_b = [Buf() for _ in range(8)]; vg_b = [Buf() for _ in range(8)]; fg_b = Buf()
    kd_b = [Buf() for _ in range(8)]; vd_b = [Buf() for _ in range(8)]; fd_b = Buf()

    with contextlib.ExitStack() as st:
        def sb(name, shape, dt):
            return st.enter_context(nc.sbuf_tensor("s_" + name, shape, dt))

        def psum(name, shape, dt):
            return st.enter_context(nc.psum_tensor("p_" + name, shape, dt))

        x_sb = sb("x_sb", [128, 8, D], F32); x_b = [Buf() for _ in range(8)]
        hT = sb("hT", [128, 8, TT], BF16); hT_b = [Buf() for _ in range(8)]
        W = [sb("w%d" % i, [128, 8, 1024], BF16) for i in range(3)]; W_b = [Buf() for _ in range(3)]
        wrr = [0]
        gbc = sb("gbc", [128, D], F32); gbc_b = Buf()
        cfl = sb("cfl", [128, 8], F32); cfl_b = Buf()
        cst = sb("cst", [128, 4, 128], F32); cst_b = Buf()
        ident = sb("ident", [128, 128], BF16); ones_bf = sb("ones_bf", [128, 128], BF16); cbf_b = Buf()
        meanw = sb("meanw", [128, 128], BF16)
        ssq = sb("ssq", [128, 16], F32); ssq_b = [Buf() for _ in range(16)]; srr = [0]
        xn = [sb("xn%d" % i, [128, D], BF16) for i in range(2)]; xn_b = [Buf(), Buf()]; xrr = [0]
        psF = [psum("psF%d" % i, [128, 512], F32) for i in range(6)]; psF_b = [Buf() for _ in range(6)]; prr = [0]
        psB = [psum("psB%d" % i, [128, 1024], BF16) for i in range(2)]; psB_b = [Buf(), Buf()]; pbr = [0]
        actA = sb("actA", [128, 8, 1056], BF16); actA_b = Buf()
        actB = [sb("actB%d" % i, [128, 8, 512], BF16) for i in range(2)]; actB_b = [Buf(), Buf()]; abr = [0]
        f32s = [sb("f32s%d" % i, [128, 512], F32) for i in range(3)]; f32s_b = [Buf() for _ in range(3)]; frr = [0]
        mpt = [sb("mpt%d" % i, [128, 512], BF16) for i in range(4)]; mpt_b = [Buf() for _ in range(4)]; mpr = [0]
        KmT = sb("KmT", [128, 8, 256], BF16); KmT_b = Buf()
        Vm = sb("Vm", [128, 2, D], BF16); Vm_b = Buf()
        memT = actB[0][:, :, 0:256]; memT_b = actB_b[0]
        memt = sb("memt", [128, D], F32); memt_b = Buf()
        big = [sb("big%d" % i, [128, 4096], BF16) for i in range(2)]; big_b = [Buf(), Buf()]
        fs = sb("fs", [128, 5, 512], F32); fs_b = [Buf() for _ in range(5)]
        Rown = sb("Rown", [128, 256], F32); Rown_b = Buf()
        out_b = Buf()

        def next_ps():
            i = prr[0]; prr[0] = (i + 1) % 4
            return psF[i], psF_b[i]

        arr = [0]

        def acc_ps():
            i = 4 + arr[0]; arr[0] ^= 1
            return psF[i], psF_b[i]

        def next_pt():
            i = mpr[0]; mpr[0] = (i + 1) % 4
            return mpt[i], mpt_b[i]

        def next_f32():
            i = frr[0]; frr[0] = (i + 1) % 3
            return f32s[i], f32s_b[i]

        def next_actB():
            i = abr[0]; abr[0] = (i + 1) % 2
            return actB[i], actB_b[i]

        def dma(eng, out, in_, reads=(), writes=()):
            P.dma(eng, lambda e: e.dma_start(out=out, in_=in_), reads=reads, writes=writes)

        def load_w(src):
            i = wrr[0]; wrr[0] = (i + 1) % 3
            n = src.shape[1]
            for c in range(8):
                dma("pool", W[i][:, c, 0:n], src[c * 128:(c + 1) * 128, :], writes=[W_b[i]])
            return W[i], W_b[i]

        def load_gain(src_row):
            dma("sp", gbc[:], src_row.broadcast_to([128, D]), writes=[gbc_b])
            return gbc, gbc_b

        def mm_group(ps_ap, pairs, reads, ps_b):
            n = len(pairs)

            def fn(e):
                ins = None
                for i, (l, r) in enumerate(pairs):
                    ins = e.matmul(ps_ap, lhsT=l, rhs=r, start=(i == 0), stop=(i == n - 1))
                return ins
            P.op("pe", fn, reads=reads, writes=[ps_b])

        def act(out, in_, func, reads, writes, bias=None, scale=None, accum_out=None):
            kw = {}
            if bias is not None:
                kw["bias"] = bias
            if scale is not None:
                kw["scale"] = scale
            if accum_out is not None:
                kw["accum_out"] = accum_out
            P.op("act", lambda e: e.activation(out=out, in_=in_, func=func, **kw), reads=reads, writes=writes)

        def tt_op(eng, out, in0, in1, op, reads, writes):
            P.op(eng, lambda e: e.tensor_tensor(out=out, in0=in0, in1=in1, op=op), reads=reads, writes=writes)

        def ts_op(eng, out, in0, s1, op0, reads, writes):
            P.op(eng, lambda e: e.tensor_scalar(out=out, in0=in0, scalar1=s1, scalar2=None, op0=op0),
                 reads=reads, writes=writes)

        def stt_op(eng, out, in0, scalar, in1, op0, op1, reads, writes):
            P.op(eng, lambda e: e.scalar_tensor_tensor(out=out, in0=in0, scalar=scalar, in1=in1, op0=op0, op1=op1),
                 reads=reads, writes=writes)

        def recip(out, in_, reads, writes):
            P.op("dve", lambda e: e.reciprocal(out=out, in_=in_), reads=reads, writes=writes)

        def copy_op(eng, out, in_, reads, writes):
            if eng == "act":
                act(out, in_, AF.Copy, reads, writes)
            else:
                P.op(eng, lambda e: e.tensor_copy(out=out, in_=in_), reads=reads, writes=writes)

        evr = [0]

        def evac_eng():
            evr[0] ^= 1
            return "act" if evr[0] else "dve"

        def rms_rstd(src_ap, src_b, npart, xi):
            i = srr[0]; srr[0] = (i + 1) % 16
            s = ssq[0:npart, i:i + 1]
            P.op("dve", lambda e: e.memset(s, 0.0), writes=[ssq_b[i]])
            act(xn[xi][0:npart, :], src_ap, AF.Square, [src_b, ssq_b[i]], [xn_b[xi], ssq_b[i]], accum_out=s)
            act(s, s, AF.Sqrt, [ssq_b[i], cfl_b], [ssq_b[i]], bias=cfl[0:npart, 6:7], scale=1.0 / D)
            recip(s, s, [ssq_b[i]], [ssq_b[i]])
            return s, ssq_b[i]

        def norm_tile(src_ap, src_b, g, g_b, dstT, dst_b, col0, npart=128):
            i = xrr[0]; xrr[0] ^= 1
            s, s_b = rms_rstd(src_ap, src_b, npart, i)
            stt_op("dve", xn[i][0:npart, :], src_ap, s, g[0:npart, :], ALU.mult, ALU.mult,
                   [src_b, s_b, g_b], [xn_b[i]])
            j = pbr[0]; pbr[0] ^= 1
            pv = psB[j][:].rearrange("p (c t) -> p c t", c=8)

            def fn(e):
                ins = None
                for c in range(8):
                    ins = e.transpose(out=pv[:, c, 0:npart], in_=xn[i][0:npart, c * 128:(c + 1) * 128],
                                      identity=ident[0:npart, 0:npart])
                return ins
            P.op("pe", fn, reads=[xn_b[i], cbf_b], writes=[psB_b[j]])
            copy_op(evac_eng(), dstT[:, :, col0:col0 + npart], pv[:, :, 0:npart], [psB_b[j]], [dst_b])

        def norm_x(gidx):
            g, g_b = load_gain(gains_d[gidx:gidx + 1, :])
            for t in range(8):
                norm_tile(x_sb[:, t, :], x_b[t], g, g_b, hT, hT_b[t], t * 128)
            return g, g_b

        def proj_fm(Wt, W_bf, wcol0, nchunk, src, src_bs, tok0, ntok, epi):
            for c in range(nchunk):
                ps, ps_b = next_ps()
                pairs = [(Wt[:, k, wcol0 + c * 128: wcol0 + (c + 1) * 128], src[:, k, tok0:tok0 + ntok]) for k in range(8)]
                mm_group(ps[:, 0:ntok], pairs, [W_bf] + list(src_bs), ps_b)
                epi(c, ps, ps_b)

        def proj_tm_resid(actT, act_bs, Wt, W_bf, sub0, nsub, extra=None):
            for s in range(nsub):
                xt = sub0 + s
                for dh in range(2):
                    ps, ps_b = next_ps()
                    pairs = [(actT[:, k, s * 128:(s + 1) * 128], Wt[:, k, dh * 512:(dh + 1) * 512]) for k in range(8)]
                    rd = [W_bf] + list(act_bs)
                    if extra is not None:
                        pairs.append((extra[0], extra[1][0:1, dh * 512:(dh + 1) * 512]))
                        rd.append(extra[2])
                    mm_group(ps[:, :], pairs, rd, ps_b)
                    tt_op("dve", x_sb[:, xt, dh * 512:(dh + 1) * 512], ps[:, :], x_sb[:, xt, dh * 512:(dh + 1) * 512],
                          ALU.add, [ps_b, x_b[xt]], [x_b[xt]])

        dma("sp", cfl[:], cfl_d[:, :], writes=[cfl_b])
        dma("sp", cst[:], consts_d[:, :, :], writes=[cst_b])
        copy_op("dve", ident[:], cst[:, 0, :], [cst_b], [cbf_b])
        copy_op("dve", ones_bf[:], cst[:, 3, :], [cst_b], [cbf_b])
        ts_op("dve", meanw[:], cst[:, 3, :], 1.0 / D, ALU.mult, [cst_b], [cbf_b])

        def mem_kv(l):
            g, g_b = load_gain(gsrc_d[l:l + 1, :])
            for i in range(2):
                dma("sp", memt[:], mem_d[i * 128:(i + 1) * 128, :], writes=[memt_b])
                norm_tile(memt[:], memt_b, g, g_b, memT, memT_b, i * 128)
            Wk, Wk_b = load_w(mem_wk[l])

            def epi_k(c, ps, ps_b):
                copy_op(evac_eng(), KmT[:, c, :], ps[:, 0:256], [ps_b], [KmT_b])
            proj_fm(Wk, Wk_b, 0, 8, memT, [memT_b], 0, 256, epi_k)
            Wv, Wv_b = load_w(mem_wv[l])
            for mb in range(2):
                for dh in range(2):
                    ps, ps_b = next_ps()
                    pairs = [(memT[:, k, mb * 128:(mb + 1) * 128], Wv[:, k, dh * 512:(dh + 1) * 512]) for k in range(8)]
                    mm_group(ps[:, :], pairs, [Wv_b, memT_b], ps_b)
                    copy_op(evac_eng(), Vm[:, mb, dh * 512:(dh + 1) * 512], ps[:, :], [ps_b], [Vm_b])

        def mem_attn(l):
            Wq, Wq_b = load_w(mem_wq[l])
            Wo, Wo_b = load_w(mem_wo[l])
            norm_x(1 if l == 0 else 5)
            for tt in range(2):
                QT, QT_b = next_actB()

                def epi_q(c, ps, ps_b, QT=QT, QT_b=QT_b):
                    eng = evac_eng()
                    if eng == "act":
                        P.op("act", lambda e, c=c: e.mul(out=QT[:, c, :], in_=ps[:, :], mul=1.0 / 16), reads=[ps_b], writes=[QT_b])
                    else:
                        ts_op("dve", QT[:, c, :], ps[:, :], 1.0 / 16, ALU.mult, [ps_b], [QT_b])
                proj_fm(Wq, Wq_b, 0, 8, hT, hT_b[tt * 4:(tt + 1) * 4], tt * 512, 512, epi_q)
                OT, OT_b = next_actB()
                for hd in range(4):
                    pts = []
                    for mb in range(2):
                        ps, ps_b = next_ps()
                        pairs = [(KmT[:, 2 * hd + cc, mb * 128:(mb + 1) * 128], QT[:, 2 * hd + cc, :]) for cc in range(2)]
                        mm_group(ps[:, :], pairs, [KmT_b, QT_b], ps_b)
                        pt, pt_b = next_pt()
                        act(pt[:, :], ps[:, :], AF.Exp, [ps_b], [pt_b])
                        pts.append((pt, pt_b))
                    psl, psl_b = acc_ps()
                    mm_group(psl[:, :], [(ones_bf[:, :], pts[mb][0][:, :]) for mb in range(2)],
                             [cbf_b, pts[0][1], pts[1][1]], psl_b)
                    rl, rl_b = next_f32()
                    recip(rl[:], psl[:, :], [psl_b], [rl_b])
                    for dvc in range(2):
                        pso, pso_b = next_ps()
                        col = hd * 256 + dvc * 128
                        mm_group(pso[:, :], [(Vm[:, mb, col:col + 128], pts[mb][0][:, :]) for mb in range(2)],
                                 [Vm_b, pts[0][1], pts[1][1]], pso_b)
                        tt_op("dve", OT[:, 2 * hd + dvc, :], pso[:, :], rl[:], ALU.mult, [pso_b, rl_b], [OT_b])
                proj_tm_resid(OT, [OT_b], Wo, Wo_b, tt * 4, 4)

        def mlp(l):
            def load_slice(sl):
                i = wrr[0]; wrr[0] = (i + 1) % 3
                flat = W[i][:].rearrange("p c n -> p (c n)")
                w1 = flat[:, 0:4096].rearrange("p (c n) -> p c n", c=8)
                w2 = flat[:, 4096:8192].rearrange("p (c n) -> p c n", c=4)
                for c in range(8):
                    dma("pool", w1[:, c, :], ff_w1[l][c * 128:(c + 1) * 128, sl * 512:(sl + 1) * 512], writes=[W_b[i]])
                for c in range(4):
                    dma("pool", w2[:, c, :], ff_w2[l][sl * 512 + c * 128: sl * 512 + (c + 1) * 128, :], writes=[W_b[i]])
                return w1, w2, W_b[i]
            pend = [load_slice(0), load_slice(1)]
            norm_x(2 if l == 0 else 6)
            for sl in range(8):
                if sl + 2 < 8:
                    pend.append(load_slice(sl + 2))
                w1, w2, wb = pend[sl]
                fT, fT_b = next_actB()
                fTv = fT[:].rearrange("p c t -> p (c t)").rearrange("p (c t) -> p c t", c=4)
                for tt in range(2):
                    for c in range(4):
                        ps, ps_b = next_ps()
                        mm_group(ps[:, :], [(w1[:, k, c * 128:(c + 1) * 128], hT[:, k, tt * 512:(tt + 1) * 512]) for k in range(8)],
                                 [wb] + hT_b[tt * 4:(tt + 1) * 4], ps_b)
                        r, r_b = next_f32()
                        act(r[:], ps[:, :], AF.Relu, [ps_b], [r_b])
                        tt_op("dve", fTv[:, c, tt * 512:(tt + 1) * 512], r[:], r[:], ALU.mult, [r_b], [fT_b])
                for s_ in range(8):
                    for dh in range(2):
                        ps, ps_b = next_ps()
                        mm_group(ps[:, :], [(fTv[:, k, s_ * 128:(s_ + 1) * 128], w2[:, k, dh * 512:(dh + 1) * 512]) for k in range(4)],
                                 [wb, fT_b], ps_b)
                        tt_op("dve", x_sb[:, s_, dh * 512:(dh + 1) * 512], ps[:, :], x_sb[:, s_, dh * 512:(dh + 1) * 512],
                              ALU.add, [ps_b, x_b[s_]], [x_b[s_]])

        if doA:
            b1col = sb("b1col", [128, 16], F32); dwcol = Rown[:, 0:248].rearrange("p (c k) -> p c k", k=31)
            bdwcol = sb("bdwcol", [128, 8], F32); lngcol = sb("lngcol", [128, 8], F32); lnbcol = sb("lnbcol", [128, 8], F32)
            b2row = sb("b2row", [1, D], BF16); small_b = Buf()
            dma("sp", b1col[:], b1col_d[:, :], writes=[small_b])
            dma("sp", dwcol, dwcol_d[:, :, :], writes=[small_b])
            dma("sp", bdwcol[:], bdwcol_d[:, :], writes=[small_b])
            dma("sp", lngcol[:], lngcol_d[:, :], writes=[small_b])
            dma("sp", lnbcol[:], lnbcol_d[:, :], writes=[small_b])
            dma("pool", b2row[:], b2row_d[:, :], writes=[small_b])
            hTh = sb("hTh", [128, 8, 32], BF16); hTh_b = Buf()
            dg = big[0][:, 0:31 * 128].rearrange("p (k j) -> p k j", j=128); dg_b = big_b[0]
            vbuf = big[1][:, :].rearrange("p (c t) -> p c t", t=512); vbuf_b = big_b[1]
            gluT = actA
            xv = x_d.ap().rearrange("(t p) d -> p t d", p=128)
            x1v = x1_d.ap().rearrange("(t p) d -> p t d", p=128)
            mem_kv(0)
            for tile0 in range(2):
                for t in range(8):
                    dma("sp", x_sb[:, t, :], xv[:, tile0 * 8 + t, :], writes=[x_b[t]])
                Wa, Wa_b = load_w(pw1_d[:, 0:1024])
                Wg, Wg_b = load_w(pw1_d[:, 1024:2048])
                g, g_b = norm_x(0)
                dma("sp", memt[0:32, :], xhalo_d[tile0], writes=[memt_b])
                norm_tile(memt[0:32, :], memt_b, g, g_b, hTh, hTh_b, 0, npart=32)
                for c in range(8):
                    for (src, src_bs, tok0, ntok, col0) in ((hTh, [hTh_b], 0, 32, 0), (hT, hT_b[0:4], 0, 512, 32),
                                                            (hT, hT_b[4:8], 512, 512, 544)):
                        psa, psa_b = next_ps()
                        mm_group(psa[:, 0:ntok], [(Wa[:, k, c * 128:(c + 1) * 128], src[:, k, tok0:tok0 + ntok]) for k in range(8)],
                                 [Wa_b] + list(src_bs), psa_b)
                        psg, psg_b = next_ps()
                        mm_group(psg[:, 0:ntok], [(Wg[:, k, c * 128:(c + 1) * 128], src[:, k, tok0:tok0 + ntok]) for k in range(8)],
                                 [Wg_b] + list(src_bs), psg_b)
                        sg, sg_b = next_f32()
                        act(sg[:, 0:ntok], psg[:, 0:ntok], AF.Sigmoid, [psg_b, small_b], [sg_b], bias=b1col[:, 8 + c:9 + c])
                        stt_op("dve", gluT[:, c, col0:col0 + ntok], psa[:, 0:ntok], b1col[:, c:c + 1], sg[:, 0:ntok],
                               ALU.add, ALU.mult, [psa_b, sg_b, small_b], [actA_b])
                ts_op("dve", gluT[:, :, 0:32], gluT[:, :, 0:32], cfl[:, tile0:tile0 + 1], ALU.mult,
                      [actA_b, cfl_b], [actA_b])
                W2c, W2c_b = load_w(pw2_d[:, :])
                for tt in range(2):
                    psM, psM_b = acc_ps()
                    psE, psE_b = acc_ps()
                    for c in range(8):
                        for k in range(31):
                            ts_op("dve", dg[:, k, :], ident[:, :], dwcol[:, c, k:k + 1], ALU.mult,
                                  [cbf_b, small_b], [dg_b])
                        ps, ps_b = next_ps()
                        base = tt * 512 + 2
                        mm_group(ps[:, :], [(dg[:, k, :], gluT[:, c, base + k: base + k + 512]) for k in range(31)],
                                 [dg_b, actA_b], ps_b)
                        act(vbuf[:, c, :], ps[:, :], AF.Identity, [ps_b, small_b], [vbuf_b], bias=bdwcol[:, c:c + 1])
                        v2, v2_b = next_pt()
                        tt_op("dve", v2[:, :], vbuf[:, c, :], vbuf[:, c, :], ALU.mult, [vbuf_b], [v2_b])

                        def fn_m(e, c=c, psM=psM):
                            return e.matmul(psM[:, :], lhsT=meanw[:, :], rhs=vbuf[:, c, :], start=(c == 0), stop=(c == 7))
                        P.op("pe", fn_m, reads=[vbuf_b, cbf_b], writes=[psM_b])

                        def fn_e(e, c=c, psE=psE, v2=v2):
                            return e.matmul(psE[:, :], lhsT=meanw[:, :], rhs=v2[:, :], start=(c == 0), stop=(c == 7))
                        P.op("pe", fn_e, reads=[v2_b, cbf_b], writes=[psE_b])
                    mean, msq, rstd = fs[:, 0, :], fs[:, 1, :], fs[:, 2, :]
                    copy_op("act", mean, psM[:, :], [psM_b], [fs_b[0]])
                    tt_op("dve", msq, mean, mean, ALU.mult, [fs_b[0]], [fs_b[1]])
                    tt_op("dve", rstd, psE[:, :], msq, ALU.subtract, [psE_b, fs_b[1]], [fs_b[2]])
                    act(rstd, rstd, AF.Sqrt, [fs_b[2], cfl_b], [fs_b[2]], bias=cfl[:, 7:8])
                    recip(rstd, rstd, [fs_b[2]], [fs_b[2]])
                    zT, zT_b = next_actB()
                    for c in range(8):
                        d, d_b = next_f32()
                        tt_op("dve", d[:], vbuf[:, c, :], mean, ALU.subtract, [vbuf_b, fs_b[0]], [d_b])
                        tt_op("dve", d[:], d[:], rstd, ALU.mult, [d_b, fs_b[2]], [d_b])
                        act(zT[:, c, :], d[:], AF.Silu, [d_b, small_b], [zT_b], bias=lnbcol[:, c:c + 1], scale=lngcol[:, c:c + 1])
                    proj_tm_resid(zT, [zT_b], W2c, W2c_b, tt * 4, 4, extra=(ones_bf[0:1, :], b2row, small_b))
                mem_attn(0)
                mlp(0)
                for t in range(8):
                    dma("sp", x1v[:, tile0 * 8 + t, :], x_sb[:, t, :], reads=[x_b[t]], writes=[out_b, x1_b[tile0]])
                Wkk, Wkk_b = load_w(kvf_d[:, 0:1024])
                Wvv, Wvv_b = load_w(kvf_d[:, 1024:2048])
                norm_x(3)
                kst = actA; kst_b = actA_b
                for tt in range(2):
                    def epi_kk(c, ps, ps_b, tt=tt):
                        copy_op(evac_eng(), kst[:, c, tt * 512:(tt + 1) * 512], ps[:, :], [ps_b], [kst_b])
                    proj_fm(Wkk, Wkk_b, 0, 8, hT, hT_b[tt * 4:(tt + 1) * 4], tt * 512, 512, epi_kk)
                for c in range(8):
                    dma("sp", kt_store(c)[:, tile0 * TT:(tile0 + 1) * TT], kst[:, c, 0:TT], reads=[kst_b], writes=[out_b, kd_b[c]])
                Wff, Wff_b = load_w(kvf_d[:, 2048:2064])
                fst, fst_b = next_f32()
                for s in range(8):
                    vst, vst_b = next_actB()
                    vflat = vst[:].rearrange("p c t -> p (c t)")
                    for dh in range(2):
                        ps, ps_b = next_ps()
                        mm_group(ps[:, :], [(hT[:, k, s * 128:(s + 1) * 128], Wvv[:, k, dh * 512:(dh + 1) * 512]) for k in range(8)],
                                 [Wvv_b, hT_b[s]], ps_b)
                        copy_op(evac_eng(), vflat[:, dh * 512:(dh + 1) * 512], ps[:, :], [ps_b], [vst_b])
                    r0 = tile0 * TT + s * 128
                    if fused:
                        for c in range(8):
                            dma("sp", v_store(c)[r0:r0 + 128, :], vflat[:, c * 128:(c + 1) * 128], reads=[vst_b], writes=[out_b, vd_b[c]])
                    else:
                        dma("sp", Vd[r0:r0 + 128, :], vflat[:, 0:D], reads=[vst_b], writes=[out_b])
                    ps, ps_b = next_ps()
                    mm_group(ps[:, 0:16], [(hT[:, k, s * 128:(s + 1) * 128], Wff[:, k, 0:16]) for k in range(8)],
                             [Wff_b, hT_b[s]], ps_b)
                    copy_op("dve", fst[:, s * 16:(s + 1) * 16], ps[:, 0:16], [ps_b], [fst_b])
                Fv = Fd.ap().rearrange("(s p) h -> p s h", p=128)
                dma("sp", Fv[:, tile0 * 8:(tile0 + 1) * 8, :], fst[:, 0:128].rearrange("p (s h) -> p s h", h=16),
                    reads=[fst_b], writes=[out_b, fd_b])

        if fused:
            rg = [[0, 1], [2, 3], [4, 5], [6, 7]]

            def gather(src, dst, rb, wb):
                P.op("pool", lambda e: e.collective_compute(
                    "AllGather", ALU.bypass, replica_groups=rg, ins=[src.ap().opt()], outs=[dst.ap().opt()]),
                    reads=[rb], writes=[wb])
            gather(Fd, Fg, fd_b, fg_b)
            for c in range(8):
                gather(KTd_c[c], KTg_c[c], kd_b[c], kg_b[c])
                gather(Vd_c[c], Vg_c[c], vd_b[c], vg_b[c])
            P.barrier()

        if doB:
            x1v = x1_d.ap().rearrange("(t p) d -> p t d", p=128)
            fl, Eb, fgb = f32s[0], f32s[1], f32s[2]
            fl_b, Eb_b, fgb_b = f32s_b[0], f32s_b[1], f32s_b[2]
            cK = fs[:, 4, :]; cK_b = fs_b[4]
            masks = sb("masks", [128, 8, 512], BF16); masks_b = Buf()
            ksb = sb("ksb", [128, 4096], BF16); ksb_b = Buf()
            vsb = [big[i][:, :].rearrange("p (b d) -> p b d", d=128) for i in range(2)]; vsb_b = big_b
            mem_kv(1)
            dma("sp", fgb[:], fgb_d[:, :], writes=[fgb_b])
            for i in range(8):
                dma("pool", masks[:, i, :], masks_d[:, i, :], writes=[masks_b])
            Fgv = Fg.ap().rearrange("(b p) h -> p b h", p=128)
            dma("sp", fl[:].rearrange("p (b h) -> p b h", h=16), Fgv, reads=[fg_b], writes=[fl_b])
            tt_op("dve", fl[:], fl[:], fgb[:], ALU.add, [fl_b, fgb_b], [fl_b])
            act(fl[:], fl[:], AF.Exp, [fl_b], [fl_b], scale=-1.0)
            act(fl[:], fl[:], AF.Ln, [fl_b, cfl_b], [fl_b], bias=cfl[:, 5:6])
            P.op("dve", lambda e: e.memset(Eb[:, 0:16], 0.0), writes=[Eb_b])
            for j in range(1, 32):
                tt_op("dve", Eb[:, j * 16:(j + 1) * 16], Eb[:, (j - 1) * 16:j * 16], fl[:, (j - 1) * 16:j * 16], ALU.add,
                      [Eb_b, fl_b], [Eb_b])
            ps, ps_b = next_ps()
            mm_group(ps[:, :], [(cst[:, 1, :], fl[:]), (cst[:, 3, :], Eb[:])], [cst_b, fl_b, Eb_b], ps_b)
            if debug:
                dma("sp", dbg_fl[:, :], fl[:], reads=[fl_b], writes=[out_b])
            copy_op("dve", cK, ps[:, :], [ps_b], [cK_b])
            ps, ps_b = next_ps()
            mm_group(ps[:, :], [(cst[:, 2, :], cK)], [cst_b, cK_b], ps_b)
            tmpR, tmpR_b = f32s[2], f32s_b[2]
            ts_op("dve", tmpR[:, 0:256], ps[:, 0:256], cfl[:, 2:3], ALU.mult, [ps_b, cfl_b], [tmpR_b])
            stt_op("dve", Rown[:], ps[:, 256:512], cfl[:, 3:4], tmpR[:, 0:256], ALU.mult, ALU.add,
                   [ps_b, cfl_b, tmpR_b], [Rown_b])
            P.op("pool", lambda e: e.memset(vsb[0][:, :, 64:128], 1.0), writes=[vsb_b[0]])
            P.op("pool", lambda e: e.memset(vsb[1][:, :, 0:64], 1.0), writes=[vsb_b[1]])
            QT = actA; QT_b = actA_b
            ov = out_d.ap().rearrange("(t p) d -> p t d", p=128)
            for tile0 in range(2):
                for t in range(8):
                    dma("sp", x_sb[:, t, :], x1v[:, tile0 * 8 + t, :], reads=[x1_b[tile0]], writes=[x_b[t]])
                for jl in range(2):
                    j = tile0 * 2 + jl
                    for kb in range(32):
                        tt_op("dve", fs[:, jl, kb * 16:(kb + 1) * 16], cK[:, kb * 16:(kb + 1) * 16],
                              Rown[:, (4 * j + 2) * 16:(4 * j + 3) * 16], ALU.subtract, [cK_b, Rown_b], [fs_b[jl]])
                    lo, hi = (4 * j + 4) * 16, (16 + 4 * j + 4) * 16
                    ts_op("dve", fs[:, jl, lo:hi], fs[:, jl, lo:hi], cfl[:, 4:5], ALU.add, [fs_b[jl], cfl_b], [fs_b[jl]])
                Wq, Wq_b = load_w(fox_wq[:, :])
                Wo, Wo_b = load_w(fox_wo[:, :])
                norm_x(4)
                for tt in range(2):
                    def epi_q(c, ps, ps_b, tt=tt):
                        eng = evac_eng()
                        if eng == "act":
                            P.op("act", lambda e, c=c, tt=tt: e.mul(out=QT[:, c, tt * 512:(tt + 1) * 512], in_=ps[:, :], mul=0.125),
                                 reads=[ps_b], writes=[QT_b])
                        else:
                            ts_op("dve", QT[:, c, tt * 512:(tt + 1) * 512], ps[:, :], 0.125, ALU.mult, [ps_b], [QT_b])
                    proj_fm(Wq, Wq_b, 0, 8, hT, hT_b[tt * 4:(tt + 1) * 4], tt * 512, 512, epi_q)
                OTb = hT
                for hp in range(8):
                    for r in range(2):
                        dma("sp", ksb[:, r * NTOK:(r + 1) * NTOK], kt_load(hp, r),
                            reads=[kg_b[hp]], writes=[ksb_b])
                    for e2 in range(2):
                        head = 2 * hp + e2
                        vcol = 0 if e2 == 0 else 64
                        for bq in range(4):
                            dma("sp", vsb[e2][:, bq * 8:(bq + 1) * 8, vcol:vcol + 64],
                                v_load(hp)[:, bq * 8:(bq + 1) * 8, e2 * 64:(e2 + 1) * 64],
                                reads=[vg_b[hp]], writes=[vsb_b[e2]])
                    for jl in range(2):
                        j = tile0 * 2 + jl
                        nkb = 16 + 4 * j + 4
                        accs = [acc_ps(), acc_ps()]

                        def s_step(kb, j=j, jl=jl, hp=hp):
                            outs = []
                            for e2 in range(2):
                                pr = slice(e2 * 64, (e2 + 1) * 64)
                                pss, pss_b = next_ps()
                                pairs = [(ksb[pr, kb * 128:(kb + 1) * 128], QT[pr, hp, jl * 512:(jl + 1) * 512])]
                                rd = [ksb_b, QT_b]
                                if 4 * j <= kb < 4 * j + 4:
                                    pairs.append((ident[:, :], masks[:, kb - 4 * j, :])); rd += [cbf_b, masks_b]
                                elif kb >= 16 + 4 * j:
                                    pairs.append((ident[:, :], masks[:, 4 + kb - 16 - 4 * j, :])); rd += [cbf_b, masks_b]
                                mm_group(pss[:, :], pairs, rd, pss_b)
                                outs.append((pss, pss_b))
                            pts = []
                            for e2 in range(2):
                                head = 2 * hp + e2
                                pss, pss_b = outs[e2]
                                pt, pt_b = next_pt()
                                act(pt[:, :], pss[:, :], AF.Exp, [pss_b, fs_b[jl]], [pt_b],
                                    bias=fs[:, jl, kb * 16 + head: kb * 16 + head + 1])
                                pts.append((pt, pt_b))
                            return pts

                        def pv_step(kb, pts, nkb=nkb, accs=accs):
                            for e2 in range(2):
                                pso, pso_b = accs[e2]
                                pt, pt_b = pts[e2]

                                def fn_pv(e, kb=kb, pso=pso, e2=e2, pt=pt):
                                    return e.matmul(pso[:, :], lhsT=vsb[e2][:, kb, :], rhs=pt[:, :],
                                                    start=(kb == 0), stop=(kb == nkb - 1))
                                P.op("pe", fn_pv, reads=[vsb_b[e2], pt_b], writes=[pso_b])
                        prev = s_step(0)
                        for kb in range(1, nkb):
                            cur = s_step(kb)
                            pv_step(kb - 1, prev)
                            prev = cur
                        pv_step(nkb - 1, prev)
                        for e2 in range(2):
                            pr = slice(e2 * 64, (e2 + 1) * 64)
                            lr = slice((1 - e2) * 64, (2 - e2) * 64)
                            pso, pso_b = accs[e2]
                            rl, rl_b = next_f32()
                            recip(rl[lr, :], pso[lr, :], [pso_b], [rl_b])
                            tt_op("dve", OTb[pr, hp, jl * 512:(jl + 1) * 512], pso[pr, :], rl[lr, :], ALU.mult,
                                  [pso_b, rl_b], hT_b[jl * 4:(jl + 1) * 4])
                if debug and tile0 == 0:
                    dma("sp", dbg_cK[:, :], cK, reads=[cK_b], writes=[out_b])
                    dma("sp", dbg_Rown[:, :], Rown[:], reads=[Rown_b], writes=[out_b])
                    dma("sp", dbg_bias[:, :, :], fs[:, 0:4, :], reads=fs_b[0:4], writes=[out_b])
                    dma("sp", dbg_OT[:, :, :], OTb[:, :, :], reads=hT_b, writes=[out_b])
                    dma("sp", dbg_QT[:, :, :], QT[:, :, 0:TT], reads=[QT_b], writes=[out_b])
                for tt in range(2):
                    proj_tm_resid(OTb[:, :, tt * 512:(tt + 1) * 512], hT_b[tt * 4:(tt + 1) * 4], Wo, Wo_b, tt * 4, 4)
                if debug:
                    for t in range(8):
                        dma("sp", ov[:, tile0 * 8 + t, :], x_sb[:, t, :], reads=[x_b[t]], writes=[out_b])
                    break
                mem_attn(1)
                mlp(1)
                g, g_b = load_gain(gains_d[7:8, :])
                for t in range(8):
                    xi = xrr[0]; xrr[0] ^= 1
                    s, s_b = rms_rstd(x_sb[:, t, :], x_b[t], 128, xi)
                    stt_op("dve", x_sb[:, t, :], x_sb[:, t, :], s, g[:, :], ALU.mult, ALU.mult,
                           [x_b[t], s_b, g_b], [x_b[t]])
                    dma("sp", ov[:, tile0 * 8 + t, :], x_sb[:, t, :], reads=[x_b[t]], writes=[out_b])
        P.wait_all("sp", [out_b])
        P.emit()
    return nc


def _consts():
    c = np.zeros((128, 4, 128), np.float32)
    c[:, 0, :] = np.eye(128, dtype=np.float32)
    c[:, 1, :] = np.triu(np.ones((128, 128), np.float32))
    c[0, 2, :] = 1.0
    c[:, 3, :] = 1.0
    return c


def _masks(h):
    p = np.arange(128)[:, None]
    q = np.arange(512)[None, :]
    m = np.zeros((128, 8, 512), np.float32)
    for o in range(4):
        tri = np.where(o * 128 + p <= q, 0.0, NEG).astype(np.float32)
        m[:, o, :] = tri if h == 0 else 0.0
        m[:, 4 + o, :] = tri
    return m


def _common_inputs(inp, c):
    b, h = c // 2, c % 2
    gains = np.stack([inp["norm_mix_g"][0], inp["norm_mem_g"][0], inp["norm_ff_g"][0], inp["kv_norm_g"],
                      inp["norm_mix_g"][1], inp["norm_mem_g"][1], inp["norm_ff_g"][1], inp["final_norm_g"]]).astype(np.float32)
    cfl = np.zeros((128, 8), np.float32)
    cfl[:, 0] = float(h)
    cfl[:, 1] = 1.0
    cfl[:, 2] = 1.0 - h
    cfl[:, 3] = float(h)
    cfl[:, 4] = NEG * (1 - h)
    cfl[:, 5] = 1.0
    cfl[:, 6] = 1e-6
    cfl[:, 7] = 1e-5
    return {
        "cflags": cfl, "consts": _consts(), "gains": gains,
        "mem": np.ascontiguousarray(inp["mem"][b]), "gsrc": np.ascontiguousarray(inp["norm_memsrc_g"]),
        "mem_wq": inp["mem_wq"], "mem_wk": inp["mem_wk"], "mem_wv": inp["mem_wv"], "mem_wo": inp["mem_wo"],
        "ff_w1": inp["ff_w1"], "ff_w2": inp["ff_w2"],
    }


def _a_inputs(inp, c):
    b, h = c // 2, c % 2
    x = inp["x"]
    xh = np.zeros((2, 32, D), np.float32)
    if h == 1:
        xh[0] = x[b, NTOK - 32:NTOK]
    xh[1] = x[b, h * NTOK + TT - 32: h * NTOK + TT]
    col = lambda v: np.ascontiguousarray(v.reshape(-1, 128).T)
    return {
        "x": np.ascontiguousarray(x[b, h * NTOK:(h + 1) * NTOK]), "xhalo": xh,
        "pw1": inp["conv_pw1_w"][0], "pw2": inp["conv_pw2_w"][0],
        "b1col": col(inp["conv_pw1_b"][0]),
        "dwcol": np.ascontiguousarray(inp["conv_dw_w"][0].reshape(31, 8, 128).transpose(2, 1, 0)),
        "bdwcol": col(inp["conv_dw_b"][0]), "lngcol": col(inp["conv_ln_g"][0]), "lnbcol": col(inp["conv_ln_b"][0]),
        "b2row": np.ascontiguousarray(inp["conv_pw2_b"][0][None, :]),
        "kvf": inp["kvf_w"],
    }


def _b_inputs(inp, c):
    h = c % 2
    return {
        "fgbrep": np.ascontiguousarray(np.broadcast_to(np.tile(inp["fgate_b"], 32)[None, :], (128, 512))).astype(np.float32),
        "masks": _masks(h),
        "fox_wq": inp["fox_wq"][0], "fox_wo": inp["fox_wo"][0],
    }


_NC_CACHE = {}


def _get_nc(mode):
    if mode not in _NC_CACHE:
        _NC_CACHE[mode] = build(mode)
    return _NC_CACHE[mode]


def kernel(**inp):
    inp = {k: np.asarray(v) for k, v in inp.items()}
    cores = list(range(8))
    if FUSED:
        maps = []
        for c in cores:
            m = _common_inputs(inp, c); m.update(_a_inputs(inp, c)); m.update(_b_inputs(inp, c))
            maps.append(m)
        res = run_bass_kernel_spmd(_get_nc("AB"), maps, core_ids=cores)
        outs = [res.results[c]["out"] for c in cores]
    else:
        maps = []
        for c in cores:
            m = _common_inputs(inp, c); m.update(_a_inputs(inp, c))
            maps.append(m)
        ra = run_bass_kernel_spmd(_get_nc("A"), maps, core_ids=cores).results
        maps = []
        for c in cores:
            p = c - c % 2
            m = _common_inputs(inp, c); m.update(_b_inputs(inp, c))
            m["x1"] = ra[c]["x1"]
            m["KTg"] = np.concatenate([ra[p]["KTd"], ra[p + 1]["KTd"]], axis=0)
            m["Vg"] = np.concatenate([ra[p]["Vd"], ra[p + 1]["Vd"]], axis=0)
            m["Fg"] = np.concatenate([ra[p]["Fd"], ra[p + 1]["Fd"]], axis=0)
            maps.append(m)
        rb = run_bass_kernel_spmd(_get_nc("B"), maps, core_ids=cores).results
        outs = [rb[c]["out"] for c in cores]
    out = np.zeros((4, 4096, D), np.float32)
    for c in cores:
        out[c // 2, (c % 2) * NTOK:(c % 2 + 1) * NTOK] = outs[c]
    return out
```

```python
import contextlib
import numpy as np
import ml_dtypes
import concourse.bass as bass
import concourse.mybir as mybir
from concourse.bass_utils import run_bass_kernel_spmd

F32 = mybir.dt.float32
BF16 = mybir.dt.bfloat16
AF = mybir.ActivationFunctionType
ALU = mybir.AluOpType

ENGS = ["pe", "act", "dve", "pool", "sp"]
D = 1024
NTOK = 2048
TT = 1024
NEG = -30000.0
FUSED = True


class Buf:
    __slots__ = ("w", "r")

    def __init__(self):
        self.w = {}
        self.r = {}


class Prog:
    def __init__(self, nc, ndma=48):
        self.nc = nc
        self.ops = {e: [] for e in ENGS}
        self.cnt = {e: 0 for e in ENGS}
        self.seen = {e: {} for e in ENGS}
        self.ndma = ndma
        self.dma_cnt = [0] * ndma
        self.dma_rr = 0
        self.dma_rr_pool = 0

    def _deps(self, eng, reads, writes):
        waits = {}
        seen = self.seen[eng]

        def need(k, v):
            if seen.get(k, 0) >= v:
                return
            if waits.get(k, 0) < v:
                waits[k] = v
        for b in reads:
            for k, v in b.w.items():
                need(k, v)
        for b in writes:
            for k, v in b.w.items():
                need(k, v)
            for k, v in b.r.items():
                need(k, v)
        for k, v in waits.items():
            seen[k] = v
        return list(waits.items())

    def _mark(self, stamp, reads, writes):
        k, v = stamp
        for b in reads:
            if b.r.get(k, 0) < v:
                b.r[k] = v
        for b in writes:
            if b.w.get(k, 0) < v:
                b.w[k] = v

    def op(self, eng, fn, reads=(), writes=()):
        waits = self._deps(eng, reads, writes)
        self.cnt[eng] += 1
        stamp = (eng, self.cnt[eng])
        self.ops[eng].append((fn, waits, stamp, 1))
        self._mark(stamp, reads, writes)

    def dma(self, eng, fn, reads=(), writes=()):
        waits = self._deps(eng, reads, writes)
        half = self.ndma // 2
        if eng == "pool":
            j = half + self.dma_rr_pool
            self.dma_rr_pool = (self.dma_rr_pool + 1) % (self.ndma - half)
        else:
            j = self.dma_rr
            self.dma_rr = (self.dma_rr + 1) % half
        prev = self.dma_cnt[j]
        if prev > 0 and self.seen[eng].get(("dma", j), 0) < prev:
            waits.append((("dma", j), prev))
            self.seen[eng][("dma", j)] = prev
        self.dma_cnt[j] += 16
        stamp = (("dma", j), self.dma_cnt[j])
        self.ops[eng].append((fn, waits, stamp, 16))
        self._mark(stamp, reads, writes)

    def barrier(self):
        tot = {e: self.cnt[e] for e in ENGS if self.cnt[e] > 0}
        for j in range(self.ndma):
            if self.dma_cnt[j] > 0:
                tot[("dma", j)] = self.dma_cnt[j]
        for e in ENGS:
            waits = [(k, v) for k, v in tot.items() if self.seen[e].get(k, 0) < v]
            for k, v in waits:
                self.seen[e][k] = v
            self.ops[e].append((None, waits, None, 0))

    def wait_self(self, eng):
        v = self.cnt[eng]
        if v > 0 and self.seen[eng].get(eng, 0) < v:
            self.seen[eng][eng] = v
            self.ops[eng].append((None, [(eng, v)], None, 0))

    def wait_all(self, eng, bufs):
        waits = self._deps(eng, bufs, ())
        self.ops[eng].append((None, waits, None, 0))

    def emit(self):
        nc = self.nc
        with contextlib.ExitStack() as st:
            sems = {}
            for e in ENGS:
                sems[e] = st.enter_context(nc.semaphore("s_" + e))
            for j in range(self.ndma):
                sems[("dma", j)] = st.enter_context(nc.semaphore("s_dma%d" % j))
            block = st.enter_context(nc.Block())

            def run(eng_name):
                def body(eng):
                    for fn, waits, stamp, incv in self.ops[eng_name]:
                        for k, v in waits:
                            eng.wait_ge(sems[k], v)
                        if fn is None:
                            continue
                        ins = fn(eng)
                        ins.then_inc(sems[stamp[0]], incv)
                return body

            block.tensor(run("pe"))
            block.scalar(run("act"))
            block.vector(run("dve"))
            block.gpsimd(run("pool"))
            block.sync(run("sp"))


def build(mode, debug=False):
    nc = bass.Bass("TRN2", target_bir_lowering=False)
    P = Prog(nc)
    doA = "A" in mode
    doB = "B" in mode
    fused = mode == "AB"

    def din(name, shape, dt=F32):
        return nc.dram_tensor(name, shape, dt, kind="ExternalInput")

    def dout(name, shape, dt=F32):
        return nc.dram_tensor(name, shape, dt, kind="ExternalOutput")

    cfl_d = din("cflags", [128, 8])
    consts_d = din("consts", [128, 4, 128])
    gains_d = din("gains", [8, D])
    mem_d = din("mem", [256, D])
    gsrc_d = din("gsrc", [2, D])
    mem_wq = din("mem_wq", [2, D, D]); mem_wk = din("mem_wk", [2, D, D])
    mem_wv = din("mem_wv", [2, D, D]); mem_wo = din("mem_wo", [2, D, D])
    ff_w1 = din("ff_w1", [2, D, 4 * D]); ff_w2 = din("ff_w2", [2, 4 * D, D])
    if doA:
        x_d = din("x", [NTOK, D])
        xhalo_d = din("xhalo", [2, 32, D])
        pw1_d = din("pw1", [D, 2 * D]); pw2_d = din("pw2", [D, D])
        b1col_d = din("b1col", [128, 16]); dwcol_d = din("dwcol", [128, 8, 31])
        bdwcol_d = din("bdwcol", [128, 8]); lngcol_d = din("lngcol", [128, 8]); lnbcol_d = din("lnbcol", [128, 8])
        b2row_d = din("b2row", [1, D])
        kvf_d = din("kvf", [D, 2 * D + 16])
    if doB:
        fgb_d = din("fgbrep", [128, 512])
        masks_d = din("masks", [128, 8, 512])
        fox_wq = din("fox_wq", [D, D]); fox_wo = din("fox_wo", [D, D])
        out_d = dout("out", [NTOK, D])
        if debug:
            dbg_cK = dout("dbg_cK", [128, 512]); dbg_Rown = dout("dbg_Rown", [128, 256]); dbg_bias = dout("dbg_bias", [128, 4, 512])
            dbg_OT = dout("dbg_OT", [128, 8, TT], BF16); dbg_QT = dout("dbg_QT", [128, 8, TT], BF16)
            dbg_fl = dout("dbg_fl", [128, 512])
    if fused:
        KTd_c = [nc.dram_tensor("KTd%d" % i, [128, NTOK], BF16) for i in range(8)]
        Vd_c = [nc.dram_tensor("Vd%d" % i, [NTOK, 128], BF16) for i in range(8)]
        KTg_c = [nc.dram_tensor("KTg%d" % i, [256, NTOK], BF16) for i in range(8)]
        Vg_c = [nc.dram_tensor("Vg%d" % i, [2 * NTOK, 128], BF16) for i in range(8)]
        Fd = nc.dram_tensor("Fd", [NTOK, 16], F32)
        Fg = nc.dram_tensor("Fg", [2 * NTOK, 16], F32)
        x1_d = nc.dram_tensor("x1s", [NTOK, D], F32)
    else:
        if doA:
            KTd = dout("KTd", [128, 8 * NTOK], BF16)
            Vd = dout("Vd", [NTOK, D], BF16)
            Fd = dout("Fd", [NTOK, 16], F32)
            x1_d = dout("x1", [NTOK, D])
        if doB:
            KTg = din("KTg", [256, 8 * NTOK], BF16)
            Vg = din("Vg", [2 * NTOK, D], BF16)
            Fg = din("Fg", [2 * NTOK, 16], F32)
            x1_d = din("x1", [NTOK, D])
    x1_b = [Buf(), Buf()]

    def kt_store(hp):
        return KTd_c[hp][:, :] if fused else KTd.ap().rearrange("p (c t) -> p c t", c=8)[:, hp, :]

    def v_store(hp):
        return Vd_c[hp][:, :] if fused else Vd[:, hp * 128:(hp + 1) * 128]

    def kt_load(hp, r):
        return KTg_c[hp][r * 128:(r + 1) * 128, :] if fused else KTg.ap().rearrange("p (c t) -> p c t", c=8)[r * 128:(r + 1) * 128, hp, :]

    def v_load(hp):
        v = Vg_c[hp].ap() if fused else Vg[:, hp * 128:(hp + 1) * 128]
        return v.rearrange("(b p) d -> p b d", p=128)
    kg_b = [Buf() for _ in range(8)]; vg_b = [Buf() for _ in range(8)]; fg_b = Buf()
    kd_b = [Buf() for _ in range(8)]; vd_b = [Buf() for _ in range(8)]; fd_b = Buf()

    with contextlib.ExitStack() as st:
        def sb(name, shape, dt):
            return st.enter_context(nc.sbuf_tensor("s_" + name, shape, dt))

        def psum(name, shape, dt):
            return st.enter_context(nc.psum_tensor("p_" + name, shape, dt))

        x_sb = sb("x_sb", [128, 8, D], F32); x_b = [Buf() for _ in range(8)]
        hT = sb("hT", [128, 8, TT], BF16); hT_b = [Buf() for _ in range(8)]
        W = [sb("w%d" % i, [128, 8, 1024], BF16) for i in range(3)]; W_b = [Buf() for _ in range(3)]
        wrr = [0]; wlast = [0]
        gbc = sb("gbc", [128, D], F32); gbc_b = Buf()
        cfl = sb("cfl", [128, 8], F32); cfl_b = Buf()
        cst = sb("cst", [128, 4, 128], F32); cst_b = Buf()
        ident = sb("ident", [128, 128], BF16); ones_bf = sb("ones_bf", [128, 128], BF16); cbf_b = Buf()
        meanw = sb("meanw", [128, 128], BF16)
        ssq = sb("ssq", [128, 16], F32); ssq_b = [Buf() for _ in range(16)]; srr = [0]
        xn = [sb("xn%d" % i, [128, D], BF16) for i in range(2)]; xn_b = [Buf(), Buf()]; xrr = [0]
        psF = [psum("psF%d" % i, [128, 512], F32) for i in range(6)]; psF_b = [Buf() for _ in range(6)]; prr = [0]
        psB = [psum("psB%d" % i, [128, 1024], BF16) for i in range(2)]; psB_b = [Buf(), Buf()]; pbr = [0]
        actA = sb("actA", [128, 8, 1056], BF16); actA_b = Buf()
        actB = [sb("actB%d" % i, [128, 8, 512], BF16) for i in range(2)]; actB_b = [Buf(), Buf()]; abr = [0]
        f32s = [sb("f32s%d" % i, [128, 512], F32) for i in range(3)]; f32s_b = [Buf() for _ in range(3)]; frr = [0]
        mpt = [sb("mpt%d" % i, [128, 512], BF16) for i in range(4)]; mpt_b = [Buf() for _ in range(4)]; mpr = [0]
        KmT = sb("KmT", [128, 8, 256], BF16); KmT_b = Buf()
        Vm = sb("Vm", [128, 2, D], BF16); Vm_b = Buf()
        memT = actB[0][:, :, 0:256]; memT_b = actB_b[0]
        memt = sb("memt", [128, D], F32); memt_b = Buf()
        big = [sb("big%d" % i, [128, 4096], BF16) for i in range(2)]; big_b = [Buf(), Buf()]
        fs = sb("fs", [128, 5, 512], F32); fs_b = [Buf() for _ in range(5)]
        Rown = sb("Rown", [128, 256], F32); Rown_b = Buf()
        out_b = Buf()

        def next_ps():
            i = prr[0]; prr[0] = (i + 1) % 4
            return psF[i], psF_b[i]

        arr = [0]

        def acc_ps():
            i = 4 + arr[0]; arr[0] ^= 1
            return psF[i], psF_b[i]

        def next_pt():
            i = mpr[0]; mpr[0] = (i + 1) % 4
            return mpt[i], mpt_b[i]

        def next_f32():
            i = frr[0]; frr[0] = (i + 1) % 3
            return f32s[i], f32s_b[i]

        def next_actB():
            i = abr[0]; abr[0] = (i + 1) % 2
            return actB[i], actB_b[i]

        def dma(eng, out, in_, reads=(), writes=()):
            P.dma(eng, lambda e: e.dma_start(out=out, in_=in_), reads=reads, writes=writes)

        def load_w(src):
            i = wrr[0]; wrr[0] = (i + 1) % 3
            wlast[0] = i
            n = src.shape[1]
            for c in range(8):
                dma("pool", W[i][:, c, 0:n], src[c * 128:(c + 1) * 128, :], writes=[W_b[i]])
            return W[i], W_b[i]

        def load_gain(src_row):
            dma("sp", gbc[:], src_row.broadcast_to([128, D]), writes=[gbc_b])
            return gbc, gbc_b

        def mm_group(ps_ap, pairs, reads, ps_b):
            n = len(pairs)

            def fn(e):
                ins = None
                for i, (l, r) in enumerate(pairs):
                    ins = e.matmul(ps_ap, lhsT=l, rhs=r, start=(i == 0), stop=(i == n - 1))
                return ins
            P.op("pe", fn, reads=reads, writes=[ps_b])

        def act(out, in_, func, reads, writes, bias=None, scale=None, accum_out=None):
            kw = {}
            if bias is not None:
                kw["bias"] = bias
            if scale is not None:
                kw["scale"] = scale
            if accum_out is not None:
                kw["accum_out"] = accum_out
            P.op("act", lambda e: e.activation(out=out, in_=in_, func=func, **kw), reads=reads, writes=writes)

        def tt_op(eng, out, in0, in1, op, reads, writes):
            P.op(eng, lambda e: e.tensor_tensor(out=out, in0=in0, in1=in1, op=op), reads=reads, writes=writes)

        def ts_op(eng, out, in0, s1, op0, reads, writes):
            P.op(eng, lambda e: e.tensor_scalar(out=out, in0=in0, scalar1=s1, scalar2=None, op0=op0),
                 reads=reads, writes=writes)

        def stt_op(eng, out, in0, scalar, in1, op0, op1, reads, writes):
            P.op(eng, lambda e: e.scalar_tensor_tensor(out=out, in0=in0, scalar=scalar, in1=in1, op0=op0, op1=op1),
                 reads=reads, writes=writes)

        def recip(out, in_, reads, writes):
            P.op("dve", lambda e: e.reciprocal(out=out, in_=in_), reads=reads, writes=writes)

        def copy_op(eng, out, in_, reads, writes):
            if eng == "act":
                act(out, in_, AF.Copy, reads, writes)
            else:
                P.op(eng, lambda e: e.tensor_copy(out=out, in_=in_), reads=reads, writes=writes)

        evr = [0]

        def evac_eng():
            evr[0] ^= 1
            return "act" if evr[0] else "dve"

        def rms_rstd(src_ap, src_b, npart, xi):
            i = srr[0]; srr[0] = (i + 1) % 16
            s = ssq[0:npart, i:i + 1]
            P.op("dve", lambda e: e.memset(s, 0.0), writes=[ssq_b[i]])
            act(xn[xi][0:npart, :], src_ap, AF.Square, [src_b, ssq_b[i]], [xn_b[xi], ssq_b[i]], accum_out=s)
            act(s, s, AF.Sqrt, [ssq_b[i], cfl_b], [ssq_b[i]], bias=cfl[0:npart, 6:7], scale=1.0 / D)
            recip(s, s, [ssq_b[i]], [ssq_b[i]])
            return s, ssq_b[i]

        def norm_tile(src_ap, src_b, g, g_b, dstT, dst_b, col0, npart=128):
            i = xrr[0]; xrr[0] ^= 1
            s, s_b = rms_rstd(src_ap, src_b, npart, i)
            stt_op("dve", xn[i][0:npart, :], src_ap, s, g[0:npart, :], ALU.mult, ALU.mult,
                   [src_b, s_b, g_b], [xn_b[i]])
            j = pbr[0]; pbr[0] ^= 1
            pv = psB[j][:].rearrange("p (c t) -> p c t", c=8)

            def fn(e):
                ins = None
                for c in range(8):
                    ins = e.transpose(out=pv[:, c, 0:npart], in_=xn[i][0:npart, c * 128:(c + 1) * 128],
                                      identity=ident[0:npart, 0:npart])
                return ins
            P.op("pe", fn, reads=[xn_b[i], cbf_b], writes=[psB_b[j]])
            copy_op(evac_eng(), dstT[:, :, col0:col0 + npart], pv[:, :, 0:npart], [psB_b[j]], [dst_b])

        def norm_x(gidx):
            g, g_b = load_gain(gains_d[gidx:gidx + 1, :])
            for t in range(8):
                norm_tile(x_sb[:, t, :], x_b[t], g, g_b, hT, hT_b[t], t * 128)
            return g, g_b

        def proj_fm(Wt, W_bf, wcol0, nchunk, src, src_bs, tok0, ntok, epi):
            for c in range(nchunk):
                ps, ps_b = next_ps()
                pairs = [(Wt[:, k, wcol0 + c * 128: wcol0 + (c + 1) * 128], src[:, k, tok0:tok0 + ntok]) for k in range(8)]
                mm_group(ps[:, 0:ntok], pairs, [W_bf] + list(src_bs), ps_b)
                epi(c, ps, ps_b)

        def proj_tm_resid(actT, act_bs, Wt, W_bf, sub0, nsub, extra=None):
            for s in range(nsub):
                xt = sub0 + s
                for dh in range(2):
                    ps, ps_b = next_ps()
                    pairs = [(actT[:, k, s * 128:(s + 1) * 128], Wt[:, k, dh * 512:(dh + 1) * 512]) for k in range(8)]
                    rd = [W_bf] + list(act_bs)
                    if extra is not None:
                        pairs.append((extra[0], extra[1][0:1, dh * 512:(dh + 1) * 512]))
                        rd.append(extra[2])
                    mm_group(ps[:, :], pairs, rd, ps_b)
                    tt_op("dve", x_sb[:, xt, dh * 512:(dh + 1) * 512], ps[:, :], x_sb[:, xt, dh * 512:(dh + 1) * 512],
                          ALU.add, [ps_b, x_b[xt]], [x_b[xt]])

        dma("sp", cfl[:], cfl_d[:, :], writes=[cfl_b])
        dma("sp", cst[:], consts_d[:, :, :], writes=[cst_b])
        copy_op("dve", ident[:], cst[:, 0, :], [cst_b], [cbf_b])
        copy_op("dve", ones_bf[:], cst[:, 3, :], [cst_b], [cbf_b])
        ts_op("dve", meanw[:], cst[:, 3, :], 1.0 / D, ALU.mult, [cst_b], [cbf_b])

        def mem_kv(l, pre=None):
            g, g_b = load_gain(gsrc_d[l:l + 1, :])
            for i in range(2):
                dma("sp", memt[:], mem_d[i * 128:(i + 1) * 128, :], writes=[memt_b])
                norm_tile(memt[:], memt_b, g, g_b, memT, memT_b, i * 128)
            Wk, Wk_b = pre[0] if pre else load_w(mem_wk[l])

            def epi_k(c, ps, ps_b):
                copy_op(evac_eng(), KmT[:, c, :], ps[:, 0:256], [ps_b], [KmT_b])
            proj_fm(Wk, Wk_b, 0, 8, memT, [memT_b], 0, 256, epi_k)
            Wv, Wv_b = pre[1] if pre else load_w(mem_wv[l])
            for mb in range(2):
                for dh in range(2):
                    ps, ps_b = next_ps()
                    pairs = [(memT[:, k, mb * 128:(mb + 1) * 128], Wv[:, k, dh * 512:(dh + 1) * 512]) for k in range(8)]
                    mm_group(ps[:, :], pairs, [Wv_b, memT_b], ps_b)
                    copy_op(evac_eng(), Vm[:, mb, dh * 512:(dh + 1) * 512], ps[:, :], [ps_b], [Vm_b])

        def mem_attn(l):
            Wq, Wq_b = load_w(mem_wq[l])
            Wo, Wo_b = load_w(mem_wo[l])
            norm_x(1 if l == 0 else 5)
            for tt in range(2):
                QT, QT_b = next_actB()

                def epi_q(c, ps, ps_b, QT=QT, QT_b=QT_b):
                    eng = evac_eng()
                    if eng == "act":
                        P.op("act", lambda e, c=c: e.mul(out=QT[:, c, :], in_=ps[:, :], mul=1.0 / 16), reads=[ps_b], writes=[QT_b])
                    else:
                        ts_op("dve", QT[:, c, :], ps[:, :], 1.0 / 16, ALU.mult, [ps_b], [QT_b])
                proj_fm(Wq, Wq_b, 0, 8, hT, hT_b[tt * 4:(tt + 1) * 4], tt * 512, 512, epi_q)
                OT, OT_b = next_actB()
                for hd in range(4):
                    pts = []
                    for mb in range(2):
                        ps, ps_b = next_ps()
                        pairs = [(KmT[:, 2 * hd + cc, mb * 128:(mb + 1) * 128], QT[:, 2 * hd + cc, :]) for cc in range(2)]
                        mm_group(ps[:, :], pairs, [KmT_b, QT_b], ps_b)
                        pt, pt_b = next_pt()
                        act(pt[:, :], ps[:, :], AF.Exp, [ps_b], [pt_b])
                        pts.append((pt, pt_b))
                    psl, psl_b = acc_ps()
                    mm_group(psl[:, :], [(ones_bf[:, :], pts[mb][0][:, :]) for mb in range(2)],
                             [cbf_b, pts[0][1], pts[1][1]], psl_b)
                    rl, rl_b = next_f32()
                    recip(rl[:], psl[:, :], [psl_b], [rl_b])
                    for dvc in range(2):
                        pso, pso_b = next_ps()
                        col = hd * 256 + dvc * 128
                        mm_group(pso[:, :], [(Vm[:, mb, col:col + 128], pts[mb][0][:, :]) for mb in range(2)],
                                 [Vm_b, pts[0][1], pts[1][1]], pso_b)
                        tt_op("dve", OT[:, 2 * hd + dvc, :], pso[:, :], rl[:], ALU.mult, [pso_b, rl_b], [OT_b])
                proj_tm_resid(OT, [OT_b], Wo, Wo_b, tt * 4, 4)

        def mlp(l):
            def load_slice(sl):
                i = wrr[0]; wrr[0] = (i + 1) % 3
                flat = W[i][:].rearrange("p c n -> p (c n)")
                w1 = flat[:, 0:4096].rearrange("p (c n) -> p c n", c=8)
                w2 = flat[:, 4096:8192].rearrange("p (c n) -> p c n", c=4)
                for c in range(8):
                    dma("pool", w1[:, c, :], ff_w1[l][c * 128:(c + 1) * 128, sl * 512:(sl + 1) * 512], writes=[W_b[i]])
                for c in range(4):
                    dma("pool", w2[:, c, :], ff_w2[l][sl * 512 + c * 128: sl * 512 + (c + 1) * 128, :], writes=[W_b[i]])
                return w1, w2, W_b[i]
            pend = [load_slice(0), load_slice(1)]
            norm_x(2 if l == 0 else 6)
            for sl in range(8):
                if sl + 2 < 8:
                    pend.append(load_slice(sl + 2))
                w1, w2, wb = pend[sl]
                fT, fT_b = next_actB()
                fTv = fT[:].rearrange("p c t -> p (c t)").rearrange("p (c t) -> p c t", c=4)
                for tt in range(2):
                    for c in range(4):
                        ps, ps_b = next_ps()
                        mm_group(ps[:, :], [(w1[:, k, c * 128:(c + 1) * 128], hT[:, k, tt * 512:(tt + 1) * 512]) for k in range(8)],
                                 [wb] + hT_b[tt * 4:(tt + 1) * 4], ps_b)
                        r, r_b = next_f32()
                        act(r[:], ps[:, :], AF.Relu, [ps_b], [r_b])
                        tt_op("dve", fTv[:, c, tt * 512:(tt + 1) * 512], r[:], r[:], ALU.mult, [r_b], [fT_b])
                for s_ in range(8):
                    for dh in range(2):
                        ps, ps_b = next_ps()
                        mm_group(ps[:, :], [(fTv[:, k, s_ * 128:(s_ + 1) * 128], w2[:, k, dh * 512:(dh + 1) * 512]) for k in range(4)],
                                 [wb, fT_b], ps_b)
                        tt_op("dve", x_sb[:, s_, dh * 512:(dh + 1) * 512], ps[:, :], x_sb[:, s_, dh * 512:(dh + 1) * 512],
                              ALU.add, [ps_b, x_b[s_]], [x_b[s_]])

        if doA:
            b1col = sb("b1col", [128, 16], F32); dwcol = Rown[:, 0:248].rearrange("p (c k) -> p c k", k=31)
            bdwcol = sb("bdwcol", [128, 8], F32); lngcol = sb("lngcol", [128, 8], F32); lnbcol = sb("lnbcol", [128, 8], F32)
            b2row = sb("b2row", [1, D], BF16); small_b = Buf()
            dma("sp", b1col[:], b1col_d[:, :], writes=[small_b])
            dma("sp", dwcol, dwcol_d[:, :, :], writes=[small_b, Rown_b])
            dma("sp", bdwcol[:], bdwcol_d[:, :], writes=[small_b])
            dma("sp", lngcol[:], lngcol_d[:, :], writes=[small_b])
            dma("sp", lnbcol[:], lnbcol_d[:, :], writes=[small_b])
            dma("pool", b2row[:], b2row_d[:, :], writes=[small_b])
            hTh = sb("hTh", [128, 8, 32], BF16); hTh_b = Buf()
            dg = big[0][:, 0:31 * 128].rearrange("p (k j) -> p k j", j=128); dg_b = big_b[0]
            vbuf = big[1][:, :].rearrange("p (c t) -> p c t", t=512); vbuf_b = big_b[1]
            gluT = actA
            xv = x_d.ap().rearrange("(t p) d -> p t d", p=128)
            x1v = x1_d.ap().rearrange("(t p) d -> p t d", p=128)
            mem_kv(0)
            for tile0 in range(2):
                for t in range(8):
                    dma("sp", x_sb[:, t, :], xv[:, tile0 * 8 + t, :], writes=[x_b[t]])
                Wa, Wa_b = load_w(pw1_d[:, 0:1024])
                Wg, Wg_b = load_w(pw1_d[:, 1024:2048])
                g, g_b = norm_x(0)
                dma("sp", memt[0:32, :], xhalo_d[tile0], writes=[memt_b])
                norm_tile(memt[0:32, :], memt_b, g, g_b, hTh, hTh_b, 0, npart=32)
                for c in range(8):
                    for (src, src_bs, tok0, ntok, col0) in ((hTh, [hTh_b], 0, 32, 0), (hT, hT_b[0:4], 0, 512, 32),
                                                            (hT, hT_b[4:8], 512, 512, 544)):
                        psa, psa_b = next_ps()
                        mm_group(psa[:, 0:ntok], [(Wa[:, k, c * 128:(c + 1) * 128], src[:, k, tok0:tok0 + ntok]) for k in range(8)],
                                 [Wa_b] + list(src_bs), psa_b)
                        psg, psg_b = next_ps()
                        mm_group(psg[:, 0:ntok], [(Wg[:, k, c * 128:(c + 1) * 128], src[:, k, tok0:tok0 + ntok]) for k in range(8)],
                                 [Wg_b] + list(src_bs), psg_b)
                        sg, sg_b = next_f32()
                        act(sg[:, 0:ntok], psg[:, 0:ntok], AF.Sigmoid, [psg_b, small_b], [sg_b], bias=b1col[:, 8 + c:9 + c])
                        stt_op("dve", gluT[:, c, col0:col0 + ntok], psa[:, 0:ntok], b1col[:, c:c + 1], sg[:, 0:ntok],
                               ALU.add, ALU.mult, [psa_b, sg_b, small_b], [actA_b])
                ts_op("dve", gluT[:, :, 0:32], gluT[:, :, 0:32], cfl[:, tile0:tile0 + 1], ALU.mult,
                      [actA_b, cfl_b], [actA_b])
                W2c, W2c_b = load_w(pw2_d[:, :])
                for tt in range(2):
                    psM, psM_b = acc_ps()
                    psE, psE_b = acc_ps()
                    for c in range(8):
                        for k in range(31):
                            if k % 2 == 0:
                                ts_op("dve", dg[:, k, :], ident[:, :], dwcol[:, c, k:k + 1], ALU.mult,
                                      [cbf_b, small_b, Rown_b], [dg_b])
                            else:
                                act(dg[:, k, :], ident[:, :], AF.Identity, [cbf_b, small_b, Rown_b], [dg_b],
                                    scale=dwcol[:, c, k:k + 1])
                        ps, ps_b = next_ps()
                        base = tt * 512 + 2
                        mm_group(ps[:, :], [(dg[:, k, :], gluT[:, c, base + k: base + k + 512]) for k in range(31)],
                                 [dg_b, actA_b], ps_b)
                        act(vbuf[:, c, :], ps[:, :], AF.Identity, [ps_b, small_b], [vbuf_b], bias=bdwcol[:, c:c + 1])
                        v2, v2_b = next_pt()
                        tt_op("dve", v2[:, :], vbuf[:, c, :], vbuf[:, c, :], ALU.mult, [vbuf_b], [v2_b])

                        def fn_m(e, c=c, psM=psM):
                            return e.matmul(psM[:, :], lhsT=meanw[:, :], rhs=vbuf[:, c, :], start=(c == 0), stop=(c == 7))
                        P.op("pe", fn_m, reads=[vbuf_b, cbf_b], writes=[psM_b])

                        def fn_e(e, c=c, psE=psE, v2=v2):
                            return e.matmul(psE[:, :], lhsT=meanw[:, :], rhs=v2[:, :], start=(c == 0), stop=(c == 7))
                        P.op("pe", fn_e, reads=[v2_b, cbf_b], writes=[psE_b])
                    mean, msq, rstd = fs[:, 0, :], fs[:, 1, :], fs[:, 2, :]
                    copy_op("act", mean, psM[:, :], [psM_b], [fs_b[0]])
                    tt_op("dve", msq, mean, mean, ALU.mult, [fs_b[0]], [fs_b[1]])
                    tt_op("dve", rstd, psE[:, :], msq, ALU.subtract, [psE_b, fs_b[1]], [fs_b[2]])
                    act(rstd, rstd, AF.Sqrt, [fs_b[2], cfl_b], [fs_b[2]], bias=cfl[:, 7:8])
                    recip(rstd, rstd, [fs_b[2]], [fs_b[2]])
                    zT, zT_b = next_actB()
                    for c in range(8):
                        d, d_b = next_f32()
                        tt_op("dve", d[:], vbuf[:, c, :], mean, ALU.subtract, [vbuf_b, fs_b[0]], [d_b])
                        tt_op("dve", d[:], d[:], rstd, ALU.mult, [d_b, fs_b[2]], [d_b])
                        act(zT[:, c, :], d[:], AF.Silu, [d_b, small_b], [zT_b], bias=lnbcol[:, c:c + 1], scale=lngcol[:, c:c + 1])
                    proj_tm_resid(zT, [zT_b], W2c, W2c_b, tt * 4, 4, extra=(ones_bf[0:1, :], b2row, small_b))
                mem_attn(0)
                mlp(0)
                for t in range(8):
                    dma("sp", x1v[:, tile0 * 8 + t, :], x_sb[:, t, :], reads=[x_b[t]], writes=[out_b, x1_b[tile0]])
                Wkk, Wkk_b = load_w(kvf_d[:, 0:1024])
                Wvv, Wvv_b = load_w(kvf_d[:, 1024:2048])
                norm_x(3)
                kst = actA; kst_b = actA_b
                for tt in range(2):
                    def epi_kk(c, ps, ps_b, tt=tt):
                        copy_op(evac_eng(), kst[:, c, tt * 512:(tt + 1) * 512], ps[:, :], [ps_b], [kst_b])
                    proj_fm(Wkk, Wkk_b, 0, 8, hT, hT_b[tt * 4:(tt + 1) * 4], tt * 512, 512, epi_kk)
                for c in range(8):
                    dma("sp", kt_store(c)[:, tile0 * TT:(tile0 + 1) * TT], kst[:, c, 0:TT], reads=[kst_b], writes=[out_b, kd_b[c]])
                Wff, Wff_b = load_w(kvf_d[:, 2048:2064])
                fst, fst_b = next_f32()
                for s in range(8):
                    vst, vst_b = next_actB()
                    vflat = vst[:].rearrange("p c t -> p (c t)")
                    for dh in range(2):
                        ps, ps_b = next_ps()
                        mm_group(ps[:, :], [(hT[:, k, s * 128:(s + 1) * 128], Wvv[:, k, dh * 512:(dh + 1) * 512]) for k in range(8)],
                                 [Wvv_b, hT_b[s]], ps_b)
                        copy_op(evac_eng(), vflat[:, dh * 512:(dh + 1) * 512], ps[:, :], [ps_b], [vst_b])
                    r0 = tile0 * TT + s * 128
                    if fused:
                        for c in range(8):
                            dma("sp", v_store(c)[r0:r0 + 128, :], vflat[:, c * 128:(c + 1) * 128], reads=[vst_b], writes=[out_b, vd_b[c]])
                    else:
                        dma("sp", Vd[r0:r0 + 128, :], vflat[:, 0:D], reads=[vst_b], writes=[out_b])
                    ps, ps_b = next_ps()
                    mm_group(ps[:, 0:16], [(hT[:, k, s * 128:(s + 1) * 128], Wff[:, k, 0:16]) for k in range(8)],
                             [Wff_b, hT_b[s]], ps_b)
                    copy_op("dve", fst[:, s * 16:(s + 1) * 16], ps[:, 0:16], [ps_b], [fst_b])
                Fv = Fd.ap().rearrange("(s p) h -> p s h", p=128)
                dma("sp", Fv[:, tile0 * 8:(tile0 + 1) * 8, :], fst[:, 0:128].rearrange("p (s h) -> p s h", h=16),
                    reads=[fst_b], writes=[out_b, fd_b])

        pre_l1 = None
        masks = masks_b = None
        if doB:
            masks = sb("masks", [128, 8, 512], BF16); masks_b = Buf()
        if fused:
            rg = [[0, 1], [2, 3], [4, 5], [6, 7]]

            def gather(src, dst, rb, wb):
                P.op("pool", lambda e: e.collective_compute(
                    "AllGather", ALU.bypass, replica_groups=rg, ins=[src.ap().opt()], outs=[dst.ap().opt()]),
                    reads=[rb], writes=[wb])
            gather(Fd, Fg, fd_b, fg_b)
            for c in range(8):
                gather(KTd_c[c], KTg_c[c], kd_b[c], kg_b[c])
                gather(Vd_c[c], Vg_c[c], vd_b[c], vg_b[c])
            P.barrier()

        if doB:
            x1v = x1_d.ap().rearrange("(t p) d -> p t d", p=128)
            fl, Eb, fgb = f32s[0], f32s[1], f32s[2]
            fl_b, Eb_b, fgb_b = f32s_b[0], f32s_b[1], f32s_b[2]
            cK = fs[:, 4, :]; cK_b = fs_b[4]
            ksb = sb("ksb", [128, 4096], BF16); ksb_b = Buf()
            vsb = [big[i][:, :].rearrange("p (b d) -> p b d", d=128) for i in range(2)]; vsb_b = big_b
            mem_kv(1, pre=pre_l1)
            dma("sp", fgb[:], fgb_d[:, :], writes=[fgb_b])
            for i in range(8):
                dma("pool", masks[:, i, :], masks_d[:, i, :], writes=[masks_b])
            Fgv = Fg.ap().rearrange("(b p) h -> p b h", p=128)
            dma("sp", fl[:].rearrange("p (b h) -> p b h", h=16), Fgv, reads=[fg_b], writes=[fl_b])
            tt_op("dve", fl[:], fl[:], fgb[:], ALU.add, [fl_b, fgb_b], [fl_b])
            act(fl[:], fl[:], AF.Exp, [fl_b], [fl_b], scale=-1.0)
            act(fl[:], fl[:], AF.Ln, [fl_b, cfl_b], [fl_b], bias=cfl[:, 5:6])
            P.op("dve", lambda e: e.memset(Eb[:, 0:16], 0.0), writes=[Eb_b])
            for j in range(1, 32):
                tt_op("dve", Eb[:, j * 16:(j + 1) * 16], Eb[:, (j - 1) * 16:j * 16], fl[:, (j - 1) * 16:j * 16], ALU.add,
                      [Eb_b, fl_b], [Eb_b])
            ps, ps_b = next_ps()
            mm_group(ps[:, :], [(cst[:, 1, :], fl[:]), (cst[:, 3, :], Eb[:])], [cst_b, fl_b, Eb_b], ps_b)
            if debug:
                dma("sp", dbg_fl[:, :], fl[:], reads=[fl_b], writes=[out_b])
            copy_op("dve", cK, ps[:, :], [ps_b], [cK_b])
            ps, ps_b = next_ps()
            mm_group(ps[:, :], [(cst[:, 2, :], cK)], [cst_b, cK_b], ps_b)
            tmpR, tmpR_b = f32s[2], f32s_b[2]
            ts_op("dve", tmpR[:, 0:256], ps[:, 0:256], cfl[:, 2:3], ALU.mult, [ps_b, cfl_b], [tmpR_b])
            stt_op("dve", Rown[:], ps[:, 256:512], cfl[:, 3:4], tmpR[:, 0:256], ALU.mult, ALU.add,
                   [ps_b, cfl_b, tmpR_b], [Rown_b])
            P.op("pool", lambda e: e.memset(vsb[0][:, :, 64:128], 1.0), writes=[vsb_b[0]])
            P.op("pool", lambda e: e.memset(vsb[1][:, :, 0:64], 1.0), writes=[vsb_b[1]])
            QT = actA; QT_b = actA_b
            ov = out_d.ap().rearrange("(t p) d -> p t d", p=128)
            for tile0 in range(2):
                for t in range(8):
                    dma("sp", x_sb[:, t, :], x1v[:, tile0 * 8 + t, :], reads=[x1_b[tile0]], writes=[x_b[t]])
                for jl in range(2):
                    j = tile0 * 2 + jl
                    for kb in range(32):
                        tt_op("dve", fs[:, jl, kb * 16:(kb + 1) * 16], cK[:, kb * 16:(kb + 1) * 16],
                              Rown[:, (4 * j + 2) * 16:(4 * j + 3) * 16], ALU.subtract, [cK_b, Rown_b], [fs_b[jl]])
                    lo, hi = (4 * j + 4) * 16, (16 + 4 * j + 4) * 16
                    ts_op("dve", fs[:, jl, lo:hi], fs[:, jl, lo:hi], cfl[:, 4:5], ALU.add, [fs_b[jl], cfl_b], [fs_b[jl]])
                if pre_l1 is not None and tile0 == 0:
                    (Wq, Wq_b), iq = pre_l1[2], pre_l1[3]
                else:
                    Wq, Wq_b = load_w(fox_wq[:, :]); iq = wlast[0]
                Wo, Wo_b = load_w(fox_wo[:, :]); io = wlast[0]
                ifree = [i for i in range(3) if i not in (iq, io)][0]
                wrr[0] = ifree
                norm_x(4)
                for tt in range(2):
                    def epi_q(c, ps, ps_b, tt=tt):
                        eng = evac_eng()
                        if eng == "act":
                            P.op("act", lambda e, c=c, tt=tt: e.mul(out=QT[:, c, tt * 512:(tt + 1) * 512], in_=ps[:, :], mul=0.125),
                                 reads=[ps_b], writes=[QT_b])
                        else:
                            ts_op("dve", QT[:, c, tt * 512:(tt + 1) * 512], ps[:, :], 0.125, ALU.mult, [ps_b], [QT_b])
                    proj_fm(Wq, Wq_b, 0, 8, hT, hT_b[tt * 4:(tt + 1) * 4], tt * 512, 512, epi_q)
                OTb = hT
                kflat = W[ifree][:].rearrange("p c n -> p (c n)")
                qflat = W[iq][:].rearrange("p c n -> p (c n)")
                ksbs = [(ksb, ksb_b), (kflat[:, 0:4096], W_b[ifree])]
                vsbs = [[(vsb[e2], vsb_b[e2]),
                         (qflat[:, e2 * 4096:(e2 + 1) * 4096].rearrange("p (b d) -> p b d", d=128), W_b[iq])] for e2 in range(2)]
                P.op("dve", lambda e, v=vsbs[0][1][0]: e.memset(v[:, :, 64:128], 1.0), writes=[W_b[iq]])
                P.op("dve", lambda e, v=vsbs[1][1][0]: e.memset(v[:, :, 0:64], 1.0), writes=[W_b[iq]])
                for hp in range(8):
                    ksb_c, ksb_cb = ksbs[hp % 2]
                    for r in range(2):
                        dma("sp", ksb_c[:, r * NTOK:(r + 1) * NTOK], kt_load(hp, r),
                            reads=[kg_b[hp]], writes=[ksb_cb])
                    vcur = [vsbs[e2][hp % 2] for e2 in range(2)]
                    for e2 in range(2):
                        vcol = 0 if e2 == 0 else 64
                        for bq in range(4):
                            dma("sp", vcur[e2][0][:, bq * 8:(bq + 1) * 8, vcol:vcol + 64],
                                v_load(hp)[:, bq * 8:(bq + 1) * 8, e2 * 64:(e2 + 1) * 64],
                                reads=[vg_b[hp]], writes=[vcur[e2][1]])
                    for jl in range(2):
                        j = tile0 * 2 + jl
                        nkb = 16 + 4 * j + 4
                        accs = [acc_ps(), acc_ps()]

                        def s_step(kb, j=j, jl=jl, hp=hp, ksb=ksb_c, ksb_b=ksb_cb):
                            outs = []
                            for e2 in range(2):
                                pr = slice(e2 * 64, (e2 + 1) * 64)
                                pss, pss_b = next_ps()
                                pairs = [(ksb[pr, kb * 128:(kb + 1) * 128], QT[pr, hp, jl * 512:(jl + 1) * 512])]
                                rd = [ksb_b, QT_b]
                                if 4 * j <= kb < 4 * j + 4:
                                    pairs.append((ident[:, :], masks[:, kb - 4 * j, :])); rd += [cbf_b, masks_b]
                                elif kb >= 16 + 4 * j:
                                    pairs.append((ident[:, :], masks[:, 4 + kb - 16 - 4 * j, :])); rd += [cbf_b, masks_b]
                                mm_group(pss[:, :], pairs, rd, pss_b)
                                outs.append((pss, pss_b))
                            pts = []
                            for e2 in range(2):
                                head = 2 * hp + e2
                                pss, pss_b = outs[e2]
                                pt, pt_b = next_pt()
                                act(pt[:, :], pss[:, :], AF.Exp, [pss_b, fs_b[jl]], [pt_b],
                                    bias=fs[:, jl, kb * 16 + head: kb * 16 + head + 1])
                                pts.append((pt, pt_b))
                            return pts

                        def pv_step(kb, pts, nkb=nkb, accs=accs, vcur=vcur):
                            for e2 in range(2):
                                pso, pso_b = accs[e2]
                                pt, pt_b = pts[e2]
                                vt, vt_b = vcur[e2]

                                def fn_pv(e, kb=kb, pso=pso, vt=vt, pt=pt):
                                    return e.matmul(pso[:, :], lhsT=vt[:, kb, :], rhs=pt[:, :],
                                                    start=(kb == 0), stop=(kb == nkb - 1))
                                P.op("pe", fn_pv, reads=[vt_b, pt_b], writes=[pso_b])
                        prev = s_step(0)
                        for kb in range(1, nkb):
                            cur = s_step(kb)
                            pv_step(kb - 1, prev)
                            prev = cur
                        pv_step(nkb - 1, prev)
                        for e2 in range(2):
                            pr = slice(e2 * 64, (e2 + 1) * 64)
                            lr = slice((1 - e2) * 64, (2 - e2) * 64)
                            pso, pso_b = accs[e2]
                            rl, rl_b = next_f32()
                            recip(rl[lr, :], pso[lr, :], [pso_b], [rl_b])
                            tt_op("dve", OTb[pr, hp, jl * 512:(jl + 1) * 512], pso[pr, :], rl[lr, :], ALU.mult,
                                  [pso_b, rl_b], hT_b[jl * 4:(jl + 1) * 4])
                if debug and tile0 == 0:
                    dma("sp", dbg_cK[:, :], cK, reads=[cK_b], writes=[out_b])
                    dma("sp", dbg_Rown[:, :], Rown[:], reads=[Rown_b], writes=[out_b])
                    dma("sp", dbg_bias[:, :, :], fs[:, 0:4, :], reads=fs_b[0:4], writes=[out_b])
                    dma("sp", dbg_OT[:, :, :], OTb[:, :, :], reads=hT_b, writes=[out_b])
                    dma("sp", dbg_QT[:, :, :], QT[:, :, 0:TT], reads=[QT_b], writes=[out_b])
                for tt in range(2):
                    proj_tm_resid(OTb[:, :, tt * 512:(tt + 1) * 512], hT_b[tt * 4:(tt + 1) * 4], Wo, Wo_b, tt * 4, 4)
                if debug:
                    for t in range(8):
                        dma("sp", ov[:, tile0 * 8 + t, :], x_sb[:, t, :], reads=[x_b[t]], writes=[out_b])
                    break
                mem_attn(1)
                mlp(1)
                g, g_b = load_gain(gains_d[7:8, :])
                for t in range(8):
                    xi = xrr[0]; xrr[0] ^= 1
                    s, s_b = rms_rstd(x_sb[:, t, :], x_b[t], 128, xi)
                    stt_op("dve", x_sb[:, t, :], x_sb[:, t, :], s, g[:, :], ALU.mult, ALU.mult,
                           [x_b[t], s_b, g_b], [x_b[t]])
                    dma("sp", ov[:, tile0 * 8 + t, :], x_sb[:, t, :], reads=[x_b[t]], writes=[out_b])
        P.wait_all("sp", [out_b])
        P.emit()
    return nc


def _consts():
    c = np.zeros((128, 4, 128), np.float32)
    c[:, 0, :] = np.eye(128, dtype=np.float32)
    c[:, 1, :] = np.triu(np.ones((128, 128), np.float32))
    c[0, 2, :] = 1.0
    c[:, 3, :] = 1.0
    return c


def _masks(h):
    p = np.arange(128)[:, None]
    q = np.arange(512)[None, :]
    m = np.zeros((128, 8, 512), np.float32)
    for o in range(4):
        tri = np.where(o * 128 + p <= q, 0.0, NEG).astype(np.float32)
        m[:, o, :] = tri if h == 0 else 0.0
        m[:, 4 + o, :] = tri
    return m


def _common_inputs(inp, c):
    b, h = c // 2, c % 2
    gains = np.stack([inp["norm_mix_g"][0], inp["norm_mem_g"][0], inp["norm_ff_g"][0], inp["kv_norm_g"],
                      inp["norm_mix_g"][1], inp["norm_mem_g"][1], inp["norm_ff_g"][1], inp["final_norm_g"]]).astype(np.float32)
    cfl = np.zeros((128, 8), np.float32)
    cfl[:, 0] = float(h)
    cfl[:, 1] = 1.0
    cfl[:, 2] = 1.0 - h
    cfl[:, 3] = float(h)
    cfl[:, 4] = NEG * (1 - h)
    cfl[:, 5] = 1.0
    cfl[:, 6] = 1e-6
    cfl[:, 7] = 1e-5
    return {
        "cflags": cfl, "consts": _consts(), "gains": gains,
        "mem": np.ascontiguousarray(inp["mem"][b]), "gsrc": np.ascontiguousarray(inp["norm_memsrc_g"]),
        "mem_wq": inp["mem_wq"], "mem_wk": inp["mem_wk"], "mem_wv": inp["mem_wv"], "mem_wo": inp["mem_wo"],
        "ff_w1": inp["ff_w1"], "ff_w2": inp["ff_w2"],
    }


def _a_inputs(inp, c):
    b, h = c // 2, c % 2
    x = inp["x"]
    xh = np.zeros((2, 32, D), np.float32)
    if h == 1:
        xh[0] = x[b, NTOK - 32:NTOK]
    xh[1] = x[b, h * NTOK + TT - 32: h * NTOK + TT]
    col = lambda v: np.ascontiguousarray(v.reshape(-1, 128).T)
    return {
        "x": np.ascontiguousarray(x[b, h * NTOK:(h + 1) * NTOK]), "xhalo": xh,
        "pw1": inp["conv_pw1_w"][0], "pw2": inp["conv_pw2_w"][0],
        "b1col": col(inp["conv_pw1_b"][0]),
        "dwcol": np.ascontiguousarray(inp["conv_dw_w"][0].reshape(31, 8, 128).transpose(2, 1, 0)),
        "bdwcol": col(inp["conv_dw_b"][0]), "lngcol": col(inp["conv_ln_g"][0]), "lnbcol": col(inp["conv_ln_b"][0]),
        "b2row": np.ascontiguousarray(inp["conv_pw2_b"][0][None, :]),
        "kvf": inp["kvf_w"],
    }


def _b_inputs(inp, c):
    h = c % 2
    return {
        "fgbrep": np.ascontiguousarray(np.broadcast_to(np.tile(inp["fgate_b"], 32)[None, :], (128, 512))).astype(np.float32),
        "masks": _masks(h),
        "fox_wq": inp["fox_wq"][0], "fox_wo": inp["fox_wo"][0],
    }


_NC_CACHE = {}


def _get_nc(mode):
    if mode not in _NC_CACHE:
        _NC_CACHE[mode] = build(mode)
    return _NC_CACHE[mode]


def kernel(**inp):
    inp = {k: np.asarray(v) for k, v in inp.items()}
    cores = list(range(8))
    if FUSED:
        maps = []
        for c in cores:
            m = _common_inputs(inp, c); m.update(_a_inputs(inp, c)); m.update(_b_inputs(inp, c))
            maps.append(m)
        res = run_bass_kernel_spmd(_get_nc("AB"), maps, core_ids=cores)
        outs = [res.results[c]["out"] for c in cores]
    else:
        maps = []
        for c in cores:
            m = _common_inputs(inp, c); m.update(_a_inputs(inp, c))
            maps.append(m)
        ra = run_bass_kernel_spmd(_get_nc("A"), maps, core_ids=cores).results
        maps = []
        for c in cores:
            p = c - c % 2
            m = _common_inputs(inp, c); m.update(_b_inputs(inp, c))
            m["x1"] = ra[c]["x1"]
            m["KTg"] = np.concatenate([ra[p]["KTd"], ra[p + 1]["KTd"]], axis=0)
            m["Vg"] = np.concatenate([ra[p]["Vd"], ra[p + 1]["Vd"]], axis=0)
            m["Fg"] = np.concatenate([ra[p]["Fd"], ra[p + 1]["Fd"]], axis=0)
            maps.append(m)
        rb = run_bass_kernel_spmd(_get_nc("B"), maps, core_ids=cores).results
        outs = [rb[c]["out"] for c in cores]
    out = np.zeros((4, 4096, D), np.float32)
    for c in cores:
        out[c // 2, (c % 2) * NTOK:(c % 2 + 1) * NTOK] = outs[c]
    return out
```
